# Optimizing a Trainium2 kernel written in Bass

```python
import math
import jax, jax.numpy as jnp
from jax import lax
import numpy as np

D_MODEL = 1024
BATCH = 8
SEQ = 4096
DEPTH = 4

N_MIXERS = 3
N_CONV_LAYERS = (DEPTH + 2) // 3
N_LRU_LAYERS = (DEPTH + 1) // 3
N_MLA_LAYERS = DEPTH // 3

CONV_WIDTH = 3

LRU_WIDTH = 1280
LRU_BLOCKS = 10
LRU_BLOCK_W = LRU_WIDTH // LRU_BLOCKS
LRU_CONV_WIDTH = 4
LRU_C = 8.0

MLA_HEADS = 8
Q_LORA_RANK = 384
KV_LORA_RANK = 256
QK_NOPE_DIM = 128
QK_ROPE_DIM = 64
V_HEAD_DIM = 128
ROPE_THETA = 10000.0
Q_BLOCK = 128

D_FF = ((8 * D_MODEL // 3 + 255) // 256) * 256

NORM_EPS = 1e-6

kernel_name = 'hybrid_conv_rglru_mla_interleaved'


def rms_norm(x, g):
    xf = x.astype(jnp.float32)
    y = xf * lax.rsqrt(jnp.mean(xf * xf, axis=-1, keepdims=True) + NORM_EPS)
    return (y * g.astype(jnp.float32)).astype(x.dtype)


def causal_depthwise_conv(x, w):
    width, ch = w.shape
    return lax.conv_general_dilated(
        x, w[:, None, :].astype(x.dtype), window_strides=(1,),
        padding=[(width - 1, 0)], dimension_numbers=('NWC', 'WIO', 'NWC'),
        feature_group_count=ch)


def short_conv_mixer(xn, w_in, w_conv, w_out):
    b_gate, c_gate, h = jnp.split(xn @ w_in, 3, axis=-1)
    y = b_gate * causal_depthwise_conv(c_gate * h, w_conv)
    return y @ w_out


def rg_lru(xs, wa, ba, wx, bx, lam):
    bsz, seq, width = xs.shape
    xb = xs.reshape(bsz, seq, LRU_BLOCKS, LRU_BLOCK_W)
    r = jax.nn.sigmoid(jnp.einsum('bsnd,nde->bsne', xb, wa) + ba).reshape(bsz, seq, width)
    i = jax.nn.sigmoid(jnp.einsum('bsnd,nde->bsne', xb, wx) + bx).reshape(bsz, seq, width)
    log_a = LRU_C * r.astype(jnp.float32) * jax.nn.log_sigmoid(lam.astype(jnp.float32))
    a = jnp.exp(log_a)
    mult = jnp.sqrt(-jnp.expm1(2.0 * log_a))
    b = mult * (i * xs).astype(jnp.float32)

    def combine(left, right):
        a_l, b_l = left
        a_r, b_r = right
        return a_l * a_r, a_r * b_l + b_r

    _, h = lax.associative_scan(combine, (a, b), axis=1)
    return h.astype(xs.dtype)


def recurrent_mixer(xn, w_in, conv_w, conv_b, gate_a_w, gate_a_b, gate_x_w, gate_x_b, lam, w_out):
    gate_branch, rec_branch = jnp.split(xn @ w_in, 2, axis=-1)
    gate = jax.nn.gelu(gate_branch, approximate=True)
    rec = causal_depthwise_conv(rec_branch, conv_w) + conv_b
    h = rg_lru(rec, gate_a_w, gate_a_b, gate_x_w, gate_x_b, lam)
    return (gate * h) @ w_out


def apply_rope(t, cos, sin):
    half = t.shape[-1] // 2
    tf = t.astype(jnp.float32)
    t1, t2 = tf[..., :half], tf[..., half:]
    return jnp.concatenate([t1 * cos - t2 * sin, t2 * cos + t1 * sin], axis=-1).astype(t.dtype)


def mla_mixer(xn, cos, sin, w_down, q_norm, kv_norm, w_uq, w_ukv,
              qn_norm, qr_norm, kn_norm, kr_norm, w_o):
    bsz, seq, _ = xn.shape
    c = xn @ w_down
    c_q = c[..., :Q_LORA_RANK]
    c_kv = c[..., Q_LORA_RANK:Q_LORA_RANK + KV_LORA_RANK]
    k_rope = c[..., Q_LORA_RANK + KV_LORA_RANK:]

    q = (rms_norm(c_q, q_norm) @ w_uq).reshape(bsz, seq, MLA_HEADS, QK_NOPE_DIM + QK_ROPE_DIM)
    kv = (rms_norm(c_kv, kv_norm) @ w_ukv).reshape(bsz, seq, MLA_HEADS, QK_NOPE_DIM + V_HEAD_DIM)
    q_nope = rms_norm(q[..., :QK_NOPE_DIM], qn_norm)
    q_rope = apply_rope(rms_norm(q[..., QK_NOPE_DIM:], qr_norm), cos[:, :, None, :], sin[:, :, None, :])
    k_nope = rms_norm(kv[..., :QK_NOPE_DIM], kn_norm)
    v = kv[..., QK_NOPE_DIM:]
    k_rope = apply_rope(rms_norm(k_rope, kr_norm), cos, sin)

    scale = 1.0 / math.sqrt(QK_NOPE_DIM + QK_ROPE_DIM)
    n_blk = seq // Q_BLOCK

    def to_blocks(t):
        return t.reshape(bsz, n_blk, Q_BLOCK, *t.shape[2:]).transpose(1, 0, 2, 3, 4)

    k_idx = jnp.arange(seq)

    def attend(args):
        qn_b, qr_b, start = args
        s = (jnp.einsum('bqhd,bkhd->bhqk', qn_b, k_nope)
             + jnp.einsum('bqhr,bkr->bhqk', qr_b, k_rope)).astype(jnp.float32) * scale
        q_idx = start + jnp.arange(Q_BLOCK)
        mask = k_idx[None, :] <= q_idx[:, None]
        s = jnp.where(mask, s, jnp.finfo(jnp.float32).min)
        p = jax.nn.softmax(s, axis=-1).astype(v.dtype)
        return jnp.einsum('bhqk,bkhd->bqhd', p, v)

    starts = jnp.arange(n_blk, dtype=jnp.int32) * Q_BLOCK
    o = lax.map(attend, (to_blocks(q_nope), to_blocks(q_rope), starts))
    o = o.transpose(1, 0, 2, 3, 4).reshape(bsz, seq, MLA_HEADS * V_HEAD_DIM)
    return o @ w_o


def swiglu_ffn(xn, w_gu, w_down):
    gate, up = jnp.split(xn @ w_gu, 2, axis=-1)
    return (jax.nn.silu(gate) * up) @ w_down


def setup_inputs(seed: int = 0) -> dict:
    key = jax.random.key(seed)
    ks = iter(jax.random.split(key, 40))
    f32 = jnp.float32
    res = (2 * DEPTH) ** -0.5

    def nrm(shape, scale):
        return jax.random.normal(next(ks), shape, f32) * scale

    def gain(shape):
        return 1.0 + 0.05 * jax.random.normal(next(ks), shape, f32)

    x = jax.random.normal(next(ks), (BATCH, SEQ, D_MODEL), f32)
    offset = jax.random.randint(next(ks), (BATCH, 1), 0, 1024, dtype=jnp.int32)
    positions = offset + jnp.arange(SEQ, dtype=jnp.int32)[None, :]

    mix_norm = gain((DEPTH, D_MODEL))

    nA = N_CONV_LAYERS
    conv_w_in = nrm((nA, D_MODEL, 3 * D_MODEL), D_MODEL ** -0.5)
    conv_w = nrm((nA, CONV_WIDTH, D_MODEL), CONV_WIDTH ** -0.5)
    conv_w_out = nrm((nA, D_MODEL, D_MODEL), D_MODEL ** -0.5 * res)

    nB = N_LRU_LAYERS
    lru_w_in = nrm((nB, D_MODEL, 2 * LRU_WIDTH), D_MODEL ** -0.5)
    lru_conv_w = nrm((nB, LRU_CONV_WIDTH, LRU_WIDTH), LRU_CONV_WIDTH ** -0.5)
    lru_conv_b = nrm((nB, LRU_WIDTH), 0.01)
    lru_gate_a_w = nrm((nB, LRU_BLOCKS, LRU_BLOCK_W, LRU_BLOCK_W), LRU_BLOCK_W ** -0.5)
    lru_gate_a_b = nrm((nB, LRU_BLOCKS, LRU_BLOCK_W), 0.01)
    lru_gate_x_w = nrm((nB, LRU_BLOCKS, LRU_BLOCK_W, LRU_BLOCK_W), LRU_BLOCK_W ** -0.5)
    lru_gate_x_b = nrm((nB, LRU_BLOCKS, LRU_BLOCK_W), 0.01)
    a0 = jax.random.uniform(next(ks), (nB, LRU_WIDTH), f32, 0.9, 0.999)
    lru_lambda = jnp.log(a0) - jnp.log1p(-a0)
    lru_w_out = nrm((nB, LRU_WIDTH, D_MODEL), LRU_WIDTH ** -0.5 * res)

    nC = N_MLA_LAYERS
    mla_w_down = nrm((nC, D_MODEL, Q_LORA_RANK + KV_LORA_RANK + QK_ROPE_DIM), D_MODEL ** -0.5)
    mla_q_norm = gain((nC, Q_LORA_RANK))
    mla_kv_norm = gain((nC, KV_LORA_RANK))
    mla_w_uq = nrm((nC, Q_LORA_RANK, MLA_HEADS * (QK_NOPE_DIM + QK_ROPE_DIM)), Q_LORA_RANK ** -0.5)
    mla_w_ukv = nrm((nC, KV_LORA_RANK, MLA_HEADS * (QK_NOPE_DIM + V_HEAD_DIM)), KV_LORA_RANK ** -0.5)
    mla_qn_norm = gain((nC, QK_NOPE_DIM))
    mla_qr_norm = gain((nC, QK_ROPE_DIM))
    mla_kn_norm = gain((nC, QK_NOPE_DIM))
    mla_kr_norm = gain((nC, QK_ROPE_DIM))
    mla_w_o = nrm((nC, MLA_HEADS * V_HEAD_DIM, D_MODEL), (MLA_HEADS * V_HEAD_DIM) ** -0.5 * res)

    ffn_norm = gain((DEPTH, D_MODEL))
    ffn_w_gu = nrm((DEPTH, D_MODEL, 2 * D_FF), D_MODEL ** -0.5)
    ffn_w_down = nrm((DEPTH, D_FF, D_MODEL), D_FF ** -0.5 * res)

    return {'x': x, 'positions': positions, 'mix_norm': mix_norm,
            'conv_w_in': conv_w_in, 'conv_w': conv_w, 'conv_w_out': conv_w_out,
            'lru_w_in': lru_w_in, 'lru_conv_w': lru_conv_w, 'lru_conv_b': lru_conv_b,
            'lru_gate_a_w': lru_gate_a_w, 'lru_gate_a_b': lru_gate_a_b,
            'lru_gate_x_w': lru_gate_x_w, 'lru_gate_x_b': lru_gate_x_b,
            'lru_lambda': lru_lambda, 'lru_w_out': lru_w_out,
            'mla_w_down': mla_w_down, 'mla_q_norm': mla_q_norm, 'mla_kv_norm': mla_kv_norm,
            'mla_w_uq': mla_w_uq, 'mla_w_ukv': mla_w_ukv,
            'mla_qn_norm': mla_qn_norm, 'mla_qr_norm': mla_qr_norm,
            'mla_kn_norm': mla_kn_norm, 'mla_kr_norm': mla_kr_norm, 'mla_w_o': mla_w_o,
            'ffn_norm': ffn_norm, 'ffn_w_gu': ffn_w_gu, 'ffn_w_down': ffn_w_down}


def reference(x, positions, mix_norm, conv_w_in, conv_w, conv_w_out,
              lru_w_in, lru_conv_w, lru_conv_b, lru_gate_a_w, lru_gate_a_b,
              lru_gate_x_w, lru_gate_x_b, lru_lambda, lru_w_out,
              mla_w_down, mla_q_norm, mla_kv_norm, mla_w_uq, mla_w_ukv,
              mla_qn_norm, mla_qr_norm, mla_kn_norm, mla_kr_norm, mla_w_o,
              ffn_norm, ffn_w_gu, ffn_w_down):
    inv_freq = ROPE_THETA ** (-jnp.arange(0, QK_ROPE_DIM, 2, dtype=jnp.float32) / QK_ROPE_DIM)
    angle = positions.astype(jnp.float32)[..., None] * inv_freq
    cos, sin = jnp.cos(angle), jnp.sin(angle)

    h = x
    for i in range(DEPTH):
        kind, j = i % N_MIXERS, i // N_MIXERS
        xn = rms_norm(h, mix_norm[i])
        if kind == 0:
            y = short_conv_mixer(xn, conv_w_in[j], conv_w[j], conv_w_out[j])
        elif kind == 1:
            y = recurrent_mixer(xn, lru_w_in[j], lru_conv_w[j], lru_conv_b[j],
                                lru_gate_a_w[j], lru_gate_a_b[j], lru_gate_x_w[j],
                                lru_gate_x_b[j], lru_lambda[j], lru_w_out[j])
        else:
            y = mla_mixer(xn, cos, sin, mla_w_down[j], mla_q_norm[j], mla_kv_norm[j],
                          mla_w_uq[j], mla_w_ukv[j], mla_qn_norm[j], mla_qr_norm[j],
                          mla_kn_norm[j], mla_kr_norm[j], mla_w_o[j])
        h = h + y
        h = h + swiglu_ffn(rms_norm(h, ffn_norm[i]), ffn_w_gu[i], ffn_w_down[i])
    return h
```

```python
import contextlib
import math
import numpy as np
import concourse.bass as bass
import concourse.mybir as mybir
from concourse.bass_utils import run_bass_kernel_spmd

F32 = mybir.dt.float32
BF16 = mybir.dt.bfloat16
I32 = mybir.dt.int32
AF = mybir.ActivationFunctionType
ALU = mybir.AluOpType

EPOCH = 12000
ENGS = ("pe", "act", "dve", "pool", "sp")


class Buf:
    __slots__ = ("name", "last_w", "readers")

    def __init__(self, name):
        self.name = name
        self.last_w = None
        self.readers = []


class DmaSem:
    __slots__ = ("name", "count", "handle", "last_op")

    def __init__(self, name):
        self.name = name
        self.count = 0
        self.handle = None
        self.last_op = None


class Op:
    __slots__ = ("eng", "fn", "reads", "writes", "dsem", "dval", "idx",
                 "deps", "signal", "sig", "waits")

    def __init__(self, eng, fn, reads, writes, dsem):
        self.eng = eng
        self.fn = fn
        self.reads = reads
        self.writes = writes
        self.dsem = dsem
        self.dval = None
        self.deps = ()
        self.signal = False
        self.sig = None
        self.waits = ()


class Prog:
    def __init__(self):
        self.ops = []
        self.dsems = []

    def dma_sem(self, name):
        s = DmaSem(name)
        self.dsems.append(s)
        return s

    def op(self, eng, fn, reads=(), writes=(), dsem=None):
        o = Op(eng, fn, tuple(reads), tuple(writes), dsem)
        o.idx = len(self.ops)
        self.ops.append(o)
        return o

    def analyze(self):
        ops = self.ops
        for o in ops:
            deps = set()
            for b in o.reads:
                if b.last_w is not None:
                    deps.add(b.last_w)
            for b in o.writes:
                if b.last_w is not None:
                    deps.add(b.last_w)
                deps.update(b.readers)
            if o.dsem is not None:
                if o.dsem.last_op is not None:
                    deps.add(o.dsem.last_op)
                o.dsem.last_op = o.idx
                o.dsem.count += 16
                o.dval = o.dsem.count
            deps.discard(o.idx)
            for b in o.reads:
                b.readers.append(o.idx)
            for b in o.writes:
                b.last_w = o.idx
                b.readers = []
            o.deps = deps
        known = {e: {} for e in ENGS}
        for o in ops:
            need = {}
            for d in o.deps:
                od = ops[d]
                if od.dsem is not None:
                    key = ("d", id(od.dsem))
                    val = od.dval
                else:
                    if od.eng == o.eng and o.dsem is None:
                        ws = od.writes
                        if not any(b in ws for b in o.reads):
                            continue
                    key = ("e", od.eng)
                    val = d
                if key not in need or need[key][0] < val:
                    need[key] = (val, d)
            o.waits = []
            k = known[o.eng]
            for key, (val, d) in need.items():
                if key in k and k[key] >= val:
                    continue
                k[key] = val
                o.waits.append(d)
                if ops[d].dsem is None:
                    ops[d].signal = True
        cnt = {e: 0 for e in ENGS}
        for o in ops:
            if o.dsem is None and o.signal:
                n = cnt[o.eng]
                cnt[o.eng] = n + 1
                o.sig = (n // EPOCH, n % EPOCH + 1)
        self.n_epochs = {e: (cnt[e] + EPOCH - 1) // EPOCH for e in ENGS}
        return cnt

    def emit(self, nc, stack):
        cnt = self.analyze()
        esem = {}
        for e in ENGS:
            esem[e] = [stack.enter_context(nc.semaphore(f"s_{e}{k}"))
                       for k in range(max(1, self.n_epochs[e]))]
        for s in self.dsems:
            if s.count:
                s.handle = stack.enter_context(nc.semaphore(f"d_{s.name}"))
        ops = self.ops
        by_eng = {e: [o for o in ops if o.eng == e] for e in ENGS}

        def run(e, engobj):
            for o in by_eng[e]:
                for d in o.waits:
                    od = ops[d]
                    if od.dsem is not None:
                        engobj.wait_ge(od.dsem.handle, od.dval)
                    else:
                        ep, v = od.sig
                        engobj.wait_ge(esem[od.eng][ep], v)
                ins = o.fn(engobj)
                if o.dsem is not None:
                    ins.then_inc(o.dsem.handle, 16)
                elif o.signal:
                    ins.then_inc(esem[e][o.sig[0]], 1)

        block = stack.enter_context(nc.Block())

        @block.tensor
        def _(eng):
            run("pe", eng)

        @block.scalar
        def _(eng):
            run("act", eng)

        @block.vector
        def _(eng):
            run("dve", eng)

        @block.gpsimd
        def _(eng):
            run("pool", eng)

        @block.sync
        def _(eng):
            run("sp", eng)
        return cnt


D = 1024
S = 4096
T = 512
NT = S // T
NCH = D // 128
DFF = 2816
NFF = DFF // 128
FFN_PARTS = ((0, 8), (8, 15), (15, 22))
LRU_W = 1280
NLC = 10
NH = 8
EPS = 1e-6
SM_SCALE = 1.0 / math.sqrt(192.0)
GELU_C = 0.044715
GELU_S = 2.0 * math.sqrt(2.0 / math.pi)

SLOT_ELEMS = (33792, 24704)

C_MIXN = 0
C_FFNN = 32
C_CONVW = 64
C_LCW = 112
C_LCB = 152
C_LGAB = 162
C_LGXB = 172
C_LLAM = 182
C_QNORM = 192
C_KVNORM = 195
C_QNN = 197
C_KNN = 198
C_QRN = 199
C_KRN = 201
C_INVF = 203
NCOL = 208

WEIGHT_NAMES = ("conv_w_in", "conv_w_out", "lru_w_in", "lru_gate_a_w", "lru_gate_x_w",
                "lru_w_out", "mla_w_down", "mla_w_uq", "mla_w_ukv", "mla_w_o",
                "ffn_w_gu", "ffn_w_down")
WEIGHT_SHAPES = {
    "conv_w_in": [2, 1024, 3072], "conv_w_out": [2, 1024, 1024],
    "lru_w_in": [1, 1024, 2560], "lru_gate_a_w": [1, 10, 128, 128],
    "lru_gate_x_w": [1, 10, 128, 128], "lru_w_out": [1, 1280, 1024],
    "mla_w_down": [1, 1024, 704], "mla_w_uq": [1, 384, 1536],
    "mla_w_ukv": [1, 256, 2048], "mla_w_o": [1, 1024, 1024],
    "ffn_w_gu": [4, 1024, 5632], "ffn_w_down": [4, 2816, 1024],
}


class TB:
    __slots__ = ("ap", "b")

    def __init__(self, ap, b):
        self.ap = ap
        self.b = b


def build(stage_limit=None, dbg_out=None):
    nc = bass.Bass("TRN2", target_bir_lowering=False)
    P = Prog()
    st = contextlib.ExitStack()

    def din(name, shape, dt=F32):
        return nc.dram_tensor(name, shape, dt, kind="ExternalInput").ap()

    def dscr(name, shape, dt):
        return nc.dram_tensor(name, shape, dt, kind="Internal").ap()

    xT = din("xT", [D, S])
    pos = din("pos", [1, S], I32)
    cst_d = din("consts", [128, NCOL])
    cmat_d = din("cmat", [128, 256])
    Wd = {n: din(n, WEIGHT_SHAPES[n]) for n in WEIGHT_NAMES}
    out = nc.dram_tensor("out", [D, S], F32, kind="ExternalOutput").ap()

    R = dscr("R", [D, S], F32)
    ACC = dscr("ACC", [D, S], F32)
    XNd = dscr("XNd", [D, S], BF16)
    QNd = dscr("QNd", [NH * 128, S], BF16)
    QRd = dscr("QRd", [NH * 64, S], BF16)
    KNd = dscr("KNd", [NH * 128, S], BF16)
    KRd = dscr("KRd", [64, S], BF16)
    Vd = dscr("Vd", [NH, 128, S], BF16)
    OTd = dscr("OTd", [D, S], BF16)
    dram_bufs = {}

    def dbuf(name, j):
        k = (name, j)
        if k not in dram_bufs:
            dram_bufs[k] = Buf(f"{name}{j}")
        return dram_bufs[k]

    with st:
        def sb(name, shape, dt):
            return st.enter_context(nc.sbuf_tensor(name, shape, dt))

        def ps(name, shape, dt):
            return st.enter_context(nc.psum_tensor(name, shape, dt))

        Wt = [sb("W0", [128, SLOT_ELEMS[0]], BF16), sb("W1", [128, SLOT_ELEMS[1]], BF16)]
        Wb = [Buf("W0"), Buf("W1")]
        XS = [sb(f"XS{i}", [128, NCH, T], F32) for i in range(2)]
        XSb = [[Buf(f"XS{i}_{c}") for c in range(NCH)] for i in range(2)]
        XN = [sb(f"XN{i}", [128, NCH, T], BF16) for i in range(2)]
        XNb = [[Buf(f"XN{i}_{c}") for c in range(NCH)] for i in range(2)]
        HB = sb("HB", [128, NLC, T], BF16)
        HBb = [Buf(f"HB{c}") for c in range(NLC)]
        NF = 8
        FT = [sb(f"FT{i}", [128, 516], F32) for i in range(NF)]
        FTb = [Buf(f"FT{i}") for i in range(NF)]
        NB = 10
        BT = [sb(f"BT{i}", [128, T], BF16) for i in range(NB)]
        BTb = [Buf(f"BT{i}") for i in range(NB)]
        cst = sb("cst", [128, NCOL], F32)
        cstb = Buf("cst")
        cmat = sb("cmatb", [128, 256], BF16)
        cmatb = Buf("cmat")
        ones = sb("ones", [128, 128], BF16)
        onesb = Buf("ones")
        halo_c = sb("halo_c", [128, NCH, 2], F32)
        halo_cb = [Buf(f"hc{c}") for c in range(NCH)]
        halo_r = sb("halo_r", [128, NLC, 3], F32)
        halo_rb = [Buf(f"hr{c}") for c in range(NLC)]
        hst = sb("hst", [128, NLC], F32)
        hstb = [Buf(f"hs{c}") for c in range(NLC)]
        ls8 = sb("ls8", [128, 2 * NLC], F32)
        ls8b = Buf("ls8")
        RS = [sb(f"RS{i}", [128, T], F32) for i in range(2)]
        RSb = [Buf(f"RS{i}") for i in range(2)]
        PB = [ps(f"PB{i}", [128, T], F32) for i in range(7)]
        PBb = [Buf(f"PB{i}") for i in range(7)]
        PT = ps("PTb", [128, 1024], BF16)
        PTb = Buf("PTb")

        sem_xs = [P.dma_sem(f"xs{i}") for i in range(2)]
        sem_xs_st = [P.dma_sem(f"xsst{i}") for i in range(2)]
        sem_xn = [P.dma_sem(f"xn{i}") for i in range(2)]
        sem_xn_st = [P.dma_sem(f"xnst{i}") for i in range(2)]
        sem_w = [P.dma_sem("w0"), P.dma_sem("w1")]
        sem_misc = P.dma_sem("misc")
        sem_bt = [P.dma_sem(f"bt{i}") for i in range(NB)]
        sem_hb = P.dma_sem("hbst")
        sem_pos = P.dma_sem("pos")

        def dtile(dr, j):
            return dr.rearrange("(c p) t -> p c t", p=128)[:, :, j * T:(j + 1) * T]

        def mm(o_ap, lhsT, rhs, start, stop, reads, writes, skip=False):
            P.op("pe", lambda e: e.matmul(o_ap, lhsT, rhs, start=start, stop=stop, skip_group_check=skip),
                 reads=reads, writes=writes)

        def act(o_ap, i_ap, func, reads, writes, bias=None, scale=None):
            kw = {}
            if bias is not None:
                kw["bias"] = bias
            if scale is not None:
                kw["scale"] = scale
            P.op("act", lambda e: e.activation(out=o_ap, in_=i_ap, func=func, **kw),
                 reads=reads, writes=writes)

        def tt(o_ap, a_ap, b_ap, op, reads, writes):
            P.op("dve", lambda e: e.tensor_tensor(out=o_ap, in0=a_ap, in1=b_ap, op=op),
                 reads=reads, writes=writes)

        def tsc(o_ap, a_ap, s1, s2, op0, op1, reads, writes):
            if s2 is None:
                P.op("dve", lambda e: e.tensor_scalar(out=o_ap, in0=a_ap, scalar1=s1, scalar2=None,
                                                      op0=op0), reads=reads, writes=writes)
            else:
                P.op("dve", lambda e: e.tensor_scalar(out=o_ap, in0=a_ap, scalar1=s1, scalar2=s2,
                                                      op0=op0, op1=op1), reads=reads, writes=writes)

        def stt(o_ap, a_ap, s_ap, b_ap, op0, op1, reads, writes):
            P.op("dve", lambda e: e.scalar_tensor_tensor(out=o_ap, in0=a_ap, scalar=s_ap, in1=b_ap,
                                                         op0=op0, op1=op1), reads=reads, writes=writes)

        def cpy(o_ap, i_ap, reads, writes):
            P.op("dve", lambda e: e.tensor_copy(out=o_ap, in_=i_ap), reads=reads, writes=writes)

        def dma(eng, o_ap, i_ap, reads, writes, sem):
            P.op(eng, lambda e: e.dma_start(out=o_ap, in_=i_ap), reads=reads, writes=writes, dsem=sem)

        def ccol(col, npart=128):
            return cst[0:npart, col:col + 1]

        dma("sp", cst[:], cst_d, [], [cstb], sem_misc)
        dma("pool", cmat[:], cmat_d, [], [cmatb], sem_w[0])
        P.op("dve", lambda e: e.memset(ones[:], 1.0), writes=[onesb])

        def wview(slot, off, k, n):
            return Wt[slot][:, off:off + k * n].rearrange("p (k n) -> p k n", k=k)

        def wload(slot, off, k, n, src):
            assert off + k * n <= SLOT_ELEMS[slot]
            dma("pool", wview(slot, off, k, n), src, [], [Wb[slot]], sem_w[slot])
            return wview(slot, off, k, n)

        def kmaj(w2d, c0=None, c1=None):
            v = w2d.rearrange("(k p) n -> p k n", p=128)
            if c0 is not None:
                v = v[:, :, c0:c1]
            return v

        sq_ring = [0]

        def norm_stats(srcs, npart, dn, rs_i, ssb_i):
            n = len(srcs)
            for c, (ap, bufs) in enumerate(srcs):
                k = 8 + (sq_ring[0] % 2)
                sq_ring[0] += 1
                act(BT[k][0:npart, :], ap, AF.Square, bufs, [BTb[k]])
                mm(PB[ssb_i][:], ones[0:npart, :], BT[k][0:npart, :], c == 0, c == n - 1,
                   [BTb[k], onesb], [PBb[ssb_i]])
            act(RS[rs_i][:], PB[ssb_i][:], AF.Sqrt, [PBb[ssb_i]], [RSb[rs_i]], bias=EPS, scale=1.0 / dn)
            P.op("dve", lambda e: e.reciprocal(out=RS[rs_i][:], in_=RS[rs_i][:]),
                 reads=[RSb[rs_i]], writes=[RSb[rs_i]])

        def norm_apply(srcs, npart, rs_i, gcols, outs):
            for (ap, bufs), gc, (oap, obufs) in zip(srcs, gcols, outs):
                stt(oap, ap, ccol(gc, npart), RS[rs_i][0:npart, :], ALU.mult, ALU.mult,
                    bufs + [RSb[rs_i], cstb], obufs)

        def load_x(src_d, src_name, j, slot):
            dma("sp", XS[slot][:], dtile(src_d, j), [dbuf(src_name, j)], XSb[slot], sem_xs[slot])

        def store_x(dst_d, dst_name, j, slot):
            dma("sp", dtile(dst_d, j), XS[slot][:], XSb[slot], [dbuf(dst_name, j)], sem_xs_st[slot])

        def xs_srcs(slot):
            return [(XS[slot][:, c, :], [XSb[slot][c]]) for c in range(NCH)]

        def xn_outs(slot):
            return [(XN[slot][:, c, :], [XNb[slot][c]]) for c in range(NCH)]

        def resid_out(slot, w_ap_fn, nk, rhs_fn, rhs_bufs, wslot, pbs):
            for m in range(NCH):
                pb = pbs[m % len(pbs)]
                for k in range(nk):
                    mm(PB[pb][:], w_ap_fn(k, m), rhs_fn(k), k == 0, k == nk - 1,
                       [Wb[wslot]] + rhs_bufs(k), [PBb[pb]])
                tt(XS[slot][:, m, :], XS[slot][:, m, :], PB[pb][:], ALU.add,
                   [XSb[slot][m], PBb[pb]], [XSb[slot][m]])

        def stage_ffn(part, l, src, dst, wslot, dst_is_out=False):
            f0, f1 = FFN_PARTS[part]
            nf = f1 - f0
            wv = {}

            def loadw():
                gu = Wd["ffn_w_gu"][l]
                wv["g"] = wload(wslot, 0, NCH, nf * 128, kmaj(gu, f0 * 128, f1 * 128))
                wv["u"] = wload(wslot, NCH * nf * 128, NCH, nf * 128,
                                kmaj(gu, DFF + f0 * 128, DFF + f1 * 128))
                wv["d"] = wload(wslot, 2 * NCH * nf * 128, nf, D,
                                Wd["ffn_w_down"][l][f0 * 128:f1 * 128, :].rearrange("(k p) n -> p k n", p=128))

            def prologue(j):
                s = j % 2
                if part == 0:
                    load_x(src[0], src[1], j, s)
                    norm_stats(xs_srcs(s), 128, D, s, 6)
                    norm_apply(xs_srcs(s), 128, s, [C_FFNN + l * 8 + c for c in range(NCH)], xn_outs(s))
                    dma("sp", dtile(XNd, j), XN[s][:], XNb[s], [dbuf("XNd", j)], sem_xn_st[s])
                else:
                    load_x(ACC, "ACC", j, s)
                    dma("sp", XN[s][:], dtile(XNd, j), [dbuf("XNd", j)], XNb[s], sem_xn[s])

            def run():
                prologue(0)
                for j in range(NT):
                    s = j % 2
                    for f in range(nf):
                        pg, pu = f % 2, 2 + f % 2
                        for k in range(NCH):
                            mm(PB[pg][:], wv["g"][:, k, f * 128:(f + 1) * 128], XN[s][:, k, :],
                               k == 0, k == NCH - 1, [Wb[wslot], XNb[s][k]], [PBb[pg]])
                        for k in range(NCH):
                            mm(PB[pu][:], wv["u"][:, k, f * 128:(f + 1) * 128], XN[s][:, k, :],
                               k == 0, k == NCH - 1, [Wb[wslot], XNb[s][k]], [PBb[pu]])
                        sg = f % 2
                        act(BT[sg][:], PB[pg][:], AF.Silu, [PBb[pg]], [BTb[sg]])
                        tt(HB[:, f, :], BT[sg][:], PB[pu][:], ALU.mult, [BTb[sg], PBb[pu]], [HBb[f]])
                        if f == 2 and j + 1 < NT:
                            prologue(j + 1)
                    resid_out(s, lambda k, m: wv["d"][:, k, m * 128:(m + 1) * 128], nf,
                              lambda k: HB[:, k, :], lambda k: [HBb[k]], wslot, (4, 5))
                    if part == 2:
                        store_x(dst[0], dst[1], j, s)
                    else:
                        store_x(ACC, "ACC", j, s)
            return loadw, run

        def stage_conv(l, jc, src, dst, wslot):
            wv = {}

            def loadw():
                wv["in"] = wload(wslot, 0, NCH, 3 * D, kmaj(Wd["conv_w_in"][jc]))
                wv["out"] = wload(wslot, NCH * 3 * D, NCH, D, kmaj(Wd["conv_w_out"][jc]))

            def prologue(j):
                s = j % 2
                load_x(src[0], src[1], j, s)
                norm_stats(xs_srcs(s), 128, D, s, 6)
                norm_apply(xs_srcs(s), 128, s, [C_MIXN + l * 8 + c for c in range(NCH)], xn_outs(s))

            def run():
                P.op("dve", lambda e: e.memset(halo_c[:], 0.0), writes=halo_cb)
                prologue(0)
                for j in range(NT):
                    s = j % 2
                    for c in range(NCH):
                        q = c % 2
                        pbB, pbC, pbH = 3 * q, 3 * q + 1, 3 * q + 2
                        for which, pb in ((0, pbB), (1, pbC), (2, pbH)):
                            col = which * D + c * 128
                            for k in range(NCH):
                                mm(PB[pb][:], wv["in"][:, k, col:col + 128], XN[s][:, k, :],
                                   k == 0, k == NCH - 1, [Wb[wslot], XNb[s][k]], [PBb[pb]])
                        cs, ut, t1 = 3 * q, 3 * q + 1, 3 * q + 2
                        act(FT[cs][:, 0:T], PB[pbC][:], AF.Copy, [PBb[pbC]], [FTb[cs]])
                        cpy(FT[ut][:, 0:2], halo_c[:, c, :], [halo_cb[c]], [FTb[ut]])
                        tt(FT[ut][:, 2:2 + T], FT[cs][:, 0:T], PB[pbH][:], ALU.mult,
                           [FTb[cs], PBb[pbH]], [FTb[ut]])
                        cpy(halo_c[:, c, :], FT[ut][:, T:T + 2], [FTb[ut]], [halo_cb[c]])
                        wc = C_CONVW + jc * 24 + c
                        tsc(FT[t1][:, 0:T], FT[ut][:, 2:2 + T], ccol(wc + 16), None, ALU.mult, None,
                            [FTb[ut], cstb], [FTb[t1]])
                        stt(FT[t1][:, 0:T], FT[ut][:, 1:1 + T], ccol(wc + 8), FT[t1][:, 0:T], ALU.mult, ALU.add,
                            [FTb[ut], FTb[t1], cstb], [FTb[t1]])
                        stt(FT[t1][:, 0:T], FT[ut][:, 0:T], ccol(wc), FT[t1][:, 0:T], ALU.mult, ALU.add,
                            [FTb[ut], FTb[t1], cstb], [FTb[t1]])
                        tt(HB[:, c, :], FT[t1][:, 0:T], PB[pbB][:], ALU.mult, [FTb[t1], PBb[pbB]], [HBb[c]])
                        if c == 2 and j + 1 < NT:
                            prologue(j + 1)
                    resid_out(s, lambda k, m: wv["out"][:, k, m * 128:(m + 1) * 128], NCH,
                              lambda k: HB[:, k, :], lambda k: [HBb[k]], wslot, (6, 0))
                    store_x(dst[0], dst[1], j, s)
            return loadw, run

        def stage_lru(l, src, dst, wslot):
            wv = {}

            def loadw():
                wv["in"] = wload(wslot, 0, NCH, 2 * LRU_W, kmaj(Wd["lru_w_in"][0]))
                o = NCH * 2 * LRU_W
                wv["a"] = wload(wslot, o, NLC, 128, Wd["lru_gate_a_w"][0].rearrange("n d e -> d n e"))
                o += NLC * 128
                wv["x"] = wload(wslot, o, NLC, 128, Wd["lru_gate_x_w"][0].rearrange("n d e -> d n e"))
                o += NLC * 128
                wv["out"] = wload(wslot, o, NLC, D, kmaj(Wd["lru_w_out"][0]))

            def prologue(j):
                s = j % 2
                load_x(src[0], src[1], j, s)
                norm_stats(xs_srcs(s), 128, D, s, 6)
                norm_apply(xs_srcs(s), 128, s, [C_MIXN + l * 8 + c for c in range(NCH)], xn_outs(s))

            def run():
                P.op("dve", lambda e: e.memset(halo_r[:], 0.0), writes=halo_rb)
                P.op("dve", lambda e: e.memset(hst[:], 0.0), writes=hstb)
                act(ls8[:, 0:NLC], cst[:, C_LLAM:C_LLAM + NLC], AF.Exp, [cstb], [ls8b], scale=-1.0)
                act(ls8[:, 0:NLC], ls8[:, 0:NLC], AF.Ln, [ls8b], [ls8b], bias=1.0)
                tsc(ls8[:, NLC:2 * NLC], ls8[:, 0:NLC], -16.0, None, ALU.mult, None, [ls8b], [ls8b])
                tsc(ls8[:, 0:NLC], ls8[:, 0:NLC], -8.0, None, ALU.mult, None, [ls8b], [ls8b])
                prologue(0)
                for j in range(NT):
                    s = j % 2
                    for c in range(NLC):
                        q = c % 2
                        pG, pR = q, 2 + q
                        pA, pX = 4, 5
                        f0, f1_, f2, f3 = 4 * q, 4 * q + 1, 4 * q + 2, 4 * q + 3
                        gtb, recb = 2 * q, 2 * q + 1
                        for which, pb in ((0, pG), (1, pR)):
                            col = which * LRU_W + c * 128
                            for k in range(NCH):
                                mm(PB[pb][:], wv["in"][:, k, col:col + 128], XN[s][:, k, :],
                                   k == 0, k == NCH - 1, [Wb[wslot], XNb[s][k]], [PBb[pb]])
                        A0, A1, A2, A3 = FT[f0][:, 0:T], FT[f1_][:, 0:T], FT[f2], FT[f3][:, 0:T]
                        act(A0, PB[pG][:], AF.Copy, [PBb[pG]], [FTb[f0]])
                        tt(A1, A0, A0, ALU.mult, [FTb[f0]], [FTb[f1_]])
                        tsc(A1, A1, GELU_C, 1.0, ALU.mult, ALU.add, [FTb[f1_]], [FTb[f1_]])
                        tt(A1, A1, A0, ALU.mult, [FTb[f1_], FTb[f0]], [FTb[f1_]])
                        act(A1, A1, AF.Sigmoid, [FTb[f1_]], [FTb[f1_]], scale=GELU_S)
                        tt(BT[gtb][:], A0, A1, ALU.mult, [FTb[f0], FTb[f1_]], [BTb[gtb]])
                        cpy(A2[:, 0:3], halo_r[:, c, :], [halo_rb[c]], [FTb[f2]])
                        act(A2[:, 3:3 + T], PB[pR][:], AF.Copy, [PBb[pR]], [FTb[f2]])
                        cpy(halo_r[:, c, :], A2[:, T:T + 3], [FTb[f2]], [halo_rb[c]])
                        tsc(A3, A2[:, 3:3 + T], ccol(C_LCW + 30 + c), ccol(C_LCB + c), ALU.mult, ALU.add,
                            [FTb[f2], cstb], [FTb[f3]])
                        for tap in (2, 1, 0):
                            stt(A3, A2[:, tap:tap + T], ccol(C_LCW + tap * 10 + c), A3, ALU.mult, ALU.add,
                                [FTb[f2], FTb[f3], cstb], [FTb[f3]])
                        act(BT[recb][:], A3, AF.Copy, [FTb[f3]], [BTb[recb]])
                        mm(PB[pA][:], wv["a"][:, c, :], BT[recb][:], True, True, [Wb[wslot], BTb[recb]], [PBb[pA]])
                        mm(PB[pX][:], wv["x"][:, c, :], BT[recb][:], True, True, [Wb[wslot], BTb[recb]], [PBb[pX]])
                        act(A0, PB[pA][:], AF.Sigmoid, [PBb[pA], cstb], [FTb[f0]], bias=ccol(C_LGAB + c))
                        act(A1, PB[pX][:], AF.Sigmoid, [PBb[pX], cstb], [FTb[f1_]], bias=ccol(C_LGXB + c))
                        A2t = A2[:, 0:T]
                        act(A2t, A0, AF.Exp, [FTb[f0], ls8b], [FTb[f2]], scale=ls8[:, c:c + 1])
                        act(A0, A0, AF.Exp, [FTb[f0], ls8b], [FTb[f0]], scale=ls8[:, NLC + c:NLC + c + 1])
                        act(A0, A0, AF.Sqrt, [FTb[f0]], [FTb[f0]], bias=1.0, scale=-1.0)
                        tt(A1, A1, A3, ALU.mult, [FTb[f1_], FTb[f3]], [FTb[f1_]])
                        tt(A1, A1, A0, ALU.mult, [FTb[f1_], FTb[f0]], [FTb[f1_]])
                        P.op("dve", (lambda e, A3=A3, A2t=A2t, A1=A1, c=c: e.tensor_tensor_scan(
                            out=A3, data0=A2t, data1=A1, initial=hst[:, c:c + 1], op0=ALU.mult, op1=ALU.add)),
                            reads=[FTb[f2], FTb[f1_], hstb[c]], writes=[FTb[f3]])
                        cpy(hst[:, c:c + 1], FT[f3][:, T - 1:T], [FTb[f3]], [hstb[c]])
                        tt(HB[:, c, :], BT[gtb][:], A3, ALU.mult, [BTb[gtb], FTb[f3]], [HBb[c]])
                        if c == 2 and j + 1 < NT:
                            prologue(j + 1)
                    resid_out(s, lambda k, m: wv["out"][:, k, m * 128:(m + 1) * 128], NLC,
                              lambda k: HB[:, k, :], lambda k: [HBb[k]], wslot, (6, 0))
                    store_x(dst[0], dst[1], j, s)
            return loadw, run

        def stage_mla1(l, src, wslot):
            wv = {}

            def loadw():
                wv["down"] = wload(wslot, 0, NCH, 704, kmaj(Wd["mla_w_down"][0]))
                o = NCH * 704
                wv["uq"] = wload(wslot, o, 3, 1536, kmaj(Wd["mla_w_uq"][0]))
                o += 3 * 1536
                wv["ukv"] = wload(wslot, o, 2, 2048, kmaj(Wd["mla_w_ukv"][0]))

            rot = [0]

            def nextpb(n=6):
                r = rot[0] % n
                rot[0] += 1
                return r

            def prologue(j):
                s = j % 2
                load_x(src[0], src[1], j, s)
                norm_stats(xs_srcs(s), 128, D, s, 6)
                norm_apply(xs_srcs(s), 128, s, [C_MIXN + l * 8 + c for c in range(NCH)], xn_outs(s))

            def rope_pair(p1, p2, b1, b2, gcol, cosb, sinb, o1, o2, ob, rs_i):
                t1, t2, t3 = FT[4][0:32, 0:T], FT[5][0:32, 0:T], FT[6][0:32, 0:T]
                cs_, sn_ = FT[cosb][0:32, 0:T], FT[sinb][0:32, 0:T]
                stt(t1, p1, ccol(gcol, 32), RS[rs_i][0:32, :], ALU.mult, ALU.mult, b1 + [RSb[rs_i], cstb], [FTb[4]])
                stt(t2, p2, ccol(gcol + 1, 32), RS[rs_i][0:32, :], ALU.mult, ALU.mult, b2 + [RSb[rs_i], cstb], [FTb[5]])
                tt(t3, t2, sn_, ALU.mult, [FTb[5], FTb[sinb]], [FTb[6]])
                tt(FT[7][0:32, 0:T], t1, cs_, ALU.mult, [FTb[4], FTb[cosb]], [FTb[7]])
                tt(o1, FT[7][0:32, 0:T], t3, ALU.subtract, [FTb[7], FTb[6]], ob)
                tt(t3, t1, sn_, ALU.mult, [FTb[4], FTb[sinb]], [FTb[6]])
                tt(FT[7][0:32, 0:T], t2, cs_, ALU.mult, [FTb[5], FTb[cosb]], [FTb[7]])
                tt(o2, FT[7][0:32, 0:T], t3, ALU.add, [FTb[7], FTb[6]], ob)

            def run():
                prologue(0)
                for j in range(NT):
                    s = j % 2
                    xk = lambda k: XN[s][:, k, :]
                    posi = FT[1][0:32, 0:T].bitcast(I32)
                    dma("sp", posi, pos[0:1, j * T:(j + 1) * T].broadcast_to([32, T]), [], [FTb[1]], sem_pos)
                    ang = FT[0][0:32, 0:T]
                    cpy(ang, posi, [FTb[1]], [FTb[0]])
                    tsc(ang, ang, ccol(C_INVF, 32), None, ALU.mult, None, [FTb[0], cstb], [FTb[0]])
                    MAGIC = 12582912.0
                    C1 = 6.28125
                    C2 = 2.0 * math.pi - C1
                    kk = FT[1][0:32, 0:T]
                    for dst, shift in ((3, 0.0), (2, 0.5 * math.pi)):
                        a2 = FT[dst][0:32, 0:T]
                        tsc(a2, ang, shift, None, ALU.add, None, [FTb[0]], [FTb[dst]])
                        tsc(kk, a2, 1.0 / (2.0 * math.pi), None, ALU.mult, None, [FTb[dst]], [FTb[1]])
                        tsc(kk, kk, MAGIC, None, ALU.add, None, [FTb[1]], [FTb[1]])
                        tsc(kk, kk, -MAGIC, None, ALU.add, None, [FTb[1]], [FTb[1]])
                        stt(a2, kk, -C1, a2, ALU.mult, ALU.add, [FTb[1], FTb[dst]], [FTb[dst]])
                        stt(a2, kk, -C2, a2, ALU.mult, ALU.add, [FTb[1], FTb[dst]], [FTb[dst]])
                        tsc(a2, a2, math.pi, -math.pi, ALU.min, ALU.max, [FTb[dst]], [FTb[dst]])
                        act(a2, a2, AF.Sin, [FTb[dst]], [FTb[dst]])
                    def proj_down(col, width, pb):
                        for k in range(NCH):
                            mm(PB[pb][0:width, :], wv["down"][:, k, col:col + width], xk(k),
                               k == 0, k == NCH - 1, [Wb[wslot], XNb[s][k]], [PBb[pb]])
                    pbs = [nextpb() for _ in range(3)]
                    for i, pb in enumerate(pbs):
                        proj_down(i * 128, 128, pb)
                    srcs = [(PB[pb][:], [PBb[pb]]) for pb in pbs]
                    norm_stats(srcs, 128, 384, 0, 6)
                    cqn = [(BT[i][:], [BTb[i]]) for i in range(3)]
                    norm_apply(srcs, 128, 0, [C_QNORM + i for i in range(3)], cqn)
                    pbs = [nextpb() for _ in range(2)]
                    for i, pb in enumerate(pbs):
                        proj_down(384 + i * 128, 128, pb)
                    srcs = [(PB[pb][:], [PBb[pb]]) for pb in pbs]
                    norm_stats(srcs, 128, 256, 1, 6)
                    ckvn = [(BT[3 + i][:], [BTb[3 + i]]) for i in range(2)]
                    norm_apply(srcs, 128, 1, [C_KVNORM + i for i in range(2)], ckvn)
                    pb1, pb2 = nextpb(), nextpb()
                    proj_down(640, 32, pb1)
                    proj_down(672, 32, pb2)
                    norm_stats([(PB[pb1][0:32, :], [PBb[pb1]]), (PB[pb2][0:32, :], [PBb[pb2]])], 32, 64, 0, 6)
                    rope_pair(PB[pb1][0:32, :], PB[pb2][0:32, :], [PBb[pb1]], [PBb[pb2]], C_KRN, 2, 3,
                              BT[5][0:32, :], BT[5][32:64, :], [BTb[5]], 0)
                    dma("sp", KRd[:, j * T:(j + 1) * T], BT[5][0:64, :], [BTb[5]], [dbuf("KRd", j)], sem_bt[5])
                    for h in range(NH):
                        pb = nextpb()
                        for k in range(3):
                            mm(PB[pb][:], wv["uq"][:, k, h * 192:h * 192 + 128], cqn[k][0], k == 0, k == 2,
                               [Wb[wslot]] + cqn[k][1], [PBb[pb]])
                        srcs = [(PB[pb][:], [PBb[pb]])]
                        norm_stats(srcs, 128, 128, 1, 6)
                        norm_apply(srcs, 128, 1, [C_QNN], [(BT[6][:], [BTb[6]])])
                        dma("sp", QNd[h * 128:(h + 1) * 128, j * T:(j + 1) * T], BT[6][:], [BTb[6]],
                            [dbuf("QNd", j)], sem_bt[6])
                        pb1, pb2 = nextpb(), nextpb()
                        for pbx, c0 in ((pb1, h * 192 + 128), (pb2, h * 192 + 160)):
                            for k in range(3):
                                mm(PB[pbx][0:32, :], wv["uq"][:, k, c0:c0 + 32], cqn[k][0], k == 0, k == 2,
                                   [Wb[wslot]] + cqn[k][1], [PBb[pbx]])
                        norm_stats([(PB[pb1][0:32, :], [PBb[pb1]]), (PB[pb2][0:32, :], [PBb[pb2]])], 32, 64, 0, 6)
                        rope_pair(PB[pb1][0:32, :], PB[pb2][0:32, :], [PBb[pb1]], [PBb[pb2]], C_QRN, 2, 3,
                                  BT[7][0:32, :], BT[7][32:64, :], [BTb[7]], 0)
                        dma("sp", QRd[h * 64:(h + 1) * 64, j * T:(j + 1) * T], BT[7][0:64, :], [BTb[7]],
                            [dbuf("QRd", j)], sem_bt[7])
                        pb = nextpb()
                        for k in range(2):
                            mm(PB[pb][:], wv["ukv"][:, k, h * 256:h * 256 + 128], ckvn[k][0], k == 0, k == 1,
                               [Wb[wslot]] + ckvn[k][1], [PBb[pb]])
                        srcs = [(PB[pb][:], [PBb[pb]])]
                        norm_stats(srcs, 128, 128, 1, 6)
                        norm_apply(srcs, 128, 1, [C_KNN], [(HB[:, 8, :], [HBb[8]])])
                        dma("sp", KNd[h * 128:(h + 1) * 128, j * T:(j + 1) * T], HB[:, 8, :], [HBb[8]],
                            [dbuf("KNd", j)], sem_hb)
                    vsrc = wv["ukv"].rearrange("p k (h e) -> p k h e", h=NH)
                    for cc in range(4):
                        for hh in range(2):
                            pb = nextpb()
                            for k in range(2):
                                mm(PB[pb][:].rearrange("p (h d) -> p h d", h=4),
                                   ckvn[k][0][:, cc * 128:(cc + 1) * 128],
                                   vsrc[:, k, hh * 4:(hh + 1) * 4, 128:256], k == 0, k == 1,
                                   [Wb[wslot]] + ckvn[k][1], [PBb[pb]])
                            act(HB[:, hh * 4:(hh + 1) * 4, cc * 128:(cc + 1) * 128],
                                PB[pb][:].rearrange("p (h d) -> p h d", h=4), AF.Copy,
                                [PBb[pb]], [HBb[hh * 4 + i] for i in range(4)])
                    dma("sp", Vd[:, :, j * T:(j + 1) * T].rearrange("h p t -> p h t"), HB[:, 0:8, :],
                        [HBb[i] for i in range(8)], [dbuf("Vd", j)], sem_hb)
                    if j + 1 < NT:
                        prologue(j + 1)
            return loadw, run

        def stage_mla2(wslot):
            W = Wt[wslot]
            o = [0]

            def carve(n):
                a = o[0]
                o[0] += n
                assert o[0] <= SLOT_ELEMS[wslot]
                return a
            KN_o = [carve(S), carve(S)]
            KR_o = carve(S)
            VR_o = carve(S)
            VH_o = [carve(32 * 129), carve(32 * 129)]
            KNb = [Buf("KN0"), Buf("KN1")]
            KRb = Buf("KRs")
            VRb = Buf("VRs")
            VHb = [Buf("VH0"), Buf("VH1")]
            sem_kn = [P.dma_sem("kn0"), P.dma_sem("kn1")]
            sem_kr = P.dma_sem("krs")
            sem_vr = P.dma_sem("vrs")
            wr = [Wb[wslot]]

            def loadw():
                pass

            def run():
                P.op("sp", lambda e: e.nop(), reads=[], writes=[Wb[wslot]])
                allkn = [dbuf("KNd", j) for j in range(NT)]
                allkr = [dbuf("KRd", j) for j in range(NT)]
                allv = [dbuf("Vd", j) for j in range(NT)]
                dma("sp", W[0:64, KR_o:KR_o + S], KRd, allkr + wr, [KRb], sem_kr)
                for i in range(2):
                    vh = W[:, VH_o[i]:VH_o[i] + 32 * 129].rearrange("p (c e) -> p c e", e=129)
                    P.op("dve", (lambda e, vh=vh: e.memset(vh[:, :, 128:129], 1.0)), reads=wr, writes=[VHb[i]])
                for h in range(NH):
                    hs_ = h % 2
                    kn = W[:, KN_o[hs_]:KN_o[hs_] + S]
                    dma("sp", kn, KNd[h * 128:(h + 1) * 128, :], allkn + wr, [KNb[hs_]], sem_kn[hs_])
                    dma("sp", W[:, VR_o:VR_o + S], Vd[h], allv + wr, [VRb], sem_vr)
                    vh = W[:, VH_o[hs_]:VH_o[hs_] + 32 * 129].rearrange("p (c e) -> p c e", e=129)
                    act(vh[:, :, 0:128], W[:, VR_o:VR_o + S].rearrange("p (c d) -> p c d", d=128), AF.Copy,
                        [VRb] + wr, [VHb[hs_]])
                    kr = W[0:64, KR_o:KR_o + S]
                    for j in range(NT):
                        qs = j % 2
                        qn_t, qr_t = 2 * qs, 2 * qs + 1
                        dma("sp", BT[qn_t][:], QNd[h * 128:(h + 1) * 128, j * T:(j + 1) * T],
                            [dbuf("QNd", j)], [BTb[qn_t]], sem_bt[qn_t])
                        dma("sp", BT[qr_t][0:64, :], QRd[h * 64:(h + 1) * 64, j * T:(j + 1) * T],
                            [dbuf("QRd", j)], [BTb[qr_t]], sem_bt[qr_t])
                        nkc = 4 * j + 4
                        for kc in range(nkc):
                            dgn = kc - 4 * j
                            q0 = max(dgn, 0) * 128
                            pb = kc % 3
                            mm(PB[pb][:, q0:T], kn[:, kc * 128:(kc + 1) * 128], BT[qn_t][:, q0:T], True, False,
                               [KNb[hs_], BTb[qn_t]] + wr, [PBb[pb]])
                            mm(PB[pb][:, q0:T], kr[:, kc * 128:(kc + 1) * 128], BT[qr_t][0:64, q0:T], False, True,
                               [KRb, BTb[qr_t]] + wr, [PBb[pb]])
                            pt = 4 + kc % 3
                            act(BT[pt][:, q0:T], PB[pb][:, q0:T], AF.Exp, [PBb[pb]], [BTb[pt]], scale=SM_SCALE)
                            if dgn >= 0:
                                tt(BT[pt][:, q0:q0 + 128], BT[pt][:, q0:q0 + 128], cmat[:, 128:256], ALU.mult,
                                   [BTb[pt], cmatb], [BTb[pt]])
                            for qb in range(max(dgn, 0), 4):
                                ob = 3 + qb // 2
                                oc = (qb % 2) * 129
                                mm(PB[ob][:, oc:oc + 129], BT[pt][:, qb * 128:(qb + 1) * 128], vh[:, kc, :],
                                   kc == 0 and qb % 2 == 0, kc == 4 * j + qb,
                                   [BTb[pt], VHb[hs_]] + wr, [PBb[ob]], skip=True)
                        for qb in range(4):
                            ob = 3 + qb // 2
                            oc = (qb % 2) * 129
                            P.op("dve", (lambda e, ob=ob, oc=oc: e.reciprocal(out=FT[0][:, 0:1],
                                                                            in_=PB[ob][:, oc + 128:oc + 129])),
                                 reads=[PBb[ob]], writes=[FTb[0]])
                            tsc(BT[7][:, 0:128], PB[ob][:, oc:oc + 128], FT[0][:, 0:1], None, ALU.mult, None,
                                [PBb[ob], FTb[0]], [BTb[7]])
                            P.op("pe", (lambda e, qb=qb: e.transpose(PT[:, qb * 128:(qb + 1) * 128],
                                                                     BT[7][:, 0:128], cmat[:, 0:128])),
                                 reads=[BTb[7], cmatb], writes=[PTb])
                        act(BT[8][:], PT[:, 0:T], AF.Copy, [PTb], [BTb[8]])
                        dma("sp", OTd[h * 128:(h + 1) * 128, j * T:(j + 1) * T], BT[8][:], [BTb[8]],
                            [dbuf("OTd", j)], sem_bt[8])
            return loadw, run

        def stage_mla3(src, dst, wslot):
            wv = {}

            def loadw():
                wv["o"] = wload(wslot, 0, NCH, D, kmaj(Wd["mla_w_o"][0]))

            def prologue(j):
                s = j % 2
                load_x(src[0], src[1], j, s)
                dma("sp", XN[s][:], dtile(OTd, j), [dbuf("OTd", j)], XNb[s], sem_xn[s])

            def run():
                prologue(0)
                for j in range(NT):
                    s = j % 2
                    if j + 1 < NT:
                        prologue(j + 1)
                    resid_out(s, lambda k, m: wv["o"][:, k, m * 128:(m + 1) * 128], NCH,
                              lambda k: XN[s][:, k, :], lambda k: [XNb[s][k]], wslot, (4, 5))
                    store_x(dst[0], dst[1], j, s)
            return loadw, run

        Rr = (R, "R")
        stages = []
        seq = [
            ("conv", 0, 0, (xT, "xT"), Rr),
            ("ffn", 0), ("lru", 1), ("ffn", 1),
            ("mla", 2), ("ffn", 2),
            ("conv", 3, 1, Rr, Rr),
            ("ffn", 3),
        ]
        plan = []
        for item in seq:
            if item[0] == "conv":
                plan.append(("conv", item[1], item[2], item[3], item[4]))
            elif item[0] == "lru":
                plan.append(("lru", item[1]))
            elif item[0] == "mla":
                plan += [("mla1", item[1]), ("mla2",), ("mla3",)]
            else:
                plan += [("ffn", p, item[1]) for p in range(3)]
        if stage_limit is not None:
            plan = plan[:stage_limit]
        built = []
        for si, pl in enumerate(plan):
            slot = si % 2
            last = (si == len(plan) - 1)
            final_dst = (out, "out") if last else Rr
            if pl[0] == "conv":
                built.append(stage_conv(pl[1], pl[2], pl[3], final_dst, slot))
            elif pl[0] == "lru":
                built.append(stage_lru(pl[1], Rr, final_dst, slot))
            elif pl[0] == "mla1":
                built.append(stage_mla1(pl[1], Rr, slot))
            elif pl[0] == "mla2":
                built.append(stage_mla2(slot))
            elif pl[0] == "mla3":
                built.append(stage_mla3(Rr, final_dst, slot))
            else:
                built.append(stage_ffn(pl[1], pl[2], Rr, final_dst, slot))
        built[0][0]()
        for si, (lw, rn) in enumerate(built):
            if si + 1 < len(built):
                built[si + 1][0]()
            rn()
        P.op("sp", lambda e: e.nop(), reads=[dbuf("out", j) for j in range(NT)])
        cnt = P.emit(nc, st)
        nc._mk_stats = (len(P.ops), cnt)
    return nc


def host_consts(inp, b):
    c = np.zeros((128, NCOL), np.float32)

    def put(col, vec):
        v = np.asarray(vec, np.float32).reshape(-1, 128)
        for i in range(v.shape[0]):
            c[:, col + i] = v[i]
    for l in range(4):
        put(C_MIXN + l * 8, inp["mix_norm"][l])
        put(C_FFNN + l * 8, inp["ffn_norm"][l])
    for jc in range(2):
        for tap in range(3):
            put(C_CONVW + jc * 24 + tap * 8, inp["conv_w"][jc, tap])
    for tap in range(4):
        put(C_LCW + tap * 10, inp["lru_conv_w"][0, tap])
    put(C_LCB, inp["lru_conv_b"][0])
    put(C_LGAB, inp["lru_gate_a_b"][0].reshape(-1))
    put(C_LGXB, inp["lru_gate_x_b"][0].reshape(-1))
    put(C_LLAM, inp["lru_lambda"][0])
    put(C_QNORM, inp["mla_q_norm"][0])
    put(C_KVNORM, inp["mla_kv_norm"][0])
    put(C_QNN, inp["mla_qn_norm"][0])
    put(C_KNN, inp["mla_kn_norm"][0])
    c[0:32, C_QRN] = inp["mla_qr_norm"][0][0:32]
    c[0:32, C_QRN + 1] = inp["mla_qr_norm"][0][32:64]
    c[0:32, C_KRN] = inp["mla_kr_norm"][0][0:32]
    c[0:32, C_KRN + 1] = inp["mla_kr_norm"][0][32:64]
    c[0:32, C_INVF] = (10000.0 ** (-np.arange(0, 64, 2, dtype=np.float32) / np.float32(64))).astype(np.float32)
    return c


def host_cmat():
    m = np.zeros((128, 256), np.float32)
    m[:, 0:128] = np.eye(128, dtype=np.float32)
    k = np.arange(128)[:, None]
    q = np.arange(128)[None, :]
    m[:, 128:256] = (k <= q).astype(np.float32)
    return m


_NC_CACHE = {}


def kernel(**inputs):
    inp = {k: np.asarray(v) for k, v in inputs.items()}
    n = 8
    if "nc" not in _NC_CACHE:
        _NC_CACHE["nc"] = build()
    nc = _NC_CACHE["nc"]
    cm = host_cmat()
    shared = {name: np.ascontiguousarray(inp[name], dtype=np.float32) for name in WEIGHT_NAMES}
    in_maps = []
    for b in range(n):
        m = dict(shared)
        m["xT"] = np.ascontiguousarray(inp["x"][b].T)
        m["pos"] = np.ascontiguousarray(inp["positions"][b].reshape(1, S).astype(np.int32))
        m["consts"] = host_consts(inp, b)
        m["cmat"] = cm
        in_maps.append(m)
    res = run_bass_kernel_spmd(nc, in_maps, core_ids=list(range(n)))
    outp = np.stack([np.ascontiguousarray(r["out"].T) for r in res.results], axis=0)
    return outp.astype(np.float32)
```

```python
import contextlib
import math
import numpy as np
import concourse.bass as bass
import concourse.mybir as mybir
from concourse.bass_utils import run_bass_kernel_spmd

F32 = mybir.dt.float32
BF16 = mybir.dt.bfloat16
I32 = mybir.dt.int32
AF = mybir.ActivationFunctionType
ALU = mybir.AluOpType

EPOCH = 12000
ENGS = ("pe", "act", "dve", "pool", "sp")


class Buf:
    __slots__ = ("name", "last_w", "readers")

    def __init__(self, name):
        self.name = name
        self.last_w = None
        self.readers = []


class DmaSem:
    __slots__ = ("name", "count", "handle", "last_op")

    def __init__(self, name):
        self.name = name
        self.count = 0
        self.handle = None
        self.last_op = None


class Op:
    __slots__ = ("eng", "fn", "reads", "writes", "dsem", "dval", "idx",
                 "deps", "signal", "sig", "waits")

    def __init__(self, eng, fn, reads, writes, dsem):
        self.eng = eng
        self.fn = fn
        self.reads = reads
        self.writes = writes
        self.dsem = dsem
        self.dval = None
        self.deps = ()
        self.signal = False
        self.sig = None
        self.waits = ()


class Prog:
    def __init__(self):
        self.ops = []
        self.dsems = []

    def dma_sem(self, name):
        s = DmaSem(name)
        self.dsems.append(s)
        return s

    def op(self, eng, fn, reads=(), writes=(), dsem=None):
        o = Op(eng, fn, tuple(reads), tuple(writes), dsem)
        o.idx = len(self.ops)
        self.ops.append(o)
        return o

    def analyze(self):
        ops = self.ops
        for o in ops:
            deps = set()
            for b in o.reads:
                if b.last_w is not None:
                    deps.add(b.last_w)
            for b in o.writes:
                if b.last_w is not None:
                    deps.add(b.last_w)
                deps.update(b.readers)
            if o.dsem is not None:
                if o.dsem.last_op is not None:
                    deps.add(o.dsem.last_op)
                o.dsem.last_op = o.idx
                o.dsem.count += 16
                o.dval = o.dsem.count
            deps.discard(o.idx)
            for b in o.reads:
                b.readers.append(o.idx)
            for b in o.writes:
                b.last_w = o.idx
                b.readers = []
            o.deps = deps
        known = {e: {} for e in ENGS}
        for o in ops:
            need = {}
            for d in o.deps:
                od = ops[d]
                if od.dsem is not None:
                    key = ("d", id(od.dsem))
                    val = od.dval
                else:
                    if od.eng == "pe" and o.eng == "pe":
                        continue
                    key = ("e", od.eng)
                    val = d
                if key not in need or need[key][0] < val:
                    need[key] = (val, d)
            o.waits = []
            k = known[o.eng]
            for key, (val, d) in need.items():
                if key in k and k[key] >= val:
                    continue
                k[key] = val
                o.waits.append(d)
                if ops[d].dsem is None:
                    ops[d].signal = True
        cnt = {e: 0 for e in ENGS}
        for o in ops:
            if o.dsem is None and o.signal:
                n = cnt[o.eng]
                cnt[o.eng] = n + 1
                o.sig = (n // EPOCH, n % EPOCH + 1)
        self.n_epochs = {e: (cnt[e] + EPOCH - 1) // EPOCH for e in ENGS}
        return cnt

    def emit(self, nc, stack):
        cnt = self.analyze()
        esem = {}
        for e in ENGS:
            esem[e] = [stack.enter_context(nc.semaphore(f"s_{e}{k}"))
                       for k in range(max(1, self.n_epochs[e]))]
        for s in self.dsems:
            if s.count:
                s.handle = stack.enter_context(nc.semaphore(f"d_{s.name}"))
        ops = self.ops
        by_eng = {e: [o for o in ops if o.eng == e] for e in ENGS}

        def run(e, engobj):
            for o in by_eng[e]:
                for d in o.waits:
                    od = ops[d]
                    if od.dsem is not None:
                        engobj.wait_ge(od.dsem.handle, od.dval)
                    else:
                        ep, v = od.sig
                        engobj.wait_ge(esem[od.eng][ep], v)
                ins = o.fn(engobj)
                if o.dsem is not None:
                    ins.then_inc(o.dsem.handle, 16)
                elif o.signal:
                    ins.then_inc(esem[e][o.sig[0]], 1)

        block = stack.enter_context(nc.Block())

        @block.tensor
        def _(eng):
            run("pe", eng)

        @block.scalar
        def _(eng):
            run("act", eng)

        @block.vector
        def _(eng):
            run("dve", eng)

        @block.gpsimd
        def _(eng):
            run("pool", eng)

        @block.sync
        def _(eng):
            run("sp", eng)
        return cnt


D = 1024
S = 4096
T = 512
NT = S // T
NCH = D // 128
DFF = 2816
NFF = DFF // 128
FFN_PARTS = ((0, 8), (8, 15), (15, 22))
LRU_W = 1280
NLC = 10
NH = 8
EPS = 1e-6
SM_SCALE = 1.0 / math.sqrt(192.0)
GELU_C = 0.044715
GELU_S = 2.0 * math.sqrt(2.0 / math.pi)

SLOT_ELEMS = (33792, 24704)

C_MIXN = 0
C_FFNN = 32
C_CONVW = 64
C_LCW = 112
C_LCB = 152
C_LGAB = 162
C_LGXB = 172
C_LLAM = 182
C_QNORM = 192
C_KVNORM = 195
C_QNN = 197
C_KNN = 198
C_QRN = 199
C_KRN = 201
C_INVF = 203
NCOL = 208

WEIGHT_NAMES = ("conv_w_in", "conv_w_out", "lru_w_in", "lru_gate_a_w", "lru_gate_x_w",
                "lru_w_out", "mla_w_down", "mla_w_uq", "mla_w_ukv", "mla_w_o",
                "ffn_w_gu", "ffn_w_down")
WEIGHT_SHAPES = {
    "conv_w_in": [2, 1024, 3072], "conv_w_out": [2, 1024, 1024],
    "lru_w_in": [1, 1024, 2560], "lru_gate_a_w": [1, 10, 128, 128],
    "lru_gate_x_w": [1, 10, 128, 128], "lru_w_out": [1, 1280, 1024],
    "mla_w_down": [1, 1024, 704], "mla_w_uq": [1, 384, 1536],
    "mla_w_ukv": [1, 256, 2048], "mla_w_o": [1, 1024, 1024],
    "ffn_w_gu": [4, 1024, 5632], "ffn_w_down": [4, 2816, 1024],
}


class TB:
    __slots__ = ("ap", "b")

    def __init__(self, ap, b):
        self.ap = ap
        self.b = b


def build(stage_limit=None, dbg_out=None):
    nc = bass.Bass("TRN2", target_bir_lowering=False)
    P = Prog()
    st = contextlib.ExitStack()

    def din(name, shape, dt=F32):
        return nc.dram_tensor(name, shape, dt, kind="ExternalInput").ap()

    def dscr(name, shape, dt):
        return nc.dram_tensor(name, shape, dt, kind="Internal").ap()

    xT = din("xT", [D, S])
    pos = din("pos", [1, S], I32)
    cst_d = din("consts", [128, NCOL])
    cmat_d = din("cmat", [128, 256])
    Wd = {n: din(n, WEIGHT_SHAPES[n]) for n in WEIGHT_NAMES}
    out = nc.dram_tensor("out", [D, S], F32, kind="ExternalOutput").ap()

    R = dscr("R", [D, S], F32)
    ACC = dscr("ACC", [D, S], F32)
    XNd = dscr("XNd", [D, S], BF16)
    QNd = dscr("QNd", [NH * 128, S], BF16)
    QRd = dscr("QRd", [NH * 64, S], BF16)
    KNd = dscr("KNd", [NH * 128, S], BF16)
    KRd = dscr("KRd", [64, S], BF16)
    Vd = dscr("Vd", [NH, 128, S], BF16)
    OTd = dscr("OTd", [D, S], BF16)
    dram_bufs = {}

    def dbuf(name, j):
        k = (name, j)
        if k not in dram_bufs:
            dram_bufs[k] = Buf(f"{name}{j}")
        return dram_bufs[k]

    with st:
        def sb(name, shape, dt):
            return st.enter_context(nc.sbuf_tensor(name, shape, dt))

        def ps(name, shape, dt):
            return st.enter_context(nc.psum_tensor(name, shape, dt))

        Wt = [sb("W0", [128, SLOT_ELEMS[0]], BF16), sb("W1", [128, SLOT_ELEMS[1]], BF16)]
        Wb = [Buf("W0"), Buf("W1")]
        XS = [sb(f"XS{i}", [128, NCH, T], F32) for i in range(2)]
        XSb = [[Buf(f"XS{i}_{c}") for c in range(NCH)] for i in range(2)]
        XN = [sb(f"XN{i}", [128, NCH, T], BF16) for i in range(2)]
        XNb = [[Buf(f"XN{i}_{c}") for c in range(NCH)] for i in range(2)]
        HB = sb("HB", [128, NLC, T], BF16)
        HBb = [Buf(f"HB{c}") for c in range(NLC)]
        NF = 8
        FT = [sb(f"FT{i}", [128, 516], F32) for i in range(NF)]
        FTb = [Buf(f"FT{i}") for i in range(NF)]
        NB = 10
        BT = [sb(f"BT{i}", [128, T], BF16) for i in range(NB)]
        BTb = [Buf(f"BT{i}") for i in range(NB)]
        cst = sb("cst", [128, NCOL], F32)
        cstb = Buf("cst")
        cmat = sb("cmatb", [128, 256], BF16)
        cmatb = Buf("cmat")
        ones = sb("ones", [128, 128], BF16)
        onesb = Buf("ones")
        halo_c = sb("halo_c", [128, NCH, 2], F32)
        halo_cb = [Buf(f"hc{c}") for c in range(NCH)]
        halo_r = sb("halo_r", [128, NLC, 3], F32)
        halo_rb = [Buf(f"hr{c}") for c in range(NLC)]
        hst = sb("hst", [128, NLC], F32)
        hstb = [Buf(f"hs{c}") for c in range(NLC)]
        ls8 = sb("ls8", [128, 2 * NLC], F32)
        ls8b = Buf("ls8")
        RS = [sb(f"RS{i}", [128, T], F32) for i in range(2)]
        RSb = [Buf(f"RS{i}") for i in range(2)]
        PB = [ps(f"PB{i}", [128, T], F32) for i in range(7)]
        PBb = [Buf(f"PB{i}") for i in range(7)]
        PT = ps("PTb", [128, 1024], BF16)
        PTb = Buf("PTb")

        sem_xs = [P.dma_sem(f"xs{i}") for i in range(2)]
        sem_xs_st = [P.dma_sem(f"xsst{i}") for i in range(2)]
        sem_xn = [P.dma_sem(f"xn{i}") for i in range(2)]
        sem_xn_st = [P.dma_sem(f"xnst{i}") for i in range(2)]
        sem_w = [P.dma_sem("w0"), P.dma_sem("w1")]
        sem_misc = P.dma_sem("misc")
        sem_bt = [P.dma_sem(f"bt{i}") for i in range(NB)]
        sem_hb = P.dma_sem("hbst")
        sem_pos = P.dma_sem("pos")

        def dtile(dr, j):
            return dr.rearrange("(c p) t -> p c t", p=128)[:, :, j * T:(j + 1) * T]

        def mm(o_ap, lhsT, rhs, start, stop, reads, writes, skip=False):
            P.op("pe", lambda e: e.matmul(o_ap, lhsT, rhs, start=start, stop=stop, skip_group_check=skip),
                 reads=reads, writes=writes)

        def act(o_ap, i_ap, func, reads, writes, bias=None, scale=None):
            kw = {}
            if bias is not None:
                kw["bias"] = bias
            if scale is not None:
                kw["scale"] = scale
            P.op("act", lambda e: e.activation(out=o_ap, in_=i_ap, func=func, **kw),
                 reads=reads, writes=writes)

        def tt(o_ap, a_ap, b_ap, op, reads, writes):
            P.op("dve", lambda e: e.tensor_tensor(out=o_ap, in0=a_ap, in1=b_ap, op=op),
                 reads=reads, writes=writes)

        def tsc(o_ap, a_ap, s1, s2, op0, op1, reads, writes):
            if s2 is None:
                P.op("dve", lambda e: e.tensor_scalar(out=o_ap, in0=a_ap, scalar1=s1, scalar2=None,
                                                      op0=op0), reads=reads, writes=writes)
            else:
                P.op("dve", lambda e: e.tensor_scalar(out=o_ap, in0=a_ap, scalar1=s1, scalar2=s2,
                                                      op0=op0, op1=op1), reads=reads, writes=writes)

        def stt(o_ap, a_ap, s_ap, b_ap, op0, op1, reads, writes):
            P.op("dve", lambda e: e.scalar_tensor_tensor(out=o_ap, in0=a_ap, scalar=s_ap, in1=b_ap,
                                                         op0=op0, op1=op1), reads=reads, writes=writes)

        def cpy(o_ap, i_ap, reads, writes):
            P.op("dve", lambda e: e.tensor_copy(out=o_ap, in_=i_ap), reads=reads, writes=writes)

        def dma(eng, o_ap, i_ap, reads, writes, sem):
            P.op(eng, lambda e: e.dma_start(out=o_ap, in_=i_ap), reads=reads, writes=writes, dsem=sem)

        def ccol(col, npart=128):
            return cst[0:npart, col:col + 1]

        dma("sp", cst[:], cst_d, [], [cstb], sem_misc)
        dma("pool", cmat[:], cmat_d, [], [cmatb], sem_w[0])
        P.op("dve", lambda e: e.memset(ones[:], 1.0), writes=[onesb])

        def wview(slot, off, k, n):
            return Wt[slot][:, off:off + k * n].rearrange("p (k n) -> p k n", k=k)

        def wload(slot, off, k, n, src):
            assert off + k * n <= SLOT_ELEMS[slot]
            v = wview(slot, off, k, n)
            for kk in range(k):
                dma("pool", v[:, kk, :], src[:, kk, :], [], [Wb[slot]], sem_w[slot])
            return v

        def kmaj(w2d, c0=None, c1=None):
            v = w2d.rearrange("(k p) n -> p k n", p=128)
            if c0 is not None:
                v = v[:, :, c0:c1]
            return v

        sq_ring = [0]

        def _rs(rs):
            if isinstance(rs, int):
                return RS[rs][:], RSb[rs]
            return rs.ap, rs.b

        def norm_stats(srcs, npart, dn, rs, ssb_i):
            rs_ap, rs_b = _rs(rs)
            n = len(srcs)
            for c, (ap, bufs) in enumerate(srcs):
                k = 8 + (sq_ring[0] % 2)
                sq_ring[0] += 1
                act(BT[k][0:npart, :], ap, AF.Square, bufs, [BTb[k]])
                mm(PB[ssb_i][:], ones[0:npart, :], BT[k][0:npart, :], c == 0, c == n - 1,
                   [BTb[k], onesb], [PBb[ssb_i]])
            act(rs_ap, PB[ssb_i][:], AF.Ln, [PBb[ssb_i]], [rs_b], bias=EPS, scale=1.0 / dn)
            act(rs_ap, rs_ap, AF.Exp, [rs_b], [rs_b], scale=-0.5)

        def norm_apply(srcs, npart, rs, gcols, outs):
            rs_ap, rs_b = _rs(rs)
            for (ap, bufs), gc, (oap, obufs) in zip(srcs, gcols, outs):
                stt(oap, ap, ccol(gc, npart), rs_ap[0:npart, :], ALU.mult, ALU.mult,
                    bufs + [rs_b, cstb], obufs)

        def load_x(src_d, src_name, j, slot):
            dma("sp", XS[slot][:], dtile(src_d, j), [dbuf(src_name, j)], XSb[slot], sem_xs[slot])

        def store_x(dst_d, dst_name, j, slot):
            dma("sp", dtile(dst_d, j), XS[slot][:], XSb[slot], [dbuf(dst_name, j)], sem_xs_st[slot])

        def xs_srcs(slot):
            return [(XS[slot][:, c, :], [XSb[slot][c]]) for c in range(NCH)]

        def xn_outs(slot):
            return [(XN[slot][:, c, :], [XNb[slot][c]]) for c in range(NCH)]

        def resid_out(slot, w_ap_fn, nk, rhs_fn, rhs_bufs, wslot, pbs):
            for m in range(NCH):
                pb = pbs[m % len(pbs)]
                for k in range(nk):
                    mm(PB[pb][:], w_ap_fn(k, m), rhs_fn(k), k == 0, k == nk - 1,
                       [Wb[wslot]] + rhs_bufs(k), [PBb[pb]])
                tt(XS[slot][:, m, :], XS[slot][:, m, :], PB[pb][:], ALU.add,
                   [XSb[slot][m], PBb[pb]], [XSb[slot][m]])

        def stage_ffn(part, l, src, dst, wslot, dst_is_out=False):
            f0, f1 = FFN_PARTS[part]
            nf = f1 - f0
            wv = {}

            def loadw():
                gu = Wd["ffn_w_gu"][l]
                wv["g"] = wload(wslot, 0, NCH, nf * 128, kmaj(gu, f0 * 128, f1 * 128))
                wv["u"] = wload(wslot, NCH * nf * 128, NCH, nf * 128,
                                kmaj(gu, DFF + f0 * 128, DFF + f1 * 128))
                wv["d"] = wload(wslot, 2 * NCH * nf * 128, nf, D,
                                Wd["ffn_w_down"][l][f0 * 128:f1 * 128, :].rearrange("(k p) n -> p k n", p=128))

            def prologue(j):
                s = j % 2
                if part == 0:
                    load_x(src[0], src[1], j, s)
                    norm_stats(xs_srcs(s), 128, D, s, 6)
                    norm_apply(xs_srcs(s), 128, s, [C_FFNN + l * 8 + c for c in range(NCH)], xn_outs(s))
                    dma("sp", dtile(XNd, j), XN[s][:], XNb[s], [dbuf("XNd", j)], sem_xn_st[s])
                else:
                    load_x(ACC, "ACC", j, s)
                    dma("sp", XN[s][:], dtile(XNd, j), [dbuf("XNd", j)], XNb[s], sem_xn[s])

            def run():
                prologue(0)
                for j in range(NT):
                    s = j % 2
                    for f in range(nf):
                        pg, pu = f % 2, 2 + f % 2
                        for k in range(NCH):
                            mm(PB[pg][:], wv["g"][:, k, f * 128:(f + 1) * 128], XN[s][:, k, :],
                               k == 0, k == NCH - 1, [Wb[wslot], XNb[s][k]], [PBb[pg]])
                        for k in range(NCH):
                            mm(PB[pu][:], wv["u"][:, k, f * 128:(f + 1) * 128], XN[s][:, k, :],
                               k == 0, k == NCH - 1, [Wb[wslot], XNb[s][k]], [PBb[pu]])
                        sg = f % 2
                        act(BT[sg][:], PB[pg][:], AF.Silu, [PBb[pg]], [BTb[sg]])
                        tt(HB[:, f, :], BT[sg][:], PB[pu][:], ALU.mult, [BTb[sg], PBb[pu]], [HBb[f]])
                        if f == 2 and j + 1 < NT:
                            prologue(j + 1)
                    resid_out(s, lambda k, m: wv["d"][:, k, m * 128:(m + 1) * 128], nf,
                              lambda k: HB[:, k, :], lambda k: [HBb[k]], wslot, (4, 5))
                    if part == 2:
                        store_x(dst[0], dst[1], j, s)
                    else:
                        store_x(ACC, "ACC", j, s)
            return loadw, run

        def stage_conv(l, jc, src, dst, wslot):
            wv = {}

            def loadw():
                wv["in"] = wload(wslot, 0, NCH, 3 * D, kmaj(Wd["conv_w_in"][jc]))
                wv["out"] = wload(wslot, NCH * 3 * D, NCH, D, kmaj(Wd["conv_w_out"][jc]))

            def prologue(j):
                s = j % 2
                load_x(src[0], src[1], j, s)
                norm_stats(xs_srcs(s), 128, D, s, 6)
                norm_apply(xs_srcs(s), 128, s, [C_MIXN + l * 8 + c for c in range(NCH)], xn_outs(s))

            def run():
                P.op("dve", lambda e: e.memset(halo_c[:], 0.0), writes=halo_cb)
                prologue(0)
                for j in range(NT):
                    s = j % 2
                    for c in range(NCH):
                        q = c % 2
                        pbB, pbC, pbH = 3 * q, 3 * q + 1, 3 * q + 2
                        for which, pb in ((0, pbB), (1, pbC), (2, pbH)):
                            col = which * D + c * 128
                            for k in range(NCH):
                                mm(PB[pb][:], wv["in"][:, k, col:col + 128], XN[s][:, k, :],
                                   k == 0, k == NCH - 1, [Wb[wslot], XNb[s][k]], [PBb[pb]])
                        cs, ut, t1 = 3 * q, 3 * q + 1, 3 * q + 2
                        act(FT[cs][:, 0:T], PB[pbC][:], AF.Copy, [PBb[pbC]], [FTb[cs]])
                        cpy(FT[ut][:, 0:2], halo_c[:, c, :], [halo_cb[c]], [FTb[ut]])
                        tt(FT[ut][:, 2:2 + T], FT[cs][:, 0:T], PB[pbH][:], ALU.mult,
                           [FTb[cs], PBb[pbH]], [FTb[ut]])
                        cpy(halo_c[:, c, :], FT[ut][:, T:T + 2], [FTb[ut]], [halo_cb[c]])
                        wc = C_CONVW + jc * 24 + c
                        tsc(FT[t1][:, 0:T], FT[ut][:, 2:2 + T], ccol(wc + 16), None, ALU.mult, None,
                            [FTb[ut], cstb], [FTb[t1]])
                        stt(FT[t1][:, 0:T], FT[ut][:, 1:1 + T], ccol(wc + 8), FT[t1][:, 0:T], ALU.mult, ALU.add,
                            [FTb[ut], FTb[t1], cstb], [FTb[t1]])
                        stt(FT[t1][:, 0:T], FT[ut][:, 0:T], ccol(wc), FT[t1][:, 0:T], ALU.mult, ALU.add,
                            [FTb[ut], FTb[t1], cstb], [FTb[t1]])
                        tt(HB[:, c, :], FT[t1][:, 0:T], PB[pbB][:], ALU.mult, [FTb[t1], PBb[pbB]], [HBb[c]])
                        if c == 2 and j + 1 < NT:
                            prologue(j + 1)
                    resid_out(s, lambda k, m: wv["out"][:, k, m * 128:(m + 1) * 128], NCH,
                              lambda k: HB[:, k, :], lambda k: [HBb[k]], wslot, (6, 0))
                    store_x(dst[0], dst[1], j, s)
            return loadw, run

        def stage_lru(l, src, dst, wslot):
            wv = {}

            def loadw():
                wv["in"] = wload(wslot, 0, NCH, 2 * LRU_W, kmaj(Wd["lru_w_in"][0]))
                o = NCH * 2 * LRU_W
                wv["a"] = wload(wslot, o, NLC, 128, Wd["lru_gate_a_w"][0].rearrange("n d e -> d n e"))
                o += NLC * 128
                wv["x"] = wload(wslot, o, NLC, 128, Wd["lru_gate_x_w"][0].rearrange("n d e -> d n e"))
                o += NLC * 128
                wv["out"] = wload(wslot, o, NLC, D, kmaj(Wd["lru_w_out"][0]))

            def prologue(j):
                s = j % 2
                load_x(src[0], src[1], j, s)
                norm_stats(xs_srcs(s), 128, D, s, 6)
                norm_apply(xs_srcs(s), 128, s, [C_MIXN + l * 8 + c for c in range(NCH)], xn_outs(s))

            def run():
                P.op("dve", lambda e: e.memset(halo_r[:], 0.0), writes=halo_rb)
                P.op("dve", lambda e: e.memset(hst[:], 0.0), writes=hstb)
                act(ls8[:, 0:NLC], cst[:, C_LLAM:C_LLAM + NLC], AF.Exp, [cstb], [ls8b], scale=-1.0)
                act(ls8[:, 0:NLC], ls8[:, 0:NLC], AF.Ln, [ls8b], [ls8b], bias=1.0)
                tsc(ls8[:, NLC:2 * NLC], ls8[:, 0:NLC], -16.0, None, ALU.mult, None, [ls8b], [ls8b])
                tsc(ls8[:, 0:NLC], ls8[:, 0:NLC], -8.0, None, ALU.mult, None, [ls8b], [ls8b])
                prologue(0)
                for j in range(NT):
                    s = j % 2
                    for c in range(NLC):
                        q = c % 2
                        pG, pR = q, 2 + q
                        pA, pX = 4, 5
                        f0, f1_, f2, f3 = 4 * q, 4 * q + 1, 4 * q + 2, 4 * q + 3
                        gtb, recb = 2 * q, 2 * q + 1
                        for which, pb in ((0, pG), (1, pR)):
                            col = which * LRU_W + c * 128
                            for k in range(NCH):
                                mm(PB[pb][:], wv["in"][:, k, col:col + 128], XN[s][:, k, :],
                                   k == 0, k == NCH - 1, [Wb[wslot], XNb[s][k]], [PBb[pb]])
                        A0, A1, A2, A3 = FT[f0][:, 0:T], FT[f1_][:, 0:T], FT[f2], FT[f3][:, 0:T]
                        act(A0, PB[pG][:], AF.Copy, [PBb[pG]], [FTb[f0]])
                        tt(A1, A0, A0, ALU.mult, [FTb[f0]], [FTb[f1_]])
                        tsc(A1, A1, GELU_C, 1.0, ALU.mult, ALU.add, [FTb[f1_]], [FTb[f1_]])
                        tt(A1, A1, A0, ALU.mult, [FTb[f1_], FTb[f0]], [FTb[f1_]])
                        act(A1, A1, AF.Sigmoid, [FTb[f1_]], [FTb[f1_]], scale=GELU_S)
                        tt(BT[gtb][:], A0, A1, ALU.mult, [FTb[f0], FTb[f1_]], [BTb[gtb]])
                        cpy(A2[:, 0:3], halo_r[:, c, :], [halo_rb[c]], [FTb[f2]])
                        act(A2[:, 3:3 + T], PB[pR][:], AF.Copy, [PBb[pR]], [FTb[f2]])
                        cpy(halo_r[:, c, :], A2[:, T:T + 3], [FTb[f2]], [halo_rb[c]])
                        tsc(A3, A2[:, 3:3 + T], ccol(C_LCW + 30 + c), ccol(C_LCB + c), ALU.mult, ALU.add,
                            [FTb[f2], cstb], [FTb[f3]])
                        for tap in (2, 1, 0):
                            stt(A3, A2[:, tap:tap + T], ccol(C_LCW + tap * 10 + c), A3, ALU.mult, ALU.add,
                                [FTb[f2], FTb[f3], cstb], [FTb[f3]])
                        act(BT[recb][:], A3, AF.Copy, [FTb[f3]], [BTb[recb]])
                        mm(PB[pA][:], wv["a"][:, c, :], BT[recb][:], True, True, [Wb[wslot], BTb[recb]], [PBb[pA]])
                        mm(PB[pX][:], wv["x"][:, c, :], BT[recb][:], True, True, [Wb[wslot], BTb[recb]], [PBb[pX]])
                        act(A0, PB[pA][:], AF.Sigmoid, [PBb[pA], cstb], [FTb[f0]], bias=ccol(C_LGAB + c))
                        act(A1, PB[pX][:], AF.Sigmoid, [PBb[pX], cstb], [FTb[f1_]], bias=ccol(C_LGXB + c))
                        A2t = A2[:, 0:T]
                        act(A2t, A0, AF.Exp, [FTb[f0], ls8b], [FTb[f2]], scale=ls8[:, c:c + 1])
                        act(A0, A0, AF.Exp, [FTb[f0], ls8b], [FTb[f0]], scale=ls8[:, NLC + c:NLC + c + 1])
                        act(A0, A0, AF.Sqrt, [FTb[f0]], [FTb[f0]], bias=1.0, scale=-1.0)
                        tt(A1, A1, A3, ALU.mult, [FTb[f1_], FTb[f3]], [FTb[f1_]])
                        tt(A1, A1, A0, ALU.mult, [FTb[f1_], FTb[f0]], [FTb[f1_]])
                        P.op("dve", (lambda e, A3=A3, A2t=A2t, A1=A1, c=c: e.tensor_tensor_scan(
                            out=A3, data0=A2t, data1=A1, initial=hst[:, c:c + 1], op0=ALU.mult, op1=ALU.add)),
                            reads=[FTb[f2], FTb[f1_], hstb[c]], writes=[FTb[f3]])
                        cpy(hst[:, c:c + 1], FT[f3][:, T - 1:T], [FTb[f3]], [hstb[c]])
                        tt(HB[:, c, :], BT[gtb][:], A3, ALU.mult, [BTb[gtb], FTb[f3]], [HBb[c]])
                        if c == 2 and j + 1 < NT:
                            prologue(j + 1)
                    resid_out(s, lambda k, m: wv["out"][:, k, m * 128:(m + 1) * 128], NLC,
                              lambda k: HB[:, k, :], lambda k: [HBb[k]], wslot, (6, 0))
                    store_x(dst[0], dst[1], j, s)
            return loadw, run

        def stage_mla1(l, src, wslot):
            wv = {}
            sem_stg = [P.dma_sem(f"stg{i}") for i in range(6)]

            def loadw():
                wv["down"] = wload(wslot, 0, NCH, 704, kmaj(Wd["mla_w_down"][0]))
                o = NCH * 704
                wv["uq"] = wload(wslot, o, 3, 1536, kmaj(Wd["mla_w_uq"][0]))
                o += 3 * 1536
                wv["ukv"] = wload(wslot, o, 2, 2048, kmaj(Wd["mla_w_ukv"][0]))

            rot = [0]
            srot = [0]

            def nextpb(n=5):
                r = rot[0] % n
                rot[0] += 1
                return r

            def nextss():
                r = 5 + srot[0] % 2
                srot[0] += 1
                return r

            def prologue(j):
                s = j % 2
                load_x(src[0], src[1], j, s)
                norm_stats(xs_srcs(s), 128, D, s, nextss())
                norm_apply(xs_srcs(s), 128, s, [C_MIXN + l * 8 + c for c in range(NCH)], xn_outs(s))

            def rope_pair(p1, p2, b1, b2, gcol, cosb, sinb, o1, o2, ob, rs, tmp):
                rs_ap, rs_b = _rs(rs)
                t1, t2, t3, t4 = [t.ap[0:32, :] for t in tmp]
                tb1, tb2, tb3, tb4 = [t.b for t in tmp]
                cs_, sn_ = FT[cosb][0:32, 0:T], FT[sinb][0:32, 0:T]
                stt(t1, p1, ccol(gcol, 32), rs_ap[0:32, :], ALU.mult, ALU.mult, b1 + [rs_b, cstb], [tb1])
                stt(t2, p2, ccol(gcol + 1, 32), rs_ap[0:32, :], ALU.mult, ALU.mult, b2 + [rs_b, cstb], [tb2])
                tt(t3, t2, sn_, ALU.mult, [tb2, FTb[sinb]], [tb3])
                tt(t4, t1, cs_, ALU.mult, [tb1, FTb[cosb]], [tb4])
                tt(o1, t4, t3, ALU.subtract, [tb4, tb3], ob)
                tt(t3, t1, sn_, ALU.mult, [tb1, FTb[sinb]], [tb3])
                tt(t4, t2, cs_, ALU.mult, [tb2, FTb[cosb]], [tb4])
                tt(o2, t4, t3, ALU.add, [tb4, tb3], ob)

            def run():
                prologue(0)
                for j in range(NT):
                    s = j % 2
                    o_ = 1 - s
                    xk = lambda k: XN[s][:, k, :]
                    xtmp = [TB(XS[o_][:, i, :], XSb[o_][i]) for i in range(NCH)]
                    stg = [TB(XN[o_][:, i, :], XNb[o_][i]) for i in range(6)]
                    ftmp = [TB(FT[4 + i][:, 0:T], FTb[4 + i]) for i in range(4)]
                    posi = FT[1][0:32, 0:T].bitcast(I32)
                    dma("sp", posi, pos[0:1, j * T:(j + 1) * T].broadcast_to([32, T]), [], [FTb[1]], sem_pos)
                    ang = FT[0][0:32, 0:T]
                    cpy(ang, posi, [FTb[1]], [FTb[0]])
                    tsc(ang, ang, ccol(C_INVF, 32), None, ALU.mult, None, [FTb[0], cstb], [FTb[0]])
                    MAGIC = 12582912.0
                    C1 = 6.28125
                    C2 = 2.0 * math.pi - C1
                    kk = FT[1][0:32, 0:T]
                    for dst, shift in ((3, 0.0), (2, 0.5 * math.pi)):
                        a2 = FT[dst][0:32, 0:T]
                        tsc(a2, ang, shift, None, ALU.add, None, [FTb[0]], [FTb[dst]])
                        tsc(kk, a2, 1.0 / (2.0 * math.pi), None, ALU.mult, None, [FTb[dst]], [FTb[1]])
                        tsc(kk, kk, MAGIC, None, ALU.add, None, [FTb[1]], [FTb[1]])
                        tsc(kk, kk, -MAGIC, None, ALU.add, None, [FTb[1]], [FTb[1]])
                        stt(a2, kk, -C1, a2, ALU.mult, ALU.add, [FTb[1], FTb[dst]], [FTb[dst]])
                        stt(a2, kk, -C2, a2, ALU.mult, ALU.add, [FTb[1], FTb[dst]], [FTb[dst]])
                        tsc(a2, a2, math.pi, -math.pi, ALU.min, ALU.max, [FTb[dst]], [FTb[dst]])
                        act(a2, a2, AF.Sin, [FTb[dst]], [FTb[dst]])
                    def proj_down(col, width, pb):
                        for k in range(NCH):
                            mm(PB[pb][0:width, :], wv["down"][:, k, col:col + width], xk(k),
                               k == 0, k == NCH - 1, [Wb[wslot], XNb[s][k]], [PBb[pb]])
                    pbs = [nextpb() for _ in range(3)]
                    for i, pb in enumerate(pbs):
                        proj_down(i * 128, 128, pb)
                    srcs = [(PB[pb][:], [PBb[pb]]) for pb in pbs]
                    norm_stats(srcs, 128, 384, xtmp[0], nextss())
                    cqn = [(BT[i][:], [BTb[i]]) for i in range(3)]
                    norm_apply(srcs, 128, xtmp[0], [C_QNORM + i for i in range(3)], cqn)
                    pbs = [nextpb() for _ in range(2)]
                    for i, pb in enumerate(pbs):
                        proj_down(384 + i * 128, 128, pb)
                    srcs = [(PB[pb][:], [PBb[pb]]) for pb in pbs]
                    norm_stats(srcs, 128, 256, xtmp[1], nextss())
                    ckvn = [(BT[3 + i][:], [BTb[3 + i]]) for i in range(2)]
                    norm_apply(srcs, 128, xtmp[1], [C_KVNORM + i for i in range(2)], ckvn)
                    pb1, pb2 = nextpb(), nextpb()
                    proj_down(640, 32, pb1)
                    proj_down(672, 32, pb2)
                    norm_stats([(PB[pb1][0:32, :], [PBb[pb1]]), (PB[pb2][0:32, :], [PBb[pb2]])], 32, 64,
                               xtmp[2], nextss())
                    rope_pair(PB[pb1][0:32, :], PB[pb2][0:32, :], [PBb[pb1]], [PBb[pb2]], C_KRN, 2, 3,
                              BT[5][0:32, :], BT[5][32:64, :], [BTb[5]], xtmp[2], ftmp)
                    dma("sp", KRd[:, j * T:(j + 1) * T], BT[5][0:64, :], [BTb[5]], [dbuf("KRd", j)], sem_bt[5])
                    for h in range(NH):
                        hp = h % 2
                        rs_q, rs_r = xtmp[hp], xtmp[2 + hp]
                        rtmp = ftmp if hp == 0 else xtmp[4:8]
                        qn_s, qr_s, kn_s = stg[hp], stg[2 + hp], stg[4 + hp]
                        pb = nextpb()
                        for k in range(3):
                            mm(PB[pb][:], wv["uq"][:, k, h * 192:h * 192 + 128], cqn[k][0], k == 0, k == 2,
                               [Wb[wslot]] + cqn[k][1], [PBb[pb]])
                        srcs = [(PB[pb][:], [PBb[pb]])]
                        norm_stats(srcs, 128, 128, rs_q, nextss())
                        norm_apply(srcs, 128, rs_q, [C_QNN], [(qn_s.ap, [qn_s.b])])
                        dma("sp", QNd[h * 128:(h + 1) * 128, j * T:(j + 1) * T], qn_s.ap, [qn_s.b],
                            [dbuf("QNd", j)], sem_stg[hp])
                        pb1, pb2 = nextpb(), nextpb()
                        for pbx, c0 in ((pb1, h * 192 + 128), (pb2, h * 192 + 160)):
                            for k in range(3):
                                mm(PB[pbx][0:32, :], wv["uq"][:, k, c0:c0 + 32], cqn[k][0], k == 0, k == 2,
                                   [Wb[wslot]] + cqn[k][1], [PBb[pbx]])
                        norm_stats([(PB[pb1][0:32, :], [PBb[pb1]]), (PB[pb2][0:32, :], [PBb[pb2]])], 32, 64,
                                   rs_r, nextss())
                        rope_pair(PB[pb1][0:32, :], PB[pb2][0:32, :], [PBb[pb1]], [PBb[pb2]], C_QRN, 2, 3,
                                  qr_s.ap[0:32, :], qr_s.ap[32:64, :], [qr_s.b], rs_r, rtmp)
                        dma("sp", QRd[h * 64:(h + 1) * 64, j * T:(j + 1) * T], qr_s.ap[0:64, :], [qr_s.b],
                            [dbuf("QRd", j)], sem_stg[2 + hp])
                        pb = nextpb()
                        for k in range(2):
                            mm(PB[pb][:], wv["ukv"][:, k, h * 256:h * 256 + 128], ckvn[k][0], k == 0, k == 1,
                               [Wb[wslot]] + ckvn[k][1], [PBb[pb]])
                        srcs = [(PB[pb][:], [PBb[pb]])]
                        norm_stats(srcs, 128, 128, rs_q, nextss())
                        norm_apply(srcs, 128, rs_q, [C_KNN], [(kn_s.ap, [kn_s.b])])
                        dma("sp", KNd[h * 128:(h + 1) * 128, j * T:(j + 1) * T], kn_s.ap, [kn_s.b],
                            [dbuf("KNd", j)], sem_stg[4 + hp])
                    vsrc = wv["ukv"].rearrange("p k (h e) -> p k h e", h=NH)
                    for cc in range(4):
                        for hh in range(2):
                            pb = nextpb()
                            for k in range(2):
                                mm(PB[pb][:].rearrange("p (h d) -> p h d", h=4),
                                   ckvn[k][0][:, cc * 128:(cc + 1) * 128],
                                   vsrc[:, k, hh * 4:(hh + 1) * 4, 128:256], k == 0, k == 1,
                                   [Wb[wslot]] + ckvn[k][1], [PBb[pb]])
                            act(HB[:, hh * 4:(hh + 1) * 4, cc * 128:(cc + 1) * 128],
                                PB[pb][:].rearrange("p (h d) -> p h d", h=4), AF.Copy,
                                [PBb[pb]], [HBb[hh * 4 + i] for i in range(4)])
                    dma("sp", Vd[:, :, j * T:(j + 1) * T].rearrange("h p t -> p h t"), HB[:, 0:8, :],
                        [HBb[i] for i in range(8)], [dbuf("Vd", j)], sem_hb)
                    if j + 1 < NT:
                        prologue(j + 1)
            return loadw, run

        def stage_mla2(wslot):
            W = Wt[wslot]
            o = [0]

            def carve(n):
                a = o[0]
                o[0] += n
                assert o[0] <= SLOT_ELEMS[wslot]
                return a
            KN_o = [carve(S), carve(S)]
            KR_o = carve(S)
            VR_o = [carve(S), carve(S)]
            KNb = [Buf("KN0"), Buf("KN1")]
            KRb = Buf("KRs")
            VRb = [Buf("VR0"), Buf("VR1")]
            sem_kn = [P.dma_sem("kn0"), P.dma_sem("kn1")]
            sem_kr = P.dma_sem("krs")
            sem_vr = [P.dma_sem("vr0"), P.dma_sem("vr1")]
            wr = [Wb[wslot]]
            allkn = [dbuf("KNd", j) for j in range(NT)]
            allkr = [dbuf("KRd", j) for j in range(NT)]
            allv = [dbuf("Vd", j) for j in range(NT)]
            kr = W[0:64, KR_o:KR_o + S]

            def kn_ap(h):
                return W[:, KN_o[h % 2]:KN_o[h % 2] + S]

            def v_ap(h):
                return W[:, VR_o[h % 2]:VR_o[h % 2] + S].rearrange("p (c d) -> p c d", d=128)

            def loadw():
                pass

            def load_head_kv(h):
                dma("sp", kn_ap(h), KNd[h * 128:(h + 1) * 128, :], allkn + wr, [KNb[h % 2]], sem_kn[h % 2])
                dma("sp", W[:, VR_o[h % 2]:VR_o[h % 2] + S], Vd[h], allv + wr, [VRb[h % 2]], sem_vr[h % 2])

            units = [(h, j) for h in range(NH) for j in range(NT)]
            G = [(u, kc) for u, (h, j) in enumerate(units) for kc in range(4 * j + 4)]

            def load_q(u):
                h, j = units[u]
                qs = u % 2
                dma("sp", BT[2 * qs][:], QNd[h * 128:(h + 1) * 128, j * T:(j + 1) * T],
                    [dbuf("QNd", j)], [BTb[2 * qs]], sem_bt[2 * qs])
                dma("sp", BT[2 * qs + 1][0:64, :], QRd[h * 64:(h + 1) * 64, j * T:(j + 1) * T],
                    [dbuf("QRd", j)], [BTb[2 * qs + 1]], sem_bt[2 * qs + 1])

            def emit_S(g):
                u, kc = G[g]
                h, j = units[u]
                if kc == 0:
                    if u + 1 < len(units):
                        load_q(u + 1)
                    if j == 1 and h + 1 < NH:
                        load_head_kv(h + 1)
                qs = u % 2
                qn_t, qr_t = 2 * qs, 2 * qs + 1
                q0 = max(kc - 4 * j, 0) * 128
                pb = g % 3
                mm(PB[pb][:, q0:T], kn_ap(h)[:, kc * 128:(kc + 1) * 128], BT[qn_t][:, q0:T], True, False,
                   [KNb[h % 2], BTb[qn_t]] + wr, [PBb[pb]])
                mm(PB[pb][:, q0:T], kr[:, kc * 128:(kc + 1) * 128], BT[qr_t][0:64, q0:T], False, True,
                   [KRb, BTb[qr_t]] + wr, [PBb[pb]])

            def emit_exp(g):
                u, kc = G[g]
                h, j = units[u]
                dgn = kc - 4 * j
                q0 = max(dgn, 0) * 128
                pb = g % 3
                pt = 4 + g % 3
                act(BT[pt][:, q0:T], PB[pb][:, q0:T], AF.Exp, [PBb[pb]], [BTb[pt]], scale=SM_SCALE)
                if dgn >= 0:
                    tt(BT[pt][:, q0:q0 + 128], BT[pt][:, q0:q0 + 128], cmat[:, 128:256], ALU.mult,
                       [BTb[pt], cmatb], [BTb[pt]])

            def emit_PV(g):
                u, kc = G[g]
                h, j = units[u]
                q0 = max(kc - 4 * j, 0) * 128
                pt = 4 + g % 3
                ob = 3 + (u % 2)
                db = 5 + (u % 2)
                last = kc == 4 * j + 3
                mm(PB[ob][:, q0:T], v_ap(h)[:, kc, :], BT[pt][:, q0:T], kc == 0, last,
                   [BTb[pt], VRb[h % 2]] + wr, [PBb[ob]], skip=True)
                mm(PB[db][:, q0:T], ones[:, :], BT[pt][:, q0:T], kc == 0, last,
                   [BTb[pt], onesb], [PBb[db]], skip=True)

            def emit_tail(u):
                h, j = units[u]
                ob = 3 + (u % 2)
                db = 5 + (u % 2)
                rd = u % 2
                P.op("dve", (lambda e, db=db, rd=rd: e.reciprocal(out=FT[rd][:, 0:T], in_=PB[db][:])),
                     reads=[PBb[db]], writes=[FTb[rd]])
                ot = 8 + u % 2
                tt(BT[ot][:], PB[ob][:], FT[rd][:, 0:T], ALU.mult, [PBb[ob], FTb[rd]], [BTb[ot]])
                dma("sp", OTd[h * 128:(h + 1) * 128, j * T:(j + 1) * T], BT[ot][:], [BTb[ot]],
                    [dbuf("OTd", j)], sem_bt[ot])

            def run():
                P.op("sp", lambda e: e.nop(), reads=[], writes=[Wb[wslot]])
                dma("sp", kr, KRd, allkr + wr, [KRb], sem_kr)
                load_head_kv(0)
                load_q(0)
                n = len(G)
                LOOK = 2
                for g in range(min(LOOK, n)):
                    emit_S(g)
                for g in range(n):
                    u, kc = G[g]
                    h, j = units[u]
                    emit_exp(g)
                    if g + LOOK < n:
                        emit_S(g + LOOK)
                    emit_PV(g)
                    if kc == 4 * j + 3:
                        emit_tail(u)
            return loadw, run

        def stage_mla3(src, dst, wslot):
            wv = {}

            def loadw():
                wv["o"] = wload(wslot, 0, NCH, D, kmaj(Wd["mla_w_o"][0]))

            def prologue(j):
                s = j % 2
                load_x(src[0], src[1], j, s)
                dma("sp", XN[s][:], dtile(OTd, j), [dbuf("OTd", j)], XNb[s], sem_xn[s])

            def run():
                prologue(0)
                for j in range(NT):
                    s = j % 2
                    if j + 1 < NT:
                        prologue(j + 1)
                    resid_out(s, lambda k, m: wv["o"][:, k, m * 128:(m + 1) * 128], NCH,
                              lambda k: XN[s][:, k, :], lambda k: [XNb[s][k]], wslot, (4, 5))
                    store_x(dst[0], dst[1], j, s)
            return loadw, run

        Rr = (R, "R")
        stages = []
        seq = [
            ("conv", 0, 0, (xT, "xT"), Rr),
            ("ffn", 0), ("lru", 1), ("ffn", 1),
            ("mla", 2), ("ffn", 2),
            ("conv", 3, 1, Rr, Rr),
            ("ffn", 3),
        ]
        plan = []
        for item in seq:
            if item[0] == "conv":
                plan.append(("conv", item[1], item[2], item[3], item[4]))
            elif item[0] == "lru":
                plan.append(("lru", item[1]))
            elif item[0] == "mla":
                plan += [("mla1", item[1]), ("mla2",), ("mla3",)]
            else:
                plan += [("ffn", p, item[1]) for p in range(3)]
        if stage_limit is not None:
            plan = plan[:stage_limit]
        built = []
        for si, pl in enumerate(plan):
            slot = si % 2
            last = (si == len(plan) - 1)
            final_dst = (out, "out") if last else Rr
            if pl[0] == "conv":
                built.append(stage_conv(pl[1], pl[2], pl[3], final_dst, slot))
            elif pl[0] == "lru":
                built.append(stage_lru(pl[1], Rr, final_dst, slot))
            elif pl[0] == "mla1":
                built.append(stage_mla1(pl[1], Rr, slot))
            elif pl[0] == "mla2":
                built.append(stage_mla2(slot))
            elif pl[0] == "mla3":
                built.append(stage_mla3(Rr, final_dst, slot))
            else:
                built.append(stage_ffn(pl[1], pl[2], Rr, final_dst, slot))
        built[0][0]()
        for si, (lw, rn) in enumerate(built):
            if si + 1 < len(built):
                built[si + 1][0]()
            rn()
        P.op("sp", lambda e: e.nop(), reads=[dbuf("out", j) for j in range(NT)])
        cnt = P.emit(nc, st)
        nc._mk_stats = (len(P.ops), cnt)
    return nc


def host_consts(inp, b):
    c = np.zeros((128, NCOL), np.float32)

    def put(col, vec):
        v = np.asarray(vec, np.float32).reshape(-1, 128)
        for i in range(v.shape[0]):
            c[:, col + i] = v[i]
    for l in range(4):
        put(C_MIXN + l * 8, inp["mix_norm"][l])
        put(C_FFNN + l * 8, inp["ffn_norm"][l])
    for jc in range(2):
        for tap in range(3):
            put(C_CONVW + jc * 24 + tap * 8, inp["conv_w"][jc, tap])
    for tap in range(4):
        put(C_LCW + tap * 10, inp["lru_conv_w"][0, tap])
    put(C_LCB, inp["lru_conv_b"][0])
    put(C_LGAB, inp["lru_gate_a_b"][0].reshape(-1))
    put(C_LGXB, inp["lru_gate_x_b"][0].reshape(-1))
    put(C_LLAM, inp["lru_lambda"][0])
    put(C_QNORM, inp["mla_q_norm"][0])
    put(C_KVNORM, inp["mla_kv_norm"][0])
    put(C_QNN, inp["mla_qn_norm"][0])
    put(C_KNN, inp["mla_kn_norm"][0])
    c[0:32, C_QRN] = inp["mla_qr_norm"][0][0:32]
    c[0:32, C_QRN + 1] = inp["mla_qr_norm"][0][32:64]
    c[0:32, C_KRN] = inp["mla_kr_norm"][0][0:32]
    c[0:32, C_KRN + 1] = inp["mla_kr_norm"][0][32:64]
    c[0:32, C_INVF] = (10000.0 ** (-np.arange(0, 64, 2, dtype=np.float32) / np.float32(64))).astype(np.float32)
    return c


def host_cmat():
    m = np.zeros((128, 256), np.float32)
    m[:, 0:128] = np.eye(128, dtype=np.float32)
    k = np.arange(128)[:, None]
    q = np.arange(128)[None, :]
    m[:, 128:256] = (k <= q).astype(np.float32)
    return m


_NC_CACHE = {}


def kernel(**inputs):
    inp = {k: np.asarray(v) for k, v in inputs.items()}
    n = 8
    if "nc" not in _NC_CACHE:
        _NC_CACHE["nc"] = build()
    nc = _NC_CACHE["nc"]
    cm = host_cmat()
    shared = {name: np.ascontiguousarray(inp[name], dtype=np.float32) for name in WEIGHT_NAMES}
    in_maps = []
    for b in range(n):
        m = dict(shared)
        m["xT"] = np.ascontiguousarray(inp["x"][b].T)
        m["pos"] = np.ascontiguousarray(inp["positions"][b].reshape(1, S).astype(np.int32))
        m["consts"] = host_consts(inp, b)
        m["cmat"] = cm
        in_maps.append(m)
    res = run_bass_kernel_spmd(nc, in_maps, core_ids=list(range(n)))
    outp = np.stack([np.ascontiguousarray(r["out"].T) for r in res.results], axis=0)
    return outp.astype(np.float32)
```

```python
import contextlib
import math
import numpy as np
import concourse.bass as bass
import concourse.mybir as mybir
from concourse.bass_utils import run_bass_kernel_spmd

F32 = mybir.dt.float32
BF16 = mybir.dt.bfloat16
I32 = mybir.dt.int32
AF = mybir.ActivationFunctionType
ALU = mybir.AluOpType

EPOCH = 12000
ENGS = ("pe", "act", "dve", "pool", "sp")


class Buf:
    __slots__ = ("name", "last_w", "readers")

    def __init__(self, name):
        self.name = name
        self.last_w = None
        self.readers = []


class DmaSem:
    __slots__ = ("name", "count", "handle", "last_op")

    def __init__(self, name):
        self.name = name
        self.count = 0
        self.handle = None
        self.last_op = None


class Op:
    __slots__ = ("eng", "fn", "reads", "writes", "dsem", "dval", "idx",
                 "deps", "signal", "sig", "waits")

    def __init__(self, eng, fn, reads, writes, dsem):
        self.eng = eng
        self.fn = fn
        self.reads = reads
        self.writes = writes
        self.dsem = dsem
        self.dval = None
        self.deps = ()
        self.signal = False
        self.sig = None
        self.waits = ()


class Prog:
    def __init__(self):
        self.ops = []
        self.dsems = []

    def dma_sem(self, name):
        s = DmaSem(name)
        self.dsems.append(s)
        return s

    def op(self, eng, fn, reads=(), writes=(), dsem=None):
        o = Op(eng, fn, tuple(reads), tuple(writes), dsem)
        o.idx = len(self.ops)
        self.ops.append(o)
        return o

    def analyze(self):
        ops = self.ops
        for o in ops:
            deps = set()
            for b in o.reads:
                if b.last_w is not None:
                    deps.add(b.last_w)
            for b in o.writes:
                if b.last_w is not None:
                    deps.add(b.last_w)
                deps.update(b.readers)
            if o.dsem is not None:
                if o.dsem.last_op is not None:
                    deps.add(o.dsem.last_op)
                o.dsem.last_op = o.idx
                o.dsem.count += 16
                o.dval = o.dsem.count
            deps.discard(o.idx)
            for b in o.reads:
                b.readers.append(o.idx)
            for b in o.writes:
                b.last_w = o.idx
                b.readers = []
            o.deps = deps
        known = {e: {} for e in ENGS}
        for o in ops:
            need = {}
            for d in o.deps:
                od = ops[d]
                if od.dsem is not None:
                    key = ("d", id(od.dsem))
                    val = od.dval
                else:
                    if od.eng == "pe" and o.eng == "pe":
                        continue
                    key = ("e", od.eng)
                    val = d
                if key not in need or need[key][0] < val:
                    need[key] = (val, d)
            o.waits = []
            k = known[o.eng]
            for key, (val, d) in need.items():
                if key in k and k[key] >= val:
                    continue
                k[key] = val
                o.waits.append(d)
                if ops[d].dsem is None:
                    ops[d].signal = True
        cnt = {e: 0 for e in ENGS}
        for o in ops:
            if o.dsem is None and o.signal:
                n = cnt[o.eng]
                cnt[o.eng] = n + 1
                o.sig = (n // EPOCH, n % EPOCH + 1)
        self.n_epochs = {e: (cnt[e] + EPOCH - 1) // EPOCH for e in ENGS}
        return cnt

    def emit(self, nc, stack):
        cnt = self.analyze()
        esem = {}
        for e in ENGS:
            esem[e] = [stack.enter_context(nc.semaphore(f"s_{e}{k}"))
                       for k in range(max(1, self.n_epochs[e]))]
        for s in self.dsems:
            if s.count:
                s.handle = stack.enter_context(nc.semaphore(f"d_{s.name}"))
        ops = self.ops
        by_eng = {e: [o for o in ops if o.eng == e] for e in ENGS}

        def run(e, engobj):
            for o in by_eng[e]:
                for d in o.waits:
                    od = ops[d]
                    if od.dsem is not None:
                        engobj.wait_ge(od.dsem.handle, od.dval)
                    else:
                        ep, v = od.sig
                        engobj.wait_ge(esem[od.eng][ep], v)
                ins = o.fn(engobj)
                if o.dsem is not None:
                    ins.then_inc(o.dsem.handle, 16)
                elif o.signal:
                    ins.then_inc(esem[e][o.sig[0]], 1)

        block = stack.enter_context(nc.Block())

        @block.tensor
        def _(eng):
            run("pe", eng)

        @block.scalar
        def _(eng):
            run("act", eng)

        @block.vector
        def _(eng):
            run("dve", eng)

        @block.gpsimd
        def _(eng):
            run("pool", eng)

        @block.sync
        def _(eng):
            run("sp", eng)
        return cnt


D = 1024
S = 4096
T = 512
NT = S // T
NCH = D // 128
DFF = 2816
NFF = DFF // 128
FFN_PARTS = ((0, 8), (8, 15), (15, 22))
LRU_W = 1280
NLC = 10
NH = 8
EPS = 1e-6
SM_SCALE = 1.0 / math.sqrt(192.0)
GELU_C = 0.044715
GELU_S = 2.0 * math.sqrt(2.0 / math.pi)

SLOT_ELEMS = (33792, 24704)

C_MIXN = 0
C_FFNN = 32
C_CONVW = 64
C_LCW = 112
C_LCB = 152
C_LGAB = 162
C_LGXB = 172
C_LLAM = 182
C_QNORM = 192
C_KVNORM = 195
C_QNN = 197
C_KNN = 198
C_QRN = 199
C_KRN = 201
C_INVF = 203
NCOL = 208

WEIGHT_NAMES = ("conv_w_in", "conv_w_out", "lru_w_in", "lru_gate_a_w", "lru_gate_x_w",
                "lru_w_out", "mla_w_down", "mla_w_uq", "mla_w_ukv", "mla_w_o",
                "ffn_w_gu", "ffn_w_down")
WEIGHT_SHAPES = {
    "conv_w_in": [2, 1024, 3072], "conv_w_out": [2, 1024, 1024],
    "lru_w_in": [1, 1024, 2560], "lru_gate_a_w": [1, 10, 128, 128],
    "lru_gate_x_w": [1, 10, 128, 128], "lru_w_out": [1, 1280, 1024],
    "mla_w_down": [1, 1024, 704], "mla_w_uq": [1, 384, 1536],
    "mla_w_ukv": [1, 256, 2048], "mla_w_o": [1, 1024, 1024],
    "ffn_w_gu": [4, 1024, 5632], "ffn_w_down": [4, 2816, 1024],
}


class TB:
    __slots__ = ("ap", "b")

    def __init__(self, ap, b):
        self.ap = ap
        self.b = b


def build(stage_limit=None, dbg_out=None):
    nc = bass.Bass("TRN2", target_bir_lowering=False)
    P = Prog()
    st = contextlib.ExitStack()

    def din(name, shape, dt=F32):
        return nc.dram_tensor(name, shape, dt, kind="ExternalInput").ap()

    def dscr(name, shape, dt):
        return nc.dram_tensor(name, shape, dt, kind="Internal").ap()

    xT = din("xT", [D, S])
    pos = din("pos", [1, S], I32)
    cst_d = din("consts", [128, NCOL])
    cmat_d = din("cmat", [128, 256])
    Wd = {n: din(n, WEIGHT_SHAPES[n]) for n in WEIGHT_NAMES}
    out = nc.dram_tensor("out", [D, S], F32, kind="ExternalOutput").ap()

    R = dscr("R", [D, S], F32)
    ACC = dscr("ACC", [D, S], F32)
    XNd = dscr("XNd", [D, S], BF16)
    QNd = dscr("QNd", [NH * 128, S], BF16)
    QRd = dscr("QRd", [NH * 64, S], BF16)
    KNd = dscr("KNd", [NH * 128, S], BF16)
    KRd = dscr("KRd", [64, S], BF16)
    Vd = dscr("Vd", [NH, 128, S], BF16)
    OTd = dscr("OTd", [D, S], BF16)
    dram_bufs = {}

    def dbuf(name, j):
        k = (name, j)
        if k not in dram_bufs:
            dram_bufs[k] = Buf(f"{name}{j}")
        return dram_bufs[k]

    with st:
        def sb(name, shape, dt):
            return st.enter_context(nc.sbuf_tensor(name, shape, dt))

        def ps(name, shape, dt):
            return st.enter_context(nc.psum_tensor(name, shape, dt))

        Wt = [sb("W0", [128, SLOT_ELEMS[0]], BF16), sb("W1", [128, SLOT_ELEMS[1]], BF16)]
        Wb = [Buf("W0"), Buf("W1")]
        XS = [sb(f"XS{i}", [128, NCH, T], F32) for i in range(2)]
        XSb = [[Buf(f"XS{i}_{c}") for c in range(NCH)] for i in range(2)]
        XN = [sb(f"XN{i}", [128, NCH, T], BF16) for i in range(2)]
        XNb = [[Buf(f"XN{i}_{c}") for c in range(NCH)] for i in range(2)]
        HB = sb("HB", [128, NLC, T], BF16)
        HBb = [Buf(f"HB{c}") for c in range(NLC)]
        NF = 8
        FT = [sb(f"FT{i}", [128, 516], F32) for i in range(NF)]
        FTb = [Buf(f"FT{i}") for i in range(NF)]
        NB = 10
        BT = [sb(f"BT{i}", [128, T], BF16) for i in range(NB)]
        BTb = [Buf(f"BT{i}") for i in range(NB)]
        cst = sb("cst", [128, NCOL], F32)
        cstb = Buf("cst")
        cmat = sb("cmatb", [128, 256], BF16)
        cmatb = Buf("cmat")
        ones = sb("ones", [128, 128], BF16)
        onesb = Buf("ones")
        halo_c = sb("halo_c", [128, NCH, 2], F32)
        halo_cb = [Buf(f"hc{c}") for c in range(NCH)]
        halo_r = sb("halo_r", [128, NLC, 3], F32)
        halo_rb = [Buf(f"hr{c}") for c in range(NLC)]
        hst = sb("hst", [128, NLC], F32)
        hstb = [Buf(f"hs{c}") for c in range(NLC)]
        ls8 = sb("ls8", [128, 2 * NLC], F32)
        ls8b = Buf("ls8")
        RS = [sb(f"RS{i}", [128, T], F32) for i in range(2)]
        RSb = [Buf(f"RS{i}") for i in range(2)]
        PB = [ps(f"PB{i}", [128, T], F32) for i in range(7)]
        PBb = [Buf(f"PB{i}") for i in range(7)]
        PT = ps("PTb", [128, 1024], BF16)
        PTb = Buf("PTb")

        sem_xs = [P.dma_sem(f"xs{i}") for i in range(2)]
        sem_xs_st = [P.dma_sem(f"xsst{i}") for i in range(2)]
        sem_xn = [P.dma_sem(f"xn{i}") for i in range(2)]
        sem_xn_st = [P.dma_sem(f"xnst{i}") for i in range(2)]
        sem_w = [P.dma_sem("w0"), P.dma_sem("w1")]
        sem_misc = P.dma_sem("misc")
        sem_bt = [P.dma_sem(f"bt{i}") for i in range(NB)]
        sem_hb = P.dma_sem("hbst")
        sem_pos = P.dma_sem("pos")

        def dtile(dr, j):
            return dr.rearrange("(c p) t -> p c t", p=128)[:, :, j * T:(j + 1) * T]

        def mm(o_ap, lhsT, rhs, start, stop, reads, writes, skip=False):
            P.op("pe", lambda e: e.matmul(o_ap, lhsT, rhs, start=start, stop=stop, skip_group_check=skip),
                 reads=reads, writes=writes)

        def act(o_ap, i_ap, func, reads, writes, bias=None, scale=None):
            kw = {}
            if bias is not None:
                kw["bias"] = bias
            if scale is not None:
                kw["scale"] = scale
            P.op("act", lambda e: e.activation(out=o_ap, in_=i_ap, func=func, **kw),
                 reads=reads, writes=writes)

        def tt(o_ap, a_ap, b_ap, op, reads, writes):
            P.op("dve", lambda e: e.tensor_tensor(out=o_ap, in0=a_ap, in1=b_ap, op=op),
                 reads=reads, writes=writes)

        def tsc(o_ap, a_ap, s1, s2, op0, op1, reads, writes):
            if s2 is None:
                P.op("dve", lambda e: e.tensor_scalar(out=o_ap, in0=a_ap, scalar1=s1, scalar2=None,
                                                      op0=op0), reads=reads, writes=writes)
            else:
                P.op("dve", lambda e: e.tensor_scalar(out=o_ap, in0=a_ap, scalar1=s1, scalar2=s2,
                                                      op0=op0, op1=op1), reads=reads, writes=writes)

        def stt(o_ap, a_ap, s_ap, b_ap, op0, op1, reads, writes):
            P.op("dve", lambda e: e.scalar_tensor_tensor(out=o_ap, in0=a_ap, scalar=s_ap, in1=b_ap,
                                                         op0=op0, op1=op1), reads=reads, writes=writes)

        def cpy(o_ap, i_ap, reads, writes):
            P.op("dve", lambda e: e.tensor_copy(out=o_ap, in_=i_ap), reads=reads, writes=writes)

        def dma(eng, o_ap, i_ap, reads, writes, sem):
            P.op(eng, lambda e: e.dma_start(out=o_ap, in_=i_ap), reads=reads, writes=writes, dsem=sem)

        def ccol(col, npart=128):
            return cst[0:npart, col:col + 1]

        dma("sp", cst[:], cst_d, [], [cstb], sem_misc)
        dma("pool", cmat[:], cmat_d, [], [cmatb], sem_w[0])
        P.op("dve", lambda e: e.memset(ones[:], 1.0), writes=[onesb])

        def wview(slot, off, k, n):
            return Wt[slot][:, off:off + k * n].rearrange("p (k n) -> p k n", k=k)

        def wload(slot, off, k, n, src):
            assert off + k * n <= SLOT_ELEMS[slot]
            v = wview(slot, off, k, n)
            for kk in range(k):
                dma("pool", v[:, kk, :], src[:, kk, :], [], [Wb[slot]], sem_w[slot])
            return v

        def kmaj(w2d, c0=None, c1=None):
            v = w2d.rearrange("(k p) n -> p k n", p=128)
            if c0 is not None:
                v = v[:, :, c0:c1]
            return v

        sq_ring = [0]

        def _rs(rs):
            if isinstance(rs, int):
                return RS[rs][:], RSb[rs]
            return rs.ap, rs.b

        def norm_stats(srcs, npart, dn, rs, ssb_i):
            rs_ap, rs_b = _rs(rs)
            n = len(srcs)
            for c, (ap, bufs) in enumerate(srcs):
                k = 8 + (sq_ring[0] % 2)
                sq_ring[0] += 1
                act(BT[k][0:npart, :], ap, AF.Square, bufs, [BTb[k]])
                mm(PB[ssb_i][:], ones[0:npart, :], BT[k][0:npart, :], c == 0, c == n - 1,
                   [BTb[k], onesb], [PBb[ssb_i]])
            act(rs_ap, PB[ssb_i][:], AF.Ln, [PBb[ssb_i]], [rs_b], bias=EPS, scale=1.0 / dn)
            act(rs_ap, rs_ap, AF.Exp, [rs_b], [rs_b], scale=-0.5)

        def norm_apply(srcs, npart, rs, gcols, outs):
            rs_ap, rs_b = _rs(rs)
            for (ap, bufs), gc, (oap, obufs) in zip(srcs, gcols, outs):
                stt(oap, ap, ccol(gc, npart), rs_ap[0:npart, :], ALU.mult, ALU.mult,
                    bufs + [rs_b, cstb], obufs)

        def load_x(src_d, src_name, j, slot):
            dma("sp", XS[slot][:], dtile(src_d, j), [dbuf(src_name, j)], XSb[slot], sem_xs[slot])

        STQ = "sp"

        def store_x(dst_d, dst_name, j, slot):
            dma(STQ, dtile(dst_d, j), XS[slot][:], XSb[slot], [dbuf(dst_name, j)], sem_xs_st[slot])

        def make_norm_pro(src, gbase, extra=None):
            def pa(j):
                load_x(src[0], src[1], j, j % 2)

            def pb(j):
                s_ = j % 2
                for c in range(NCH):
                    act(XN[s_][:, c, :], XS[s_][:, c, :], AF.Square, [XSb[s_][c]], [XNb[s_][c]])

            def pc(j):
                s_ = j % 2
                for c in range(NCH):
                    mm(PB[6][:], ones[:, :], XN[s_][:, c, :], c == 0, c == NCH - 1,
                       [XNb[s_][c], onesb], [PBb[6]])
                act(RS[s_][:], PB[6][:], AF.Ln, [PBb[6]], [RSb[s_]], bias=EPS, scale=1.0 / D)
                act(RS[s_][:], RS[s_][:], AF.Exp, [RSb[s_]], [RSb[s_]], scale=-0.5)

            def pd(j, c):
                s_ = j % 2
                stt(XN[s_][:, c, :], XS[s_][:, c, :], ccol(gbase + c), RS[s_][:], ALU.mult, ALU.mult,
                    [XSb[s_][c], RSb[s_], cstb], [XNb[s_][c]])
                if c == NCH - 1 and extra is not None:
                    extra(j)
            return [pa, pb, pc, pd]

        def pro_all(pro, j):
            pro[0](j)
            pro[1](j)
            pro[2](j)
            for c in range(NCH):
                pro[3](j, c)

        NOPRO = [lambda j: None, lambda j: None, lambda j: None, lambda j, c: None]

        def hook(j, step, pro, nxt):
            if j + 1 < NT:
                pro[step](j + 1)
            elif nxt is not None:
                nxt[step](0)

        def hook_d(j, pro, nxt):
            if j + 1 < NT:
                return lambda m: pro[3](j + 1, m)
            if nxt is not None:
                return lambda m: nxt[3](0, m)
            return None

        def xs_srcs(slot):
            return [(XS[slot][:, c, :], [XSb[slot][c]]) for c in range(NCH)]

        def xn_outs(slot):
            return [(XN[slot][:, c, :], [XNb[slot][c]]) for c in range(NCH)]

        def resid_out(slot, w_ap_fn, nk, rhs_fn, rhs_bufs, wslot, pbs, inter=None):
            for m in range(NCH):
                pb = pbs[m % len(pbs)]
                for k in range(nk):
                    mm(PB[pb][:], w_ap_fn(k, m), rhs_fn(k), k == 0, k == nk - 1,
                       [Wb[wslot]] + rhs_bufs(k), [PBb[pb]])
                tt(XS[slot][:, m, :], XS[slot][:, m, :], PB[pb][:], ALU.add,
                   [XSb[slot][m], PBb[pb]], [XSb[slot][m]])
                if inter is not None:
                    inter(m)

        def stage_ffn(part, l, src, dst, wslot, dst_is_out=False):
            f0, f1 = FFN_PARTS[part]
            nf = f1 - f0
            wv = {}

            def loadw():
                gu = Wd["ffn_w_gu"][l]
                wv["g"] = wload(wslot, 0, NCH, nf * 128, kmaj(gu, f0 * 128, f1 * 128))
                wv["u"] = wload(wslot, NCH * nf * 128, NCH, nf * 128,
                                kmaj(gu, DFF + f0 * 128, DFF + f1 * 128))
                wv["d"] = wload(wslot, 2 * NCH * nf * 128, nf, D,
                                Wd["ffn_w_down"][l][f0 * 128:f1 * 128, :].rearrange("(k p) n -> p k n", p=128))

            if part == 0:
                pro = make_norm_pro(src, C_FFNN + l * 8, extra=lambda j: dma(
                    STQ, dtile(XNd, j), XN[j % 2][:], XNb[j % 2], [dbuf("XNd", j)], sem_xn_st[j % 2]))
            else:
                def _pa(j):
                    load_x(ACC, "ACC", j, j % 2)
                    dma("sp", XN[j % 2][:], dtile(XNd, j), [dbuf("XNd", j)], XNb[j % 2], sem_xn[j % 2])
                pro = [_pa, lambda j: None, lambda j: None, lambda j, c: None]

            def run(nxt=None, hoisted=False):
                if not hoisted:
                    pro_all(pro, 0)
                for j in range(NT):
                    s = j % 2
                    for f in range(nf):
                        pg, pu = f % 2, 2 + f % 2
                        for k in range(NCH):
                            mm(PB[pg][:], wv["g"][:, k, f * 128:(f + 1) * 128], XN[s][:, k, :],
                               k == 0, k == NCH - 1, [Wb[wslot], XNb[s][k]], [PBb[pg]])
                        for k in range(NCH):
                            mm(PB[pu][:], wv["u"][:, k, f * 128:(f + 1) * 128], XN[s][:, k, :],
                               k == 0, k == NCH - 1, [Wb[wslot], XNb[s][k]], [PBb[pu]])
                        sg = f % 2
                        act(BT[sg][:], PB[pg][:], AF.Silu, [PBb[pg]], [BTb[sg]])
                        tt(HB[:, f, :], BT[sg][:], PB[pu][:], ALU.mult, [BTb[sg], PBb[pu]], [HBb[f]])
                        if f in (0, 4, 6):
                            hook(j, {0: 0, 4: 1, 6: 2}[f], pro, nxt)
                    resid_out(s, lambda k, m: wv["d"][:, k, m * 128:(m + 1) * 128], nf,
                              lambda k: HB[:, k, :], lambda k: [HBb[k]], wslot, (4, 5), hook_d(j, pro, nxt))
                    if part == 2:
                        store_x(dst[0], dst[1], j, s)
                    else:
                        store_x(ACC, "ACC", j, s)
            return {"loadw": loadw, "run": run, "pro": pro}

        def stage_conv(l, jc, src, dst, wslot):
            wv = {}

            def loadw():
                wv["in"] = wload(wslot, 0, NCH, 3 * D, kmaj(Wd["conv_w_in"][jc]))
                wv["out"] = wload(wslot, NCH * 3 * D, NCH, D, kmaj(Wd["conv_w_out"][jc]))

            pro = make_norm_pro(src, C_MIXN + l * 8)

            def run(nxt=None, hoisted=False):
                P.op("dve", lambda e: e.memset(halo_c[:], 0.0), writes=halo_cb)
                if not hoisted:
                    pro_all(pro, 0)
                for j in range(NT):
                    s = j % 2
                    for c in range(NCH):
                        q = c % 2
                        pbB, pbC, pbH = 3 * q, 3 * q + 1, 3 * q + 2
                        for which, pb in ((0, pbB), (1, pbC), (2, pbH)):
                            col = which * D + c * 128
                            for k in range(NCH):
                                mm(PB[pb][:], wv["in"][:, k, col:col + 128], XN[s][:, k, :],
                                   k == 0, k == NCH - 1, [Wb[wslot], XNb[s][k]], [PBb[pb]])
                        cs, ut, t1 = 3 * q, 3 * q + 1, 3 * q + 2
                        act(FT[cs][:, 0:T], PB[pbC][:], AF.Copy, [PBb[pbC]], [FTb[cs]])
                        cpy(FT[ut][:, 0:2], halo_c[:, c, :], [halo_cb[c]], [FTb[ut]])
                        tt(FT[ut][:, 2:2 + T], FT[cs][:, 0:T], PB[pbH][:], ALU.mult,
                           [FTb[cs], PBb[pbH]], [FTb[ut]])
                        cpy(halo_c[:, c, :], FT[ut][:, T:T + 2], [FTb[ut]], [halo_cb[c]])
                        wc = C_CONVW + jc * 24 + c
                        tsc(FT[t1][:, 0:T], FT[ut][:, 2:2 + T], ccol(wc + 16), None, ALU.mult, None,
                            [FTb[ut], cstb], [FTb[t1]])
                        stt(FT[t1][:, 0:T], FT[ut][:, 1:1 + T], ccol(wc + 8), FT[t1][:, 0:T], ALU.mult, ALU.add,
                            [FTb[ut], FTb[t1], cstb], [FTb[t1]])
                        stt(FT[t1][:, 0:T], FT[ut][:, 0:T], ccol(wc), FT[t1][:, 0:T], ALU.mult, ALU.add,
                            [FTb[ut], FTb[t1], cstb], [FTb[t1]])
                        tt(HB[:, c, :], FT[t1][:, 0:T], PB[pbB][:], ALU.mult, [FTb[t1], PBb[pbB]], [HBb[c]])
                        if c in (0, 4, 6):
                            hook(j, {0: 0, 4: 1, 6: 2}[c], pro, nxt)
                    resid_out(s, lambda k, m: wv["out"][:, k, m * 128:(m + 1) * 128], NCH,
                              lambda k: HB[:, k, :], lambda k: [HBb[k]], wslot, (6, 0), hook_d(j, pro, nxt))
                    store_x(dst[0], dst[1], j, s)
            return {"loadw": loadw, "run": run, "pro": pro}

        def stage_lru(l, src, dst, wslot):
            wv = {}

            def loadw():
                wv["in"] = wload(wslot, 0, NCH, 2 * LRU_W, kmaj(Wd["lru_w_in"][0]))
                o = NCH * 2 * LRU_W
                wv["a"] = wload(wslot, o, NLC, 128, Wd["lru_gate_a_w"][0].rearrange("n d e -> d n e"))
                o += NLC * 128
                wv["x"] = wload(wslot, o, NLC, 128, Wd["lru_gate_x_w"][0].rearrange("n d e -> d n e"))
                o += NLC * 128
                wv["out"] = wload(wslot, o, NLC, D, kmaj(Wd["lru_w_out"][0]))

            pro = make_norm_pro(src, C_MIXN + l * 8)

            def run(nxt=None, hoisted=False):
                P.op("dve", lambda e: e.memset(halo_r[:], 0.0), writes=halo_rb)
                P.op("dve", lambda e: e.memset(hst[:], 0.0), writes=hstb)
                act(ls8[:, 0:NLC], cst[:, C_LLAM:C_LLAM + NLC], AF.Exp, [cstb], [ls8b], scale=-1.0)
                act(ls8[:, 0:NLC], ls8[:, 0:NLC], AF.Ln, [ls8b], [ls8b], bias=1.0)
                tsc(ls8[:, NLC:2 * NLC], ls8[:, 0:NLC], -16.0, None, ALU.mult, None, [ls8b], [ls8b])
                tsc(ls8[:, 0:NLC], ls8[:, 0:NLC], -8.0, None, ALU.mult, None, [ls8b], [ls8b])
                if not hoisted:
                    pro_all(pro, 0)
                for j in range(NT):
                    s = j % 2
                    for c in range(NLC):
                        q = c % 2
                        pG, pR = q, 2 + q
                        pA, pX = 4, 5
                        f0, f1_, f2, f3 = 4 * q, 4 * q + 1, 4 * q + 2, 4 * q + 3
                        gtb, recb = 2 * q, 2 * q + 1
                        for which, pb in ((0, pG), (1, pR)):
                            col = which * LRU_W + c * 128
                            for k in range(NCH):
                                mm(PB[pb][:], wv["in"][:, k, col:col + 128], XN[s][:, k, :],
                                   k == 0, k == NCH - 1, [Wb[wslot], XNb[s][k]], [PBb[pb]])
                        A0, A1, A2, A3 = FT[f0][:, 0:T], FT[f1_][:, 0:T], FT[f2], FT[f3][:, 0:T]
                        act(A0, PB[pG][:], AF.Copy, [PBb[pG]], [FTb[f0]])
                        tt(A1, A0, A0, ALU.mult, [FTb[f0]], [FTb[f1_]])
                        tsc(A1, A1, GELU_C, 1.0, ALU.mult, ALU.add, [FTb[f1_]], [FTb[f1_]])
                        tt(A1, A1, A0, ALU.mult, [FTb[f1_], FTb[f0]], [FTb[f1_]])
                        act(A1, A1, AF.Sigmoid, [FTb[f1_]], [FTb[f1_]], scale=GELU_S)
                        tt(BT[gtb][:], A0, A1, ALU.mult, [FTb[f0], FTb[f1_]], [BTb[gtb]])
                        cpy(A2[:, 0:3], halo_r[:, c, :], [halo_rb[c]], [FTb[f2]])
                        act(A2[:, 3:3 + T], PB[pR][:], AF.Copy, [PBb[pR]], [FTb[f2]])
                        cpy(halo_r[:, c, :], A2[:, T:T + 3], [FTb[f2]], [halo_rb[c]])
                        tsc(A3, A2[:, 3:3 + T], ccol(C_LCW + 30 + c), ccol(C_LCB + c), ALU.mult, ALU.add,
                            [FTb[f2], cstb], [FTb[f3]])
                        for tap in (2, 1, 0):
                            stt(A3, A2[:, tap:tap + T], ccol(C_LCW + tap * 10 + c), A3, ALU.mult, ALU.add,
                                [FTb[f2], FTb[f3], cstb], [FTb[f3]])
                        act(BT[recb][:], A3, AF.Copy, [FTb[f3]], [BTb[recb]])
                        mm(PB[pA][:], wv["a"][:, c, :], BT[recb][:], True, True, [Wb[wslot], BTb[recb]], [PBb[pA]])
                        mm(PB[pX][:], wv["x"][:, c, :], BT[recb][:], True, True, [Wb[wslot], BTb[recb]], [PBb[pX]])
                        act(A0, PB[pA][:], AF.Sigmoid, [PBb[pA], cstb], [FTb[f0]], bias=ccol(C_LGAB + c))
                        act(A1, PB[pX][:], AF.Sigmoid, [PBb[pX], cstb], [FTb[f1_]], bias=ccol(C_LGXB + c))
                        A2t = A2[:, 0:T]
                        act(A2t, A0, AF.Exp, [FTb[f0], ls8b], [FTb[f2]], scale=ls8[:, c:c + 1])
                        act(A0, A0, AF.Exp, [FTb[f0], ls8b], [FTb[f0]], scale=ls8[:, NLC + c:NLC + c + 1])
                        act(A0, A0, AF.Sqrt, [FTb[f0]], [FTb[f0]], bias=1.0, scale=-1.0)
                        tt(A1, A1, A3, ALU.mult, [FTb[f1_], FTb[f3]], [FTb[f1_]])
                        tt(A1, A1, A0, ALU.mult, [FTb[f1_], FTb[f0]], [FTb[f1_]])
                        P.op("dve", (lambda e, A3=A3, A2t=A2t, A1=A1, c=c: e.tensor_tensor_scan(
                            out=A3, data0=A2t, data1=A1, initial=hst[:, c:c + 1], op0=ALU.mult, op1=ALU.add)),
                            reads=[FTb[f2], FTb[f1_], hstb[c]], writes=[FTb[f3]])
                        cpy(hst[:, c:c + 1], FT[f3][:, T - 1:T], [FTb[f3]], [hstb[c]])
                        tt(HB[:, c, :], BT[gtb][:], A3, ALU.mult, [BTb[gtb], FTb[f3]], [HBb[c]])
                        if c in (0, 4, 6):
                            hook(j, {0: 0, 4: 1, 6: 2}[c], pro, nxt)
                    resid_out(s, lambda k, m: wv["out"][:, k, m * 128:(m + 1) * 128], NLC,
                              lambda k: HB[:, k, :], lambda k: [HBb[k]], wslot, (6, 0), hook_d(j, pro, nxt))
                    store_x(dst[0], dst[1], j, s)
            return {"loadw": loadw, "run": run, "pro": pro}

        def stage_mla1(l, src, wslot):
            wv = {}
            sem_stg = [P.dma_sem(f"stg{i}") for i in range(6)]

            def loadw():
                wv["down"] = wload(wslot, 0, NCH, 704, kmaj(Wd["mla_w_down"][0]))
                o = NCH * 704
                wv["uq"] = wload(wslot, o, 3, 1536, kmaj(Wd["mla_w_uq"][0]))
                o += 3 * 1536
                wv["ukv"] = wload(wslot, o, 2, 2048, kmaj(Wd["mla_w_ukv"][0]))

            rot = [0]
            srot = [0]

            def nextpb(n=5):
                r = rot[0] % n
                rot[0] += 1
                return r

            def nextss():
                r = 5 + srot[0] % 2
                srot[0] += 1
                return r

            pro = make_norm_pro(src, C_MIXN + l * 8)

            def prologue(j):
                pro_all(pro, j)

            def rope_pair(p1, p2, b1, b2, gcol, cosb, sinb, o1, o2, ob, rs, tmp):
                rs_ap, rs_b = _rs(rs)
                t1, t2, t3, t4 = [t.ap[0:32, :] for t in tmp]
                tb1, tb2, tb3, tb4 = [t.b for t in tmp]
                cs_, sn_ = FT[cosb][0:32, 0:T], FT[sinb][0:32, 0:T]
                stt(t1, p1, ccol(gcol, 32), rs_ap[0:32, :], ALU.mult, ALU.mult, b1 + [rs_b, cstb], [tb1])
                stt(t2, p2, ccol(gcol + 1, 32), rs_ap[0:32, :], ALU.mult, ALU.mult, b2 + [rs_b, cstb], [tb2])
                tt(t3, t2, sn_, ALU.mult, [tb2, FTb[sinb]], [tb3])
                tt(t4, t1, cs_, ALU.mult, [tb1, FTb[cosb]], [tb4])
                tt(o1, t4, t3, ALU.subtract, [tb4, tb3], ob)
                tt(t3, t1, sn_, ALU.mult, [tb1, FTb[sinb]], [tb3])
                tt(t4, t2, cs_, ALU.mult, [tb2, FTb[cosb]], [tb4])
                tt(o2, t4, t3, ALU.add, [tb4, tb3], ob)

            def run(nxt=None, hoisted=False):
                if not hoisted:
                    prologue(0)
                for j in range(NT):
                    s = j % 2
                    o_ = 1 - s
                    xk = lambda k: XN[s][:, k, :]
                    xtmp = [TB(XS[o_][:, i, :], XSb[o_][i]) for i in range(NCH)]
                    stg = [TB(XN[o_][:, i, :], XNb[o_][i]) for i in range(6)]
                    ftmp = [TB(FT[4 + i][:, 0:T], FTb[4 + i]) for i in range(4)]
                    posi = FT[1][0:32, 0:T].bitcast(I32)
                    dma("sp", posi, pos[0:1, j * T:(j + 1) * T].broadcast_to([32, T]), [], [FTb[1]], sem_pos)
                    ang = FT[0][0:32, 0:T]
                    cpy(ang, posi, [FTb[1]], [FTb[0]])
                    tsc(ang, ang, ccol(C_INVF, 32), None, ALU.mult, None, [FTb[0], cstb], [FTb[0]])
                    MAGIC = 12582912.0
                    C1 = 6.28125
                    C2 = 2.0 * math.pi - C1
                    kk = FT[1][0:32, 0:T]
                    for dst, shift in ((3, 0.0), (2, 0.5 * math.pi)):
                        a2 = FT[dst][0:32, 0:T]
                        tsc(a2, ang, shift, None, ALU.add, None, [FTb[0]], [FTb[dst]])
                        tsc(kk, a2, 1.0 / (2.0 * math.pi), None, ALU.mult, None, [FTb[dst]], [FTb[1]])
                        tsc(kk, kk, MAGIC, None, ALU.add, None, [FTb[1]], [FTb[1]])
                        tsc(kk, kk, -MAGIC, None, ALU.add, None, [FTb[1]], [FTb[1]])
                        stt(a2, kk, -C1, a2, ALU.mult, ALU.add, [FTb[1], FTb[dst]], [FTb[dst]])
                        stt(a2, kk, -C2, a2, ALU.mult, ALU.add, [FTb[1], FTb[dst]], [FTb[dst]])
                        tsc(a2, a2, math.pi, -math.pi, ALU.min, ALU.max, [FTb[dst]], [FTb[dst]])
                        act(a2, a2, AF.Sin, [FTb[dst]], [FTb[dst]])
                    def proj_down(col, width, pb):
                        for k in range(NCH):
                            mm(PB[pb][0:width, :], wv["down"][:, k, col:col + width], xk(k),
                               k == 0, k == NCH - 1, [Wb[wslot], XNb[s][k]], [PBb[pb]])
                    pbs = [nextpb() for _ in range(3)]
                    for i, pb in enumerate(pbs):
                        proj_down(i * 128, 128, pb)
                    srcs = [(PB[pb][:], [PBb[pb]]) for pb in pbs]
                    norm_stats(srcs, 128, 384, xtmp[0], nextss())
                    cqn = [(BT[i][:], [BTb[i]]) for i in range(3)]
                    norm_apply(srcs, 128, xtmp[0], [C_QNORM + i for i in range(3)], cqn)
                    pbs = [nextpb() for _ in range(2)]
                    for i, pb in enumerate(pbs):
                        proj_down(384 + i * 128, 128, pb)
                    srcs = [(PB[pb][:], [PBb[pb]]) for pb in pbs]
                    norm_stats(srcs, 128, 256, xtmp[1], nextss())
                    ckvn = [(BT[3 + i][:], [BTb[3 + i]]) for i in range(2)]
                    norm_apply(srcs, 128, xtmp[1], [C_KVNORM + i for i in range(2)], ckvn)
                    pb1, pb2 = nextpb(), nextpb()
                    proj_down(640, 32, pb1)
                    proj_down(672, 32, pb2)
                    norm_stats([(PB[pb1][0:32, :], [PBb[pb1]]), (PB[pb2][0:32, :], [PBb[pb2]])], 32, 64,
                               xtmp[2], nextss())
                    rope_pair(PB[pb1][0:32, :], PB[pb2][0:32, :], [PBb[pb1]], [PBb[pb2]], C_KRN, 2, 3,
                              BT[5][0:32, :], BT[5][32:64, :], [BTb[5]], xtmp[2], ftmp)
                    dma(STQ, KRd[:, j * T:(j + 1) * T], BT[5][0:64, :], [BTb[5]], [dbuf("KRd", j)], sem_bt[5])
                    for h in range(NH):
                        hp = h % 2
                        rs_q, rs_r = xtmp[hp], xtmp[2 + hp]
                        rtmp = ftmp if hp == 0 else xtmp[4:8]
                        qn_s, qr_s, kn_s = stg[hp], stg[2 + hp], stg[4 + hp]
                        pb = nextpb()
                        for k in range(3):
                            mm(PB[pb][:], wv["uq"][:, k, h * 192:h * 192 + 128], cqn[k][0], k == 0, k == 2,
                               [Wb[wslot]] + cqn[k][1], [PBb[pb]])
                        srcs = [(PB[pb][:], [PBb[pb]])]
                        norm_stats(srcs, 128, 128, rs_q, nextss())
                        norm_apply(srcs, 128, rs_q, [C_QNN], [(qn_s.ap, [qn_s.b])])
                        dma(STQ, QNd[h * 128:(h + 1) * 128, j * T:(j + 1) * T], qn_s.ap, [qn_s.b],
                            [dbuf("QNd", j)], sem_stg[hp])
                        pb1, pb2 = nextpb(), nextpb()
                        for pbx, c0 in ((pb1, h * 192 + 128), (pb2, h * 192 + 160)):
                            for k in range(3):
                                mm(PB[pbx][0:32, :], wv["uq"][:, k, c0:c0 + 32], cqn[k][0], k == 0, k == 2,
                                   [Wb[wslot]] + cqn[k][1], [PBb[pbx]])
                        norm_stats([(PB[pb1][0:32, :], [PBb[pb1]]), (PB[pb2][0:32, :], [PBb[pb2]])], 32, 64,
                                   rs_r, nextss())
                        rope_pair(PB[pb1][0:32, :], PB[pb2][0:32, :], [PBb[pb1]], [PBb[pb2]], C_QRN, 2, 3,
                                  qr_s.ap[0:32, :], qr_s.ap[32:64, :], [qr_s.b], rs_r, rtmp)
                        dma(STQ, QRd[h * 64:(h + 1) * 64, j * T:(j + 1) * T], qr_s.ap[0:64, :], [qr_s.b],
                            [dbuf("QRd", j)], sem_stg[2 + hp])
                        pb = nextpb()
                        for k in range(2):
                            mm(PB[pb][:], wv["ukv"][:, k, h * 256:h * 256 + 128], ckvn[k][0], k == 0, k == 1,
                               [Wb[wslot]] + ckvn[k][1], [PBb[pb]])
                        srcs = [(PB[pb][:], [PBb[pb]])]
                        norm_stats(srcs, 128, 128, rs_q, nextss())
                        norm_apply(srcs, 128, rs_q, [C_KNN], [(kn_s.ap, [kn_s.b])])
                        dma(STQ, KNd[h * 128:(h + 1) * 128, j * T:(j + 1) * T], kn_s.ap, [kn_s.b],
                            [dbuf("KNd", j)], sem_stg[4 + hp])
                    vsrc = wv["ukv"].rearrange("p k (h e) -> p k h e", h=NH)
                    for cc in range(4):
                        for hh in range(2):
                            pb = nextpb()
                            for k in range(2):
                                mm(PB[pb][:].rearrange("p (h d) -> p h d", h=4),
                                   ckvn[k][0][:, cc * 128:(cc + 1) * 128],
                                   vsrc[:, k, hh * 4:(hh + 1) * 4, 128:256], k == 0, k == 1,
                                   [Wb[wslot]] + ckvn[k][1], [PBb[pb]])
                            act(HB[:, hh * 4:(hh + 1) * 4, cc * 128:(cc + 1) * 128],
                                PB[pb][:].rearrange("p (h d) -> p h d", h=4), AF.Copy,
                                [PBb[pb]], [HBb[hh * 4 + i] for i in range(4)])
                    dma(STQ, Vd[:, :, j * T:(j + 1) * T].rearrange("h p t -> p h t"), HB[:, 0:8, :],
                        [HBb[i] for i in range(8)], [dbuf("Vd", j)], sem_hb)
                    if j + 1 < NT:
                        prologue(j + 1)
            return {"loadw": loadw, "run": run, "pro": pro}

        def stage_mla2(wslot):
            W = Wt[wslot]
            o = [0]

            def carve(n):
                a = o[0]
                o[0] += n
                assert o[0] <= SLOT_ELEMS[wslot]
                return a
            KN_o = [carve(S), carve(S)]
            KR_o = carve(S)
            VR_o = [carve(S), carve(S)]
            KNb = [Buf("KN0"), Buf("KN1")]
            KRb = Buf("KRs")
            VRb = [Buf("VR0"), Buf("VR1")]
            sem_kn = [P.dma_sem("kn0"), P.dma_sem("kn1")]
            sem_kr = P.dma_sem("krs")
            sem_vr = [P.dma_sem("vr0"), P.dma_sem("vr1")]
            wr = [Wb[wslot]]
            allkn = [dbuf("KNd", j) for j in range(NT)]
            allkr = [dbuf("KRd", j) for j in range(NT)]
            allv = [dbuf("Vd", j) for j in range(NT)]
            kr = W[0:64, KR_o:KR_o + S]

            def kn_ap(h):
                return W[:, KN_o[h % 2]:KN_o[h % 2] + S]

            def v_ap(h):
                return W[:, VR_o[h % 2]:VR_o[h % 2] + S].rearrange("p (c d) -> p c d", d=128)

            def loadw():
                pass

            def load_head_kv(h):
                dma("sp", kn_ap(h), KNd[h * 128:(h + 1) * 128, :], allkn + wr, [KNb[h % 2]], sem_kn[h % 2])
                dma("sp", W[:, VR_o[h % 2]:VR_o[h % 2] + S], Vd[h], allv + wr, [VRb[h % 2]], sem_vr[h % 2])

            units = [(h, j) for h in range(NH) for j in range(NT)]
            G = [(u, kc) for u, (h, j) in enumerate(units) for kc in range(4 * j + 4)]

            def load_q(u):
                h, j = units[u]
                qs = u % 2
                dma("sp", BT[2 * qs][:], QNd[h * 128:(h + 1) * 128, j * T:(j + 1) * T],
                    [dbuf("QNd", j)], [BTb[2 * qs]], sem_bt[2 * qs])
                dma("sp", BT[2 * qs + 1][0:64, :], QRd[h * 64:(h + 1) * 64, j * T:(j + 1) * T],
                    [dbuf("QRd", j)], [BTb[2 * qs + 1]], sem_bt[2 * qs + 1])

            def emit_S(g):
                u, kc = G[g]
                h, j = units[u]
                if kc == 0:
                    if u + 1 < len(units):
                        load_q(u + 1)
                    if j == 1 and h + 1 < NH:
                        load_head_kv(h + 1)
                qs = u % 2
                qn_t, qr_t = 2 * qs, 2 * qs + 1
                q0 = max(kc - 4 * j, 0) * 128
                pb = g % 3
                mm(PB[pb][:, q0:T], kn_ap(h)[:, kc * 128:(kc + 1) * 128], BT[qn_t][:, q0:T], True, False,
                   [KNb[h % 2], BTb[qn_t]] + wr, [PBb[pb]])
                mm(PB[pb][:, q0:T], kr[:, kc * 128:(kc + 1) * 128], BT[qr_t][0:64, q0:T], False, True,
                   [KRb, BTb[qr_t]] + wr, [PBb[pb]])

            def emit_exp(g):
                u, kc = G[g]
                h, j = units[u]
                dgn = kc - 4 * j
                q0 = max(dgn, 0) * 128
                pb = g % 3
                pt = 4 + g % 3
                act(BT[pt][:, q0:T], PB[pb][:, q0:T], AF.Exp, [PBb[pb]], [BTb[pt]], scale=SM_SCALE)
                if dgn >= 0:
                    tt(BT[pt][:, q0:q0 + 128], BT[pt][:, q0:q0 + 128], cmat[:, 128:256], ALU.mult,
                       [BTb[pt], cmatb], [BTb[pt]])

            def emit_PV(g):
                u, kc = G[g]
                h, j = units[u]
                q0 = max(kc - 4 * j, 0) * 128
                pt = 4 + g % 3
                ob = 3 + (u % 2)
                db = 5 + (u % 2)
                last = kc == 4 * j + 3
                mm(PB[ob][:, q0:T], v_ap(h)[:, kc, :], BT[pt][:, q0:T], kc == 0, last,
                   [BTb[pt], VRb[h % 2]] + wr, [PBb[ob]], skip=True)
                mm(PB[db][:, q0:T], ones[:, :], BT[pt][:, q0:T], kc == 0, last,
                   [BTb[pt], onesb], [PBb[db]], skip=True)

            def emit_tail(u):
                h, j = units[u]
                ob = 3 + (u % 2)
                db = 5 + (u % 2)
                rd = u % 2
                P.op("dve", (lambda e, db=db, rd=rd: e.reciprocal(out=FT[rd][:, 0:T], in_=PB[db][:])),
                     reads=[PBb[db]], writes=[FTb[rd]])
                ot = 8 + u % 2
                tt(BT[ot][:], PB[ob][:], FT[rd][:, 0:T], ALU.mult, [PBb[ob], FTb[rd]], [BTb[ot]])
                dma(STQ, OTd[h * 128:(h + 1) * 128, j * T:(j + 1) * T], BT[ot][:], [BTb[ot]],
                    [dbuf("OTd", j)], sem_bt[ot])

            def run(nxt=None, hoisted=False):
                P.op("sp", lambda e: e.nop(), reads=[], writes=[Wb[wslot]])
                dma("sp", kr, KRd, allkr + wr, [KRb], sem_kr)
                load_head_kv(0)
                load_q(0)
                n = len(G)
                LOOK = 2
                for g in range(min(LOOK, n)):
                    emit_S(g)
                for g in range(n):
                    u, kc = G[g]
                    h, j = units[u]
                    emit_exp(g)
                    if g + LOOK < n:
                        emit_S(g + LOOK)
                    emit_PV(g)
                    if kc == 4 * j + 3:
                        emit_tail(u)
            return {"loadw": loadw, "run": run, "pro": None}

        def stage_mla3(src, dst, wslot):
            wv = {}

            def loadw():
                wv["o"] = wload(wslot, 0, NCH, D, kmaj(Wd["mla_w_o"][0]))

            def prologue(j):
                s = j % 2
                load_x(src[0], src[1], j, s)
                dma("sp", XN[s][:], dtile(OTd, j), [dbuf("OTd", j)], XNb[s], sem_xn[s])

            def run(nxt=None, hoisted=False):
                prologue(0)
                for j in range(NT):
                    s = j % 2
                    if j + 1 < NT:
                        prologue(j + 1)
                    elif nxt is not None:
                        nxt[0](0)
                        nxt[1](0)
                        nxt[2](0)
                    resid_out(s, lambda k, m: wv["o"][:, k, m * 128:(m + 1) * 128], NCH,
                              lambda k: XN[s][:, k, :], lambda k: [XNb[s][k]], wslot, (4, 5),
                              hook_d(j, NOPRO, nxt))
                    store_x(dst[0], dst[1], j, s)
            return {"loadw": loadw, "run": run, "pro": None}

        Rr = (R, "R")
        stages = []
        seq = [
            ("conv", 0, 0, (xT, "xT"), Rr),
            ("ffn", 0), ("lru", 1), ("ffn", 1),
            ("mla", 2), ("ffn", 2),
            ("conv", 3, 1, Rr, Rr),
            ("ffn", 3),
        ]
        plan = []
        for item in seq:
            if item[0] == "conv":
                plan.append(("conv", item[1], item[2], item[3], item[4]))
            elif item[0] == "lru":
                plan.append(("lru", item[1]))
            elif item[0] == "mla":
                plan += [("mla1", item[1]), ("mla2",), ("mla3",)]
            else:
                plan += [("ffn", p, item[1]) for p in range(3)]
        if stage_limit is not None:
            plan = plan[:stage_limit]
        built = []
        for si, pl in enumerate(plan):
            slot = si % 2
            last = (si == len(plan) - 1)
            final_dst = (out, "out") if last else Rr
            if pl[0] == "conv":
                built.append(stage_conv(pl[1], pl[2], pl[3], final_dst, slot))
            elif pl[0] == "lru":
                built.append(stage_lru(pl[1], Rr, final_dst, slot))
            elif pl[0] == "mla1":
                built.append(stage_mla1(pl[1], Rr, slot))
            elif pl[0] == "mla2":
                built.append(stage_mla2(slot))
            elif pl[0] == "mla3":
                built.append(stage_mla3(Rr, final_dst, slot))
            else:
                built.append(stage_ffn(pl[1], pl[2], Rr, final_dst, slot))
        built[0]["loadw"]()
        hoisted = False
        for si, stg_ in enumerate(built):
            nxt = None
            if si + 1 < len(built):
                built[si + 1]["loadw"]()
                nxt = built[si + 1]["pro"]
                if plan[si][0] in ("mla1", "mla2"):
                    nxt = None
            stg_["run"](nxt, hoisted)
            hoisted = nxt is not None
        P.op("sp", lambda e: e.nop(), reads=[dbuf("out", j) for j in range(NT)])
        cnt = P.emit(nc, st)
        nc._mk_stats = (len(P.ops), cnt)
    return nc


def host_consts(inp, b):
    c = np.zeros((128, NCOL), np.float32)

    def put(col, vec):
        v = np.asarray(vec, np.float32).reshape(-1, 128)
        for i in range(v.shape[0]):
            c[:, col + i] = v[i]
    for l in range(4):
        put(C_MIXN + l * 8, inp["mix_norm"][l])
        put(C_FFNN + l * 8, inp["ffn_norm"][l])
    for jc in range(2):
        for tap in range(3):
            put(C_CONVW + jc * 24 + tap * 8, inp["conv_w"][jc, tap])
    for tap in range(4):
        put(C_LCW + tap * 10, inp["lru_conv_w"][0, tap])
    put(C_LCB, inp["lru_conv_b"][0])
    put(C_LGAB, inp["lru_gate_a_b"][0].reshape(-1))
    put(C_LGXB, inp["lru_gate_x_b"][0].reshape(-1))
    put(C_LLAM, inp["lru_lambda"][0])
    put(C_QNORM, inp["mla_q_norm"][0])
    put(C_KVNORM, inp["mla_kv_norm"][0])
    put(C_QNN, inp["mla_qn_norm"][0])
    put(C_KNN, inp["mla_kn_norm"][0])
    c[0:32, C_QRN] = inp["mla_qr_norm"][0][0:32]
    c[0:32, C_QRN + 1] = inp["mla_qr_norm"][0][32:64]
    c[0:32, C_KRN] = inp["mla_kr_norm"][0][0:32]
    c[0:32, C_KRN + 1] = inp["mla_kr_norm"][0][32:64]
    c[0:32, C_INVF] = (10000.0 ** (-np.arange(0, 64, 2, dtype=np.float32) / np.float32(64))).astype(np.float32)
    return c


def host_cmat():
    m = np.zeros((128, 256), np.float32)
    m[:, 0:128] = np.eye(128, dtype=np.float32)
    k = np.arange(128)[:, None]
    q = np.arange(128)[None, :]
    m[:, 128:256] = (k <= q).astype(np.float32)
    return m


_NC_CACHE = {}


def kernel(**inputs):
    inp = {k: np.asarray(v) for k, v in inputs.items()}
    n = 8
    if "nc" not in _NC_CACHE:
        _NC_CACHE["nc"] = build()
    nc = _NC_CACHE["nc"]
    cm = host_cmat()
    shared = {name: np.ascontiguousarray(inp[name], dtype=np.float32) for name in WEIGHT_NAMES}
    in_maps = []
    for b in range(n):
        m = dict(shared)
        m["xT"] = np.ascontiguousarray(inp["x"][b].T)
        m["pos"] = np.ascontiguousarray(inp["positions"][b].reshape(1, S).astype(np.int32))
        m["consts"] = host_consts(inp, b)
        m["cmat"] = cm
        in_maps.append(m)
    res = run_bass_kernel_spmd(nc, in_maps, core_ids=list(range(n)))
    outp = np.stack([np.ascontiguousarray(r["out"].T) for r in res.results], axis=0)
    return outp.astype(np.float32)
```

```python
import contextlib
import math
import numpy as np
import concourse.bass as bass
import concourse.mybir as mybir
from concourse.bass_utils import run_bass_kernel_spmd

F32 = mybir.dt.float32
BF16 = mybir.dt.bfloat16
I32 = mybir.dt.int32
AF = mybir.ActivationFunctionType
ALU = mybir.AluOpType

EPOCH = 12000
ENGS = ("pe", "act", "dve", "pool", "sp")


class Buf:
    __slots__ = ("name", "last_w", "readers")

    def __init__(self, name):
        self.name = name
        self.last_w = None
        self.readers = []


class DmaSem:
    __slots__ = ("name", "count", "handle", "last_op")

    def __init__(self, name):
        self.name = name
        self.count = 0
        self.handle = None
        self.last_op = None


class Op:
    __slots__ = ("eng", "fn", "reads", "writes", "dsem", "dval", "idx",
                 "deps", "signal", "sig", "waits")

    def __init__(self, eng, fn, reads, writes, dsem):
        self.eng = eng
        self.fn = fn
        self.reads = reads
        self.writes = writes
        self.dsem = dsem
        self.dval = None
        self.deps = ()
        self.signal = False
        self.sig = None
        self.waits = ()


class Prog:
    def __init__(self):
        self.ops = []
        self.dsems = []

    def dma_sem(self, name):
        s = DmaSem(name)
        self.dsems.append(s)
        return s

    def op(self, eng, fn, reads=(), writes=(), dsem=None):
        o = Op(eng, fn, tuple(reads), tuple(writes), dsem)
        o.idx = len(self.ops)
        self.ops.append(o)
        return o

    def analyze(self):
        ops = self.ops
        for o in ops:
            deps = set()
            for b in o.reads:
                if b.last_w is not None:
                    deps.add(b.last_w)
            for b in o.writes:
                if b.last_w is not None:
                    deps.add(b.last_w)
                deps.update(b.readers)
            if o.dsem is not None:
                if o.dsem.last_op is not None:
                    deps.add(o.dsem.last_op)
                o.dsem.last_op = o.idx
                o.dsem.count += 16
                o.dval = o.dsem.count
            deps.discard(o.idx)
            for b in o.reads:
                b.readers.append(o.idx)
            for b in o.writes:
                b.last_w = o.idx
                b.readers = []
            o.deps = deps
        known = {e: {} for e in ENGS}
        for o in ops:
            need = {}
            for d in o.deps:
                od = ops[d]
                if od.dsem is not None:
                    key = ("d", id(od.dsem))
                    val = od.dval
                else:
                    if od.eng == "pe" and o.eng == "pe":
                        continue
                    key = ("e", od.eng)
                    val = d
                if key not in need or need[key][0] < val:
                    need[key] = (val, d)
            o.waits = []
            k = known[o.eng]
            for key, (val, d) in need.items():
                if key in k and k[key] >= val:
                    continue
                k[key] = val
                o.waits.append(d)
                if ops[d].dsem is None:
                    ops[d].signal = True
        cnt = {e: 0 for e in ENGS}
        for o in ops:
            if o.dsem is None and o.signal:
                n = cnt[o.eng]
                cnt[o.eng] = n + 1
                o.sig = (n // EPOCH, n % EPOCH + 1)
        self.n_epochs = {e: (cnt[e] + EPOCH - 1) // EPOCH for e in ENGS}
        return cnt

    def emit(self, nc, stack):
        cnt = self.analyze()
        esem = {}
        for e in ENGS:
            esem[e] = [stack.enter_context(nc.semaphore(f"s_{e}{k}"))
                       for k in range(max(1, self.n_epochs[e]))]
        for s in self.dsems:
            if s.count:
                s.handle = stack.enter_context(nc.semaphore(f"d_{s.name}"))
        ops = self.ops
        by_eng = {e: [o for o in ops if o.eng == e] for e in ENGS}

        def run(e, engobj):
            for o in by_eng[e]:
                for d in o.waits:
                    od = ops[d]
                    if od.dsem is not None:
                        engobj.wait_ge(od.dsem.handle, od.dval)
                    else:
                        ep, v = od.sig
                        engobj.wait_ge(esem[od.eng][ep], v)
                ins = o.fn(engobj)
                if o.dsem is not None:
                    ins.then_inc(o.dsem.handle, 16)
                elif o.signal:
                    ins.then_inc(esem[e][o.sig[0]], 1)

        block = stack.enter_context(nc.Block())

        @block.tensor
        def _(eng):
            run("pe", eng)

        @block.scalar
        def _(eng):
            run("act", eng)

        @block.vector
        def _(eng):
            run("dve", eng)

        @block.gpsimd
        def _(eng):
            run("pool", eng)

        @block.sync
        def _(eng):
            run("sp", eng)
        return cnt


D = 1024
S = 4096
T = 512
NT = S // T
NCH = D // 128
DFF = 2816
NFF = DFF // 128
FFN_PARTS = ((0, 8), (8, 15), (15, 22))
LRU_W = 1280
NLC = 10
NH = 8
EPS = 1e-6
SM_SCALE = 1.0 / math.sqrt(192.0)
GELU_C = 0.044715
GELU_S = 2.0 * math.sqrt(2.0 / math.pi)

SLOT_ELEMS = (33792, 24704)

C_MIXN = 0
C_FFNN = 32
C_CONVW = 64
C_LCW = 112
C_LCB = 152
C_LGAB = 162
C_LGXB = 172
C_LLAM = 182
C_QNORM = 192
C_KVNORM = 195
C_QNN = 197
C_KNN = 198
C_QRN = 199
C_KRN = 201
C_INVF = 203
NCOL = 208

WEIGHT_NAMES = ("conv_w_in", "conv_w_out", "lru_w_in", "lru_gate_a_w", "lru_gate_x_w",
                "lru_w_out", "mla_w_down", "mla_w_uq", "mla_w_ukv", "mla_w_o",
                "ffn_w_gu", "ffn_w_down")
WEIGHT_SHAPES = {
    "conv_w_in": [2, 1024, 3072], "conv_w_out": [2, 1024, 1024],
    "lru_w_in": [1, 1024, 2560], "lru_gate_a_w": [1, 10, 128, 128],
    "lru_gate_x_w": [1, 10, 128, 128], "lru_w_out": [1, 1280, 1024],
    "mla_w_down": [1, 1024, 704], "mla_w_uq": [1, 384, 1536],
    "mla_w_ukv": [1, 256, 2048], "mla_w_o": [1, 1024, 1024],
    "ffn_w_gu": [4, 1024, 5632], "ffn_w_down": [4, 2816, 1024],
}


class TB:
    __slots__ = ("ap", "b")

    def __init__(self, ap, b):
        self.ap = ap
        self.b = b


def build(stage_limit=None, dbg_out=None):
    nc = bass.Bass("TRN2", target_bir_lowering=False)
    P = Prog()
    st = contextlib.ExitStack()

    def din(name, shape, dt=F32):
        return nc.dram_tensor(name, shape, dt, kind="ExternalInput").ap()

    def dscr(name, shape, dt):
        return nc.dram_tensor(name, shape, dt, kind="Internal").ap()

    xT = din("xT", [D, S])
    pos = din("pos", [1, S], I32)
    cst_d = din("consts", [128, NCOL])
    cmat_d = din("cmat", [128, 256])
    Wd = {n: din(n, WEIGHT_SHAPES[n]) for n in WEIGHT_NAMES}
    out = nc.dram_tensor("out", [D, S], F32, kind="ExternalOutput").ap()

    R = dscr("R", [D, S], F32)
    ACC = dscr("ACC", [D, S], F32)
    XNd = dscr("XNd", [D, S], BF16)
    QNd = dscr("QNd", [NH * 128, S], BF16)
    QRd = dscr("QRd", [NH * 64, S], BF16)
    KNd = dscr("KNd", [NH * 128, S], BF16)
    KRd = dscr("KRd", [64, S], BF16)
    Vd = dscr("Vd", [NH, 128, S], BF16)
    OTd = dscr("OTd", [D, S], BF16)
    dram_bufs = {}

    def dbuf(name, j):
        k = (name, j)
        if k not in dram_bufs:
            dram_bufs[k] = Buf(f"{name}{j}")
        return dram_bufs[k]

    with st:
        def sb(name, shape, dt):
            return st.enter_context(nc.sbuf_tensor(name, shape, dt))

        def ps(name, shape, dt):
            return st.enter_context(nc.psum_tensor(name, shape, dt))

        Wt = [sb("W0", [128, SLOT_ELEMS[0]], BF16), sb("W1", [128, SLOT_ELEMS[1]], BF16)]
        Wb = [Buf("W0"), Buf("W1")]
        XS = [sb(f"XS{i}", [128, NCH, T], F32) for i in range(2)]
        XSb = [[Buf(f"XS{i}_{c}") for c in range(NCH)] for i in range(2)]
        XN = [sb(f"XN{i}", [128, NCH, T], BF16) for i in range(2)]
        XNb = [[Buf(f"XN{i}_{c}") for c in range(NCH)] for i in range(2)]
        HB = sb("HB", [128, NLC, T], BF16)
        HBb = [Buf(f"HB{c}") for c in range(NLC)]
        NF = 8
        FT = [sb(f"FT{i}", [128, 516], F32) for i in range(NF)]
        FTb = [Buf(f"FT{i}") for i in range(NF)]
        NB = 9
        BT = [sb(f"BT{i}", [128, T], BF16) for i in range(NB)]
        BTb = [Buf(f"BT{i}") for i in range(NB)]
        cst = sb("cst", [128, NCOL], F32)
        cstb = Buf("cst")
        cmat = sb("cmatb", [128, 256], BF16)
        cmatb = Buf("cmat")
        ones = sb("ones", [128, 128], BF16)
        onesb = Buf("ones")
        halo_c = sb("halo_c", [128, NCH, 2], F32)
        halo_cb = [Buf(f"hc{c}") for c in range(NCH)]
        halo_r = sb("halo_r", [128, NLC, 3], F32)
        halo_rb = [Buf(f"hr{c}") for c in range(NLC)]
        hst = sb("hst", [128, NLC], F32)
        hstb = [Buf(f"hs{c}") for c in range(NLC)]
        ls8 = sb("ls8", [128, 4 * NLC], F32)
        ls8b = Buf("ls8")
        posi = sb("posi", [32, T // 2], I32)
        posib = Buf("posi")
        RS = [sb(f"RS{i}", [128, T], F32) for i in range(2)]
        RSb = [Buf(f"RS{i}") for i in range(2)]
        PB = [ps(f"PB{i}", [128, T], F32) for i in range(7)]
        PBb = [Buf(f"PB{i}") for i in range(7)]
        PT = ps("PTb", [128, 1024], BF16)
        PTb = Buf("PTb")

        sem_xs = [P.dma_sem(f"xs{i}") for i in range(2)]
        sem_xs_st = [P.dma_sem(f"xsst{i}") for i in range(2)]
        sem_xn = [P.dma_sem(f"xn{i}") for i in range(2)]
        sem_xn_st = [P.dma_sem(f"xnst{i}") for i in range(2)]
        sem_w = [P.dma_sem("w0"), P.dma_sem("w1")]
        sem_misc = P.dma_sem("misc")
        sem_bt = [P.dma_sem(f"bt{i}") for i in range(NB)]
        sem_hb = P.dma_sem("hbst")
        sem_pos = P.dma_sem("pos")

        def dtile(dr, j):
            return dr.rearrange("(c p) t -> p c t", p=128)[:, :, j * T:(j + 1) * T]

        def mm(o_ap, lhsT, rhs, start, stop, reads, writes, skip=False):
            P.op("pe", lambda e: e.matmul(o_ap, lhsT, rhs, start=start, stop=stop, skip_group_check=skip),
                 reads=reads, writes=writes)

        def act(o_ap, i_ap, func, reads, writes, bias=None, scale=None):
            kw = {}
            if bias is not None:
                kw["bias"] = bias
            if scale is not None:
                kw["scale"] = scale
            P.op("act", lambda e: e.activation(out=o_ap, in_=i_ap, func=func, **kw),
                 reads=reads, writes=writes)

        def tt(o_ap, a_ap, b_ap, op, reads, writes):
            P.op("dve", lambda e: e.tensor_tensor(out=o_ap, in0=a_ap, in1=b_ap, op=op),
                 reads=reads, writes=writes)

        def tsc(o_ap, a_ap, s1, s2, op0, op1, reads, writes):
            if s2 is None:
                P.op("dve", lambda e: e.tensor_scalar(out=o_ap, in0=a_ap, scalar1=s1, scalar2=None,
                                                      op0=op0), reads=reads, writes=writes)
            else:
                P.op("dve", lambda e: e.tensor_scalar(out=o_ap, in0=a_ap, scalar1=s1, scalar2=s2,
                                                      op0=op0, op1=op1), reads=reads, writes=writes)

        def stt(o_ap, a_ap, s_ap, b_ap, op0, op1, reads, writes):
            P.op("dve", lambda e: e.scalar_tensor_tensor(out=o_ap, in0=a_ap, scalar=s_ap, in1=b_ap,
                                                         op0=op0, op1=op1), reads=reads, writes=writes)

        def cpy(o_ap, i_ap, reads, writes):
            P.op("dve", lambda e: e.tensor_copy(out=o_ap, in_=i_ap), reads=reads, writes=writes)

        def dma(eng, o_ap, i_ap, reads, writes, sem):
            P.op(eng, lambda e: e.dma_start(out=o_ap, in_=i_ap), reads=reads, writes=writes, dsem=sem)

        def ccol(col, npart=128):
            return cst[0:npart, col:col + 1]

        dma("sp", cst[:], cst_d, [], [cstb], sem_misc)
        dma("pool", cmat[:], cmat_d, [], [cmatb], sem_w[0])
        P.op("dve", lambda e: e.memset(ones[:], 1.0), writes=[onesb])

        def wview(slot, off, k, n):
            return Wt[slot][:, off:off + k * n].rearrange("p (k n) -> p k n", k=k)

        def wload(slot, off, k, n, src):
            assert off + k * n <= SLOT_ELEMS[slot]
            v = wview(slot, off, k, n)
            for kk in range(k):
                dma("pool", v[:, kk, :], src[:, kk, :], [], [Wb[slot]], sem_w[slot])
            return v

        def kmaj(w2d, c0=None, c1=None):
            v = w2d.rearrange("(k p) n -> p k n", p=128)
            if c0 is not None:
                v = v[:, :, c0:c1]
            return v

        sq_ring = [0]

        def _rs(rs):
            if isinstance(rs, int):
                return RS[rs][:], RSb[rs]
            return rs.ap, rs.b

        def norm_stats(srcs, npart, dn, rs, ssb_i):
            rs_ap, rs_b = _rs(rs)
            n = len(srcs)
            for c, (ap, bufs) in enumerate(srcs):
                k = 8 - (sq_ring[0] % 2)
                sq_ring[0] += 1
                act(BT[k][0:npart, :], ap, AF.Square, bufs, [BTb[k]])
                mm(PB[ssb_i][:], ones[0:npart, :], BT[k][0:npart, :], c == 0, c == n - 1,
                   [BTb[k], onesb], [PBb[ssb_i]])
            act(rs_ap, PB[ssb_i][:], AF.Ln, [PBb[ssb_i]], [rs_b], bias=EPS, scale=1.0 / dn)
            act(rs_ap, rs_ap, AF.Exp, [rs_b], [rs_b], scale=-0.5)

        def norm_apply(srcs, npart, rs, gcols, outs):
            rs_ap, rs_b = _rs(rs)
            for (ap, bufs), gc, (oap, obufs) in zip(srcs, gcols, outs):
                stt(oap, ap, ccol(gc, npart), rs_ap[0:npart, :], ALU.mult, ALU.mult,
                    bufs + [rs_b, cstb], obufs)

        def load_x(src_d, src_name, j, slot):
            dma("sp", XS[slot][:], dtile(src_d, j), [dbuf(src_name, j)], XSb[slot], sem_xs[slot])

        STQ = "sp"

        def store_x(dst_d, dst_name, j, slot):
            dma(STQ, dtile(dst_d, j), XS[slot][:], XSb[slot], [dbuf(dst_name, j)], sem_xs_st[slot])

        def make_norm_pro(src, gbase, extra=None):
            def pa(j):
                load_x(src[0], src[1], j, j % 2)

            def pb(j):
                s_ = j % 2
                for c in range(NCH):
                    act(XN[s_][:, c, :], XS[s_][:, c, :], AF.Square, [XSb[s_][c]], [XNb[s_][c]])

            def pc(j):
                s_ = j % 2
                for c in range(NCH):
                    mm(PB[6][:], ones[:, :], XN[s_][:, c, :], c == 0, c == NCH - 1,
                       [XNb[s_][c], onesb], [PBb[6]])
                act(RS[s_][:], PB[6][:], AF.Ln, [PBb[6]], [RSb[s_]], bias=EPS, scale=1.0 / D)
                act(RS[s_][:], RS[s_][:], AF.Exp, [RSb[s_]], [RSb[s_]], scale=-0.5)

            def pd(j, c):
                s_ = j % 2
                stt(XN[s_][:, c, :], XS[s_][:, c, :], ccol(gbase + c), RS[s_][:], ALU.mult, ALU.mult,
                    [XSb[s_][c], RSb[s_], cstb], [XNb[s_][c]])
                if c == NCH - 1 and extra is not None:
                    extra(j)
            return [pa, pb, pc, pd]

        def pro_all(pro, j):
            pro[0](j)
            pro[1](j)
            pro[2](j)
            for c in range(NCH):
                pro[3](j, c)

        NOPRO = [lambda j: None, lambda j: None, lambda j: None, lambda j, c: None]

        def hook(j, step, pro, nxt):
            if j + 1 < NT:
                pro[step](j + 1)
            elif nxt is not None:
                nxt[step](0)

        def hook_d(j, pro, nxt):
            if j + 1 < NT:
                return lambda m: pro[3](j + 1, m)
            if nxt is not None:
                return lambda m: nxt[3](0, m)
            return None

        def xs_srcs(slot):
            return [(XS[slot][:, c, :], [XSb[slot][c]]) for c in range(NCH)]

        def xn_outs(slot):
            return [(XN[slot][:, c, :], [XNb[slot][c]]) for c in range(NCH)]

        def resid_out(slot, w_ap_fn, nk, rhs_fn, rhs_bufs, wslot, pbs, inter=None):
            for m in range(NCH):
                pb = pbs[m % len(pbs)]
                for k in range(nk):
                    mm(PB[pb][:], w_ap_fn(k, m), rhs_fn(k), k == 0, k == nk - 1,
                       [Wb[wslot]] + rhs_bufs(k), [PBb[pb]])
                tt(XS[slot][:, m, :], XS[slot][:, m, :], PB[pb][:], ALU.add,
                   [XSb[slot][m], PBb[pb]], [XSb[slot][m]])
                if inter is not None:
                    inter(m)

        def stage_ffn(part, l, src, dst, wslot, dst_is_out=False):
            f0, f1 = FFN_PARTS[part]
            nf = f1 - f0
            wv = {}

            def loadw():
                gu = Wd["ffn_w_gu"][l]
                wv["g"] = wload(wslot, 0, NCH, nf * 128, kmaj(gu, f0 * 128, f1 * 128))
                wv["u"] = wload(wslot, NCH * nf * 128, NCH, nf * 128,
                                kmaj(gu, DFF + f0 * 128, DFF + f1 * 128))
                wv["d"] = wload(wslot, 2 * NCH * nf * 128, nf, D,
                                Wd["ffn_w_down"][l][f0 * 128:f1 * 128, :].rearrange("(k p) n -> p k n", p=128))

            if part == 0:
                pro = make_norm_pro(src, C_FFNN + l * 8, extra=lambda j: dma(
                    STQ, dtile(XNd, j), XN[j % 2][:], XNb[j % 2], [dbuf("XNd", j)], sem_xn_st[j % 2]))
            else:
                def _pa(j):
                    load_x(ACC, "ACC", j, j % 2)
                    dma("sp", XN[j % 2][:], dtile(XNd, j), [dbuf("XNd", j)], XNb[j % 2], sem_xn[j % 2])
                pro = [_pa, lambda j: None, lambda j: None, lambda j, c: None]

            def run(nxt=None, hoisted=False):
                if not hoisted:
                    pro_all(pro, 0)
                for j in range(NT):
                    s = j % 2
                    for f in range(nf):
                        pg, pu = f % 2, 2 + f % 2
                        for k in range(NCH):
                            mm(PB[pg][:], wv["g"][:, k, f * 128:(f + 1) * 128], XN[s][:, k, :],
                               k == 0, k == NCH - 1, [Wb[wslot], XNb[s][k]], [PBb[pg]])
                        for k in range(NCH):
                            mm(PB[pu][:], wv["u"][:, k, f * 128:(f + 1) * 128], XN[s][:, k, :],
                               k == 0, k == NCH - 1, [Wb[wslot], XNb[s][k]], [PBb[pu]])
                        sg = f % 2
                        act(BT[sg][:], PB[pg][:], AF.Silu, [PBb[pg]], [BTb[sg]])
                        tt(HB[:, f, :], BT[sg][:], PB[pu][:], ALU.mult, [BTb[sg], PBb[pu]], [HBb[f]])
                        if f in (0, 4, 6):
                            hook(j, {0: 0, 4: 1, 6: 2}[f], pro, nxt)
                    resid_out(s, lambda k, m: wv["d"][:, k, m * 128:(m + 1) * 128], nf,
                              lambda k: HB[:, k, :], lambda k: [HBb[k]], wslot, (4, 5), hook_d(j, pro, nxt))
                    if part == 2:
                        store_x(dst[0], dst[1], j, s)
                    else:
                        store_x(ACC, "ACC", j, s)
            return {"loadw": loadw, "run": run, "pro": pro}

        def stage_conv(l, jc, src, dst, wslot):
            wv = {}

            def loadw():
                wv["in"] = wload(wslot, 0, NCH, 3 * D, kmaj(Wd["conv_w_in"][jc]))
                wv["out"] = wload(wslot, NCH * 3 * D, NCH, D, kmaj(Wd["conv_w_out"][jc]))

            pro = make_norm_pro(src, C_MIXN + l * 8)

            def run(nxt=None, hoisted=False):
                P.op("dve", lambda e: e.memset(halo_c[:], 0.0), writes=halo_cb)
                if not hoisted:
                    pro_all(pro, 0)
                for j in range(NT):
                    s = j % 2
                    for c in range(NCH):
                        q = c % 2
                        pbB, pbC, pbH = 3 * q, 3 * q + 1, 3 * q + 2
                        for which, pb in ((0, pbB), (1, pbC), (2, pbH)):
                            col = which * D + c * 128
                            for k in range(NCH):
                                mm(PB[pb][:], wv["in"][:, k, col:col + 128], XN[s][:, k, :],
                                   k == 0, k == NCH - 1, [Wb[wslot], XNb[s][k]], [PBb[pb]])
                        cs, ut, t1 = 3 * q, 3 * q + 1, 3 * q + 2
                        act(FT[cs][:, 0:T], PB[pbC][:], AF.Copy, [PBb[pbC]], [FTb[cs]])
                        cpy(FT[ut][:, 0:2], halo_c[:, c, :], [halo_cb[c]], [FTb[ut]])
                        tt(FT[ut][:, 2:2 + T], FT[cs][:, 0:T], PB[pbH][:], ALU.mult,
                           [FTb[cs], PBb[pbH]], [FTb[ut]])
                        cpy(halo_c[:, c, :], FT[ut][:, T:T + 2], [FTb[ut]], [halo_cb[c]])
                        wc = C_CONVW + jc * 24 + c
                        tsc(FT[t1][:, 0:T], FT[ut][:, 2:2 + T], ccol(wc + 16), None, ALU.mult, None,
                            [FTb[ut], cstb], [FTb[t1]])
                        stt(FT[t1][:, 0:T], FT[ut][:, 1:1 + T], ccol(wc + 8), FT[t1][:, 0:T], ALU.mult, ALU.add,
                            [FTb[ut], FTb[t1], cstb], [FTb[t1]])
                        stt(FT[t1][:, 0:T], FT[ut][:, 0:T], ccol(wc), FT[t1][:, 0:T], ALU.mult, ALU.add,
                            [FTb[ut], FTb[t1], cstb], [FTb[t1]])
                        tt(HB[:, c, :], FT[t1][:, 0:T], PB[pbB][:], ALU.mult, [FTb[t1], PBb[pbB]], [HBb[c]])
                        if c in (0, 4, 6):
                            hook(j, {0: 0, 4: 1, 6: 2}[c], pro, nxt)
                    resid_out(s, lambda k, m: wv["out"][:, k, m * 128:(m + 1) * 128], NCH,
                              lambda k: HB[:, k, :], lambda k: [HBb[k]], wslot, (6, 0), hook_d(j, pro, nxt))
                    store_x(dst[0], dst[1], j, s)
            return {"loadw": loadw, "run": run, "pro": pro}

        def stage_lru(l, src, dst, wslot):
            wv = {}

            def loadw():
                wv["in"] = wload(wslot, 0, NCH, 2 * LRU_W, kmaj(Wd["lru_w_in"][0]))
                o = NCH * 2 * LRU_W
                wv["a"] = wload(wslot, o, NLC, 128, Wd["lru_gate_a_w"][0].rearrange("n d e -> d n e"))
                o += NLC * 128
                wv["x"] = wload(wslot, o, NLC, 128, Wd["lru_gate_x_w"][0].rearrange("n d e -> d n e"))
                o += NLC * 128
                wv["out"] = wload(wslot, o, NLC, D, kmaj(Wd["lru_w_out"][0]))

            pro = make_norm_pro(src, C_MIXN + l * 8)

            def run(nxt=None, hoisted=False):
                P.op("dve", lambda e: e.memset(halo_r[:], 0.0), writes=halo_rb)
                P.op("dve", lambda e: e.memset(hst[:], 0.0), writes=hstb)
                act(ls8[:, NLC:2 * NLC], cst[:, C_LLAM:C_LLAM + NLC], AF.Exp, [cstb], [ls8b], scale=-1.0)
                act(ls8[:, NLC:2 * NLC], ls8[:, NLC:2 * NLC], AF.Ln, [ls8b], [ls8b], bias=1.0)
                tsc(ls8[:, 0:NLC], ls8[:, NLC:2 * NLC], -4.0, None, ALU.mult, None, [ls8b], [ls8b])
                tsc(ls8[:, NLC:2 * NLC], ls8[:, NLC:2 * NLC], -8.0, None, ALU.mult, None, [ls8b], [ls8b])
                tsc(ls8[:, 2 * NLC:3 * NLC], cst[:, C_LGAB:C_LGAB + NLC], 0.5, None, ALU.mult, None,
                    [cstb, ls8b], [ls8b])
                tsc(ls8[:, 3 * NLC:4 * NLC], cst[:, C_LGXB:C_LGXB + NLC], 0.5, None, ALU.mult, None,
                    [cstb, ls8b], [ls8b])
                if not hoisted:
                    pro_all(pro, 0)
                for j in range(NT):
                    s = j % 2
                    for c in range(NLC):
                        q = c % 2
                        pG, pR = q, 2 + q
                        pA, pX = 4, 5
                        f0, f1_, f2, f3 = 4 * q, 4 * q + 1, 4 * q + 2, 4 * q + 3
                        gtb, recb = 2 * q, 2 * q + 1
                        for which, pb in ((0, pG), (1, pR)):
                            col = which * LRU_W + c * 128
                            for k in range(NCH):
                                mm(PB[pb][:], wv["in"][:, k, col:col + 128], XN[s][:, k, :],
                                   k == 0, k == NCH - 1, [Wb[wslot], XNb[s][k]], [PBb[pb]])
                        A0, A1, A2, A3 = FT[f0][:, 0:T], FT[f1_][:, 0:T], FT[f2], FT[f3][:, 0:T]
                        act(BT[gtb][:], PB[pG][:], AF.Gelu_apprx_tanh, [PBb[pG]], [BTb[gtb]])
                        cpy(A2[:, 0:3], halo_r[:, c, :], [halo_rb[c]], [FTb[f2]])
                        act(A2[:, 3:3 + T], PB[pR][:], AF.Copy, [PBb[pR]], [FTb[f2]])
                        cpy(halo_r[:, c, :], A2[:, T:T + 3], [FTb[f2]], [halo_rb[c]])
                        tsc(A3, A2[:, 3:3 + T], ccol(C_LCW + 30 + c), ccol(C_LCB + c), ALU.mult, ALU.add,
                            [FTb[f2], cstb], [FTb[f3]])
                        for tap in (2, 1, 0):
                            stt(A3, A2[:, tap:tap + T], ccol(C_LCW + tap * 10 + c), A3, ALU.mult, ALU.add,
                                [FTb[f2], FTb[f3], cstb], [FTb[f3]])
                        act(BT[recb][:], A3, AF.Copy, [FTb[f3]], [BTb[recb]])
                        mm(PB[pA][:], wv["a"][:, c, :], BT[recb][:], True, True, [Wb[wslot], BTb[recb]], [PBb[pA]])
                        mm(PB[pX][:], wv["x"][:, c, :], BT[recb][:], True, True, [Wb[wslot], BTb[recb]], [PBb[pX]])
                        act(A0, PB[pA][:], AF.Tanh, [PBb[pA], ls8b], [FTb[f0]],
                            bias=ls8[:, 2 * NLC + c:2 * NLC + c + 1], scale=0.5)
                        act(A1, PB[pX][:], AF.Tanh, [PBb[pX], ls8b], [FTb[f1_]],
                            bias=ls8[:, 3 * NLC + c:3 * NLC + c + 1], scale=0.5)
                        A2t = A2[:, 0:T]
                        act(A2t, A0, AF.Exp, [FTb[f0], ls8b], [FTb[f2]],
                            bias=ls8[:, c:c + 1], scale=ls8[:, c:c + 1])
                        act(A0, A0, AF.Exp, [FTb[f0], ls8b], [FTb[f0]],
                            bias=ls8[:, NLC + c:NLC + c + 1], scale=ls8[:, NLC + c:NLC + c + 1])
                        act(A0, A0, AF.Sqrt, [FTb[f0]], [FTb[f0]], bias=1.0, scale=-1.0)
                        stt(A1, A1, 1.0, A3, ALU.add, ALU.mult, [FTb[f1_], FTb[f3]], [FTb[f1_]])
                        stt(A1, A1, 0.5, A0, ALU.mult, ALU.mult, [FTb[f1_], FTb[f0]], [FTb[f1_]])
                        P.op("dve", (lambda e, A3=A3, A2t=A2t, A1=A1, c=c: e.tensor_tensor_scan(
                            out=A3, data0=A2t, data1=A1, initial=hst[:, c:c + 1], op0=ALU.mult, op1=ALU.add)),
                            reads=[FTb[f2], FTb[f1_], hstb[c]], writes=[FTb[f3]])
                        cpy(hst[:, c:c + 1], FT[f3][:, T - 1:T], [FTb[f3]], [hstb[c]])
                        tt(HB[:, c, :], BT[gtb][:], A3, ALU.mult, [BTb[gtb], FTb[f3]], [HBb[c]])
                        if c in (0, 4, 6):
                            hook(j, {0: 0, 4: 1, 6: 2}[c], pro, nxt)
                    resid_out(s, lambda k, m: wv["out"][:, k, m * 128:(m + 1) * 128], NLC,
                              lambda k: HB[:, k, :], lambda k: [HBb[k]], wslot, (6, 0), hook_d(j, pro, nxt))
                    store_x(dst[0], dst[1], j, s)
            return {"loadw": loadw, "run": run, "pro": pro}

        def stage_mla1(l, src, wslot):
            wv = {}
            sem_stg = [P.dma_sem(f"stg{i}") for i in range(6)]

            def loadw():
                wv["down"] = wload(wslot, 0, NCH, 704, kmaj(Wd["mla_w_down"][0]))
                o = NCH * 704
                wv["uq"] = wload(wslot, o, 3, 1536, kmaj(Wd["mla_w_uq"][0]))
                o += 3 * 1536
                wv["ukv"] = wload(wslot, o, 2, 2048, kmaj(Wd["mla_w_ukv"][0]))

            rot = [0]
            srot = [0]

            def nextpb(n=5):
                r = rot[0] % n
                rot[0] += 1
                return r

            def nextss():
                r = 5 + srot[0] % 2
                srot[0] += 1
                return r

            pro = make_norm_pro(src, C_MIXN + l * 8)

            def prologue(j):
                pro_all(pro, j)

            def rope_pair(p1, p2, b1, b2, gcol, cosb, sinb, o1, o2, ob, rs, tmp):
                rs_ap, rs_b = _rs(rs)
                t1, t2, t3, t4 = [t.ap[0:32, :] for t in tmp]
                tb1, tb2, tb3, tb4 = [t.b for t in tmp]
                cs_, sn_ = FT[cosb][0:32, 0:T], FT[sinb][0:32, 0:T]
                stt(t1, p1, ccol(gcol, 32), rs_ap[0:32, :], ALU.mult, ALU.mult, b1 + [rs_b, cstb], [tb1])
                stt(t2, p2, ccol(gcol + 1, 32), rs_ap[0:32, :], ALU.mult, ALU.mult, b2 + [rs_b, cstb], [tb2])
                tt(t3, t2, sn_, ALU.mult, [tb2, FTb[sinb]], [tb3])
                tt(t4, t1, cs_, ALU.mult, [tb1, FTb[cosb]], [tb4])
                tt(o1, t4, t3, ALU.subtract, [tb4, tb3], ob)
                tt(t3, t1, sn_, ALU.mult, [tb1, FTb[sinb]], [tb3])
                tt(t4, t2, cs_, ALU.mult, [tb2, FTb[cosb]], [tb4])
                tt(o2, t4, t3, ALU.add, [tb4, tb3], ob)

            def run(nxt=None, hoisted=False):
                if not hoisted:
                    prologue(0)
                for j in range(NT):
                    s = j % 2
                    o_ = 1 - s
                    xk = lambda k: XN[s][:, k, :]
                    xtmp = [TB(XS[o_][:, i, :], XSb[o_][i]) for i in range(NCH)]
                    stg = [TB(XN[o_][:, i, :], XNb[o_][i]) for i in range(6)]
                    ftmp = [TB(FT[4 + i][:, 0:T], FTb[4 + i]) for i in range(4)]
                    ang = FT[0][0:32, 0:T]
                    for hf in range(2):
                        t0_ = j * T + hf * (T // 2)
                        dma("sp", posi[:], pos[0:1, t0_:t0_ + T // 2].broadcast_to([32, T // 2]), [], [posib],
                            sem_pos)
                        cpy(FT[0][0:32, hf * (T // 2):(hf + 1) * (T // 2)], posi[:], [posib], [FTb[0]])
                    tsc(ang, ang, ccol(C_INVF, 32), None, ALU.mult, None, [FTb[0], cstb], [FTb[0]])
                    MAGIC = 12582912.0
                    C1 = 6.28125
                    C2 = 2.0 * math.pi - C1
                    kk = FT[1][0:32, 0:T]
                    for dst, shift in ((3, 0.0), (2, 0.5 * math.pi)):
                        a2 = FT[dst][0:32, 0:T]
                        tsc(a2, ang, shift, None, ALU.add, None, [FTb[0]], [FTb[dst]])
                        tsc(kk, a2, 1.0 / (2.0 * math.pi), None, ALU.mult, None, [FTb[dst]], [FTb[1]])
                        tsc(kk, kk, MAGIC, None, ALU.add, None, [FTb[1]], [FTb[1]])
                        tsc(kk, kk, -MAGIC, None, ALU.add, None, [FTb[1]], [FTb[1]])
                        stt(a2, kk, -C1, a2, ALU.mult, ALU.add, [FTb[1], FTb[dst]], [FTb[dst]])
                        stt(a2, kk, -C2, a2, ALU.mult, ALU.add, [FTb[1], FTb[dst]], [FTb[dst]])
                        tsc(a2, a2, math.pi, -math.pi, ALU.min, ALU.max, [FTb[dst]], [FTb[dst]])
                        act(a2, a2, AF.Sin, [FTb[dst]], [FTb[dst]])
                    def proj_down(col, width, pb):
                        for k in range(NCH):
                            mm(PB[pb][0:width, :], wv["down"][:, k, col:col + width], xk(k),
                               k == 0, k == NCH - 1, [Wb[wslot], XNb[s][k]], [PBb[pb]])
                    pbs = [nextpb() for _ in range(3)]
                    for i, pb in enumerate(pbs):
                        proj_down(i * 128, 128, pb)
                    srcs = [(PB[pb][:], [PBb[pb]]) for pb in pbs]
                    norm_stats(srcs, 128, 384, xtmp[0], nextss())
                    cqn = [(BT[i][:], [BTb[i]]) for i in range(3)]
                    norm_apply(srcs, 128, xtmp[0], [C_QNORM + i for i in range(3)], cqn)
                    pbs = [nextpb() for _ in range(2)]
                    for i, pb in enumerate(pbs):
                        proj_down(384 + i * 128, 128, pb)
                    srcs = [(PB[pb][:], [PBb[pb]]) for pb in pbs]
                    norm_stats(srcs, 128, 256, xtmp[1], nextss())
                    ckvn = [(BT[3 + i][:], [BTb[3 + i]]) for i in range(2)]
                    norm_apply(srcs, 128, xtmp[1], [C_KVNORM + i for i in range(2)], ckvn)
                    pb1, pb2 = nextpb(), nextpb()
                    proj_down(640, 32, pb1)
                    proj_down(672, 32, pb2)
                    norm_stats([(PB[pb1][0:32, :], [PBb[pb1]]), (PB[pb2][0:32, :], [PBb[pb2]])], 32, 64,
                               xtmp[2], nextss())
                    rope_pair(PB[pb1][0:32, :], PB[pb2][0:32, :], [PBb[pb1]], [PBb[pb2]], C_KRN, 2, 3,
                              BT[5][0:32, :], BT[5][32:64, :], [BTb[5]], xtmp[2], ftmp)
                    dma(STQ, KRd[:, j * T:(j + 1) * T], BT[5][0:64, :], [BTb[5]], [dbuf("KRd", j)], sem_bt[5])
                    for h in range(NH):
                        hp = h % 2
                        rs_q, rs_r = xtmp[hp], xtmp[2 + hp]
                        rtmp = ftmp if hp == 0 else xtmp[4:8]
                        qn_s, qr_s, kn_s = stg[hp], stg[2 + hp], stg[4 + hp]
                        pb = nextpb()
                        for k in range(3):
                            mm(PB[pb][:], wv["uq"][:, k, h * 192:h * 192 + 128], cqn[k][0], k == 0, k == 2,
                               [Wb[wslot]] + cqn[k][1], [PBb[pb]])
                        srcs = [(PB[pb][:], [PBb[pb]])]
                        norm_stats(srcs, 128, 128, rs_q, nextss())
                        norm_apply(srcs, 128, rs_q, [C_QNN], [(qn_s.ap, [qn_s.b])])
                        dma(STQ, QNd[h * 128:(h + 1) * 128, j * T:(j + 1) * T], qn_s.ap, [qn_s.b],
                            [dbuf("QNd", j)], sem_stg[hp])
                        pb1, pb2 = nextpb(), nextpb()
                        for pbx, c0 in ((pb1, h * 192 + 128), (pb2, h * 192 + 160)):
                            for k in range(3):
                                mm(PB[pbx][0:32, :], wv["uq"][:, k, c0:c0 + 32], cqn[k][0], k == 0, k == 2,
                                   [Wb[wslot]] + cqn[k][1], [PBb[pbx]])
                        norm_stats([(PB[pb1][0:32, :], [PBb[pb1]]), (PB[pb2][0:32, :], [PBb[pb2]])], 32, 64,
                                   rs_r, nextss())
                        rope_pair(PB[pb1][0:32, :], PB[pb2][0:32, :], [PBb[pb1]], [PBb[pb2]], C_QRN, 2, 3,
                                  qr_s.ap[0:32, :], qr_s.ap[32:64, :], [qr_s.b], rs_r, rtmp)
                        dma(STQ, QRd[h * 64:(h + 1) * 64, j * T:(j + 1) * T], qr_s.ap[0:64, :], [qr_s.b],
                            [dbuf("QRd", j)], sem_stg[2 + hp])
                        pb = nextpb()
                        for k in range(2):
                            mm(PB[pb][:], wv["ukv"][:, k, h * 256:h * 256 + 128], ckvn[k][0], k == 0, k == 1,
                               [Wb[wslot]] + ckvn[k][1], [PBb[pb]])
                        srcs = [(PB[pb][:], [PBb[pb]])]
                        norm_stats(srcs, 128, 128, rs_q, nextss())
                        norm_apply(srcs, 128, rs_q, [C_KNN], [(kn_s.ap, [kn_s.b])])
                        dma(STQ, KNd[h * 128:(h + 1) * 128, j * T:(j + 1) * T], kn_s.ap, [kn_s.b],
                            [dbuf("KNd", j)], sem_stg[4 + hp])
                    vsrc = wv["ukv"].rearrange("p k (h e) -> p k h e", h=NH)
                    for cc in range(4):
                        for hh in range(2):
                            pb = nextpb()
                            for k in range(2):
                                mm(PB[pb][:].rearrange("p (h d) -> p h d", h=4),
                                   ckvn[k][0][:, cc * 128:(cc + 1) * 128],
                                   vsrc[:, k, hh * 4:(hh + 1) * 4, 128:256], k == 0, k == 1,
                                   [Wb[wslot]] + ckvn[k][1], [PBb[pb]])
                            act(HB[:, hh * 4:(hh + 1) * 4, cc * 128:(cc + 1) * 128],
                                PB[pb][:].rearrange("p (h d) -> p h d", h=4), AF.Copy,
                                [PBb[pb]], [HBb[hh * 4 + i] for i in range(4)])
                    dma(STQ, Vd[:, :, j * T:(j + 1) * T].rearrange("h p t -> p h t"), HB[:, 0:8, :],
                        [HBb[i] for i in range(8)], [dbuf("Vd", j)], sem_hb)
                    if j + 1 < NT:
                        prologue(j + 1)
            return {"loadw": loadw, "run": run, "pro": pro}

        def stage_mla2(wslot):
            W = Wt[wslot]
            o = [0]

            def carve(n):
                a = o[0]
                o[0] += n
                assert o[0] <= SLOT_ELEMS[wslot]
                return a
            KN_o = [carve(S), carve(S)]
            KR_o = carve(S)
            VR_o = [carve(S), carve(S)]
            KNb = [Buf("KN0"), Buf("KN1")]
            KRb = Buf("KRs")
            VRb = [Buf("VR0"), Buf("VR1")]
            sem_kn = [P.dma_sem("kn0"), P.dma_sem("kn1")]
            sem_kr = P.dma_sem("krs")
            sem_vr = [P.dma_sem("vr0"), P.dma_sem("vr1")]
            wr = [Wb[wslot]]
            allkn = [dbuf("KNd", j) for j in range(NT)]
            allkr = [dbuf("KRd", j) for j in range(NT)]
            allv = [dbuf("Vd", j) for j in range(NT)]
            kr = W[0:64, KR_o:KR_o + S]

            def kn_ap(h):
                return W[:, KN_o[h % 2]:KN_o[h % 2] + S]

            def v_ap(h):
                return W[:, VR_o[h % 2]:VR_o[h % 2] + S].rearrange("p (c d) -> p c d", d=128)

            def loadw():
                pass

            def load_head_kv(h):
                dma("sp", kn_ap(h), KNd[h * 128:(h + 1) * 128, :], allkn + wr, [KNb[h % 2]], sem_kn[h % 2])
                dma("sp", W[:, VR_o[h % 2]:VR_o[h % 2] + S], Vd[h], allv + wr, [VRb[h % 2]], sem_vr[h % 2])

            units = [(h, j) for h in range(NH) for j in range(NT)]
            G = [(u, kc) for u, (h, j) in enumerate(units) for kc in range(4 * j + 4)]

            def load_q(u):
                h, j = units[u]
                qs = u % 2
                dma("sp", BT[2 * qs][:], QNd[h * 128:(h + 1) * 128, j * T:(j + 1) * T],
                    [dbuf("QNd", j)], [BTb[2 * qs]], sem_bt[2 * qs])
                dma("sp", BT[2 * qs + 1][0:64, :], QRd[h * 64:(h + 1) * 64, j * T:(j + 1) * T],
                    [dbuf("QRd", j)], [BTb[2 * qs + 1]], sem_bt[2 * qs + 1])

            def emit_S(g):
                u, kc = G[g]
                h, j = units[u]
                if kc == 0:
                    if u + 1 < len(units):
                        load_q(u + 1)
                    if j == 1 and h + 1 < NH:
                        load_head_kv(h + 1)
                qs = u % 2
                qn_t, qr_t = 2 * qs, 2 * qs + 1
                q0 = max(kc - 4 * j, 0) * 128
                pb = g % 3
                mm(PB[pb][:, q0:T], kn_ap(h)[:, kc * 128:(kc + 1) * 128], BT[qn_t][:, q0:T], True, False,
                   [KNb[h % 2], BTb[qn_t]] + wr, [PBb[pb]])
                mm(PB[pb][:, q0:T], kr[:, kc * 128:(kc + 1) * 128], BT[qr_t][0:64, q0:T], False, True,
                   [KRb, BTb[qr_t]] + wr, [PBb[pb]])

            def emit_exp(g):
                u, kc = G[g]
                h, j = units[u]
                dgn = kc - 4 * j
                q0 = max(dgn, 0) * 128
                pb = g % 3
                pt = 4 + g % 3
                act(BT[pt][:, q0:T], PB[pb][:, q0:T], AF.Exp, [PBb[pb]], [BTb[pt]], scale=SM_SCALE)
                if dgn >= 0:
                    tt(BT[pt][:, q0:q0 + 128], BT[pt][:, q0:q0 + 128], cmat[:, 128:256], ALU.mult,
                       [BTb[pt], cmatb], [BTb[pt]])

            def emit_PV(g):
                u, kc = G[g]
                h, j = units[u]
                q0 = max(kc - 4 * j, 0) * 128
                pt = 4 + g % 3
                ob = 3 + (u % 2)
                db = 5 + (u % 2)
                last = kc == 4 * j + 3
                mm(PB[ob][:, q0:T], v_ap(h)[:, kc, :], BT[pt][:, q0:T], kc == 0, last,
                   [BTb[pt], VRb[h % 2]] + wr, [PBb[ob]], skip=True)
                mm(PB[db][:, q0:T], ones[:, :], BT[pt][:, q0:T], kc == 0, last,
                   [BTb[pt], onesb], [PBb[db]], skip=True)

            def emit_tail(u):
                h, j = units[u]
                ob = 3 + (u % 2)
                db = 5 + (u % 2)
                rd = u % 2
                P.op("dve", (lambda e, db=db, rd=rd: e.reciprocal(out=FT[rd][:, 0:T], in_=PB[db][:])),
                     reads=[PBb[db]], writes=[FTb[rd]])
                ot = 7 + u % 2
                tt(BT[ot][:], PB[ob][:], FT[rd][:, 0:T], ALU.mult, [PBb[ob], FTb[rd]], [BTb[ot]])
                dma(STQ, OTd[h * 128:(h + 1) * 128, j * T:(j + 1) * T], BT[ot][:], [BTb[ot]],
                    [dbuf("OTd", j)], sem_bt[ot])

            def run(nxt=None, hoisted=False):
                P.op("sp", lambda e: e.nop(), reads=[], writes=[Wb[wslot]])
                dma("sp", kr, KRd, allkr + wr, [KRb], sem_kr)
                load_head_kv(0)
                load_q(0)
                n = len(G)
                LOOK = 2
                for g in range(min(LOOK, n)):
                    emit_S(g)
                for g in range(n):
                    u, kc = G[g]
                    h, j = units[u]
                    emit_exp(g)
                    if g + LOOK < n:
                        emit_S(g + LOOK)
                    emit_PV(g)
                    if kc == 4 * j + 3:
                        emit_tail(u)
            return {"loadw": loadw, "run": run, "pro": None}

        def stage_mla3(src, dst, wslot):
            wv = {}

            def loadw():
                wv["o"] = wload(wslot, 0, NCH, D, kmaj(Wd["mla_w_o"][0]))

            def prologue(j):
                s = j % 2
                load_x(src[0], src[1], j, s)
                dma("sp", XN[s][:], dtile(OTd, j), [dbuf("OTd", j)], XNb[s], sem_xn[s])

            def run(nxt=None, hoisted=False):
                prologue(0)
                for j in range(NT):
                    s = j % 2
                    if j + 1 < NT:
                        prologue(j + 1)
                    elif nxt is not None:
                        nxt[0](0)
                        nxt[1](0)
                        nxt[2](0)
                    resid_out(s, lambda k, m: wv["o"][:, k, m * 128:(m + 1) * 128], NCH,
                              lambda k: XN[s][:, k, :], lambda k: [XNb[s][k]], wslot, (4, 5),
                              hook_d(j, NOPRO, nxt))
                    store_x(dst[0], dst[1], j, s)
            return {"loadw": loadw, "run": run, "pro": None}

        Rr = (R, "R")
        stages = []
        seq = [
            ("conv", 0, 0, (xT, "xT"), Rr),
            ("ffn", 0), ("lru", 1), ("ffn", 1),
            ("mla", 2), ("ffn", 2),
            ("conv", 3, 1, Rr, Rr),
            ("ffn", 3),
        ]
        plan = []
        for item in seq:
            if item[0] == "conv":
                plan.append(("conv", item[1], item[2], item[3], item[4]))
            elif item[0] == "lru":
                plan.append(("lru", item[1]))
            elif item[0] == "mla":
                plan += [("mla1", item[1]), ("mla2",), ("mla3",)]
            else:
                plan += [("ffn", p, item[1]) for p in range(3)]
        if stage_limit is not None:
            plan = plan[:stage_limit]
        built = []
        for si, pl in enumerate(plan):
            slot = si % 2
            last = (si == len(plan) - 1)
            final_dst = (out, "out") if last else Rr
            if pl[0] == "conv":
                built.append(stage_conv(pl[1], pl[2], pl[3], final_dst, slot))
            elif pl[0] == "lru":
                built.append(stage_lru(pl[1], Rr, final_dst, slot))
            elif pl[0] == "mla1":
                built.append(stage_mla1(pl[1], Rr, slot))
            elif pl[0] == "mla2":
                built.append(stage_mla2(slot))
            elif pl[0] == "mla3":
                built.append(stage_mla3(Rr, final_dst, slot))
            else:
                built.append(stage_ffn(pl[1], pl[2], Rr, final_dst, slot))
        built[0]["loadw"]()
        hoisted = False
        for si, stg_ in enumerate(built):
            nxt = None
            if si + 1 < len(built):
                built[si + 1]["loadw"]()
                nxt = built[si + 1]["pro"]
                if plan[si][0] in ("mla1", "mla2"):
                    nxt = None
            stg_["run"](nxt, hoisted)
            hoisted = nxt is not None
        P.op("sp", lambda e: e.nop(), reads=[dbuf("out", j) for j in range(NT)])
        cnt = P.emit(nc, st)
        nc._mk_stats = (len(P.ops), cnt)
    return nc


def host_consts(inp, b):
    c = np.zeros((128, NCOL), np.float32)

    def put(col, vec):
        v = np.asarray(vec, np.float32).reshape(-1, 128)
        for i in range(v.shape[0]):
            c[:, col + i] = v[i]
    for l in range(4):
        put(C_MIXN + l * 8, inp["mix_norm"][l])
        put(C_FFNN + l * 8, inp["ffn_norm"][l])
    for jc in range(2):
        for tap in range(3):
            put(C_CONVW + jc * 24 + tap * 8, inp["conv_w"][jc, tap])
    for tap in range(4):
        put(C_LCW + tap * 10, inp["lru_conv_w"][0, tap])
    put(C_LCB, inp["lru_conv_b"][0])
    put(C_LGAB, inp["lru_gate_a_b"][0].reshape(-1))
    put(C_LGXB, inp["lru_gate_x_b"][0].reshape(-1))
    put(C_LLAM, inp["lru_lambda"][0])
    put(C_QNORM, inp["mla_q_norm"][0])
    put(C_KVNORM, inp["mla_kv_norm"][0])
    put(C_QNN, inp["mla_qn_norm"][0])
    put(C_KNN, inp["mla_kn_norm"][0])
    c[0:32, C_QRN] = inp["mla_qr_norm"][0][0:32]
    c[0:32, C_QRN + 1] = inp["mla_qr_norm"][0][32:64]
    c[0:32, C_KRN] = inp["mla_kr_norm"][0][0:32]
    c[0:32, C_KRN + 1] = inp["mla_kr_norm"][0][32:64]
    c[0:32, C_INVF] = (10000.0 ** (-np.arange(0, 64, 2, dtype=np.float32) / np.float32(64))).astype(np.float32)
    return c


def host_cmat():
    m = np.zeros((128, 256), np.float32)
    m[:, 0:128] = np.eye(128, dtype=np.float32)
    k = np.arange(128)[:, None]
    q = np.arange(128)[None, :]
    m[:, 128:256] = (k <= q).astype(np.float32)
    return m


_NC_CACHE = {}


def kernel(**inputs):
    inp = {k: np.asarray(v) for k, v in inputs.items()}
    n = 8
    if "nc" not in _NC_CACHE:
        _NC_CACHE["nc"] = build()
    nc = _NC_CACHE["nc"]
    cm = host_cmat()
    shared = {name: np.ascontiguousarray(inp[name], dtype=np.float32) for name in WEIGHT_NAMES}
    in_maps = []
    for b in range(n):
        m = dict(shared)
        m["xT"] = np.ascontiguousarray(inp["x"][b].T)
        m["pos"] = np.ascontiguousarray(inp["positions"][b].reshape(1, S).astype(np.int32))
        m["consts"] = host_consts(inp, b)
        m["cmat"] = cm
        in_maps.append(m)
    res = run_bass_kernel_spmd(nc, in_maps, core_ids=list(range(n)))
    outp = np.stack([np.ascontiguousarray(r["out"].T) for r in res.results], axis=0)
    return outp.astype(np.float32)
```

```python
import contextlib
import math
import numpy as np
import concourse.bass as bass
import concourse.mybir as mybir
from concourse.bass_utils import run_bass_kernel_spmd

F32 = mybir.dt.float32
BF16 = mybir.dt.bfloat16
I32 = mybir.dt.int32
AF = mybir.ActivationFunctionType
ALU = mybir.AluOpType

EPOCH = 12000
ENGS = ("pe", "act", "dve", "pool", "sp")


class Buf:
    __slots__ = ("name", "last_w", "readers")

    def __init__(self, name):
        self.name = name
        self.last_w = None
        self.readers = []


class DmaSem:
    __slots__ = ("name", "count", "handle", "last_op")

    def __init__(self, name):
        self.name = name
        self.count = 0
        self.handle = None
        self.last_op = None


class Op:
    __slots__ = ("eng", "fn", "reads", "writes", "dsem", "dval", "idx",
                 "deps", "signal", "sig", "waits")

    def __init__(self, eng, fn, reads, writes, dsem):
        self.eng = eng
        self.fn = fn
        self.reads = reads
        self.writes = writes
        self.dsem = dsem
        self.dval = None
        self.deps = ()
        self.signal = False
        self.sig = None
        self.waits = ()


class Prog:
    def __init__(self):
        self.ops = []
        self.dsems = []

    def dma_sem(self, name):
        s = DmaSem(name)
        self.dsems.append(s)
        return s

    def op(self, eng, fn, reads=(), writes=(), dsem=None):
        o = Op(eng, fn, tuple(reads), tuple(writes), dsem)
        o.idx = len(self.ops)
        self.ops.append(o)
        return o

    def analyze(self):
        ops = self.ops
        for o in ops:
            deps = set()
            for b in o.reads:
                if b.last_w is not None:
                    deps.add(b.last_w)
            for b in o.writes:
                if b.last_w is not None:
                    deps.add(b.last_w)
                deps.update(b.readers)
            if o.dsem is not None:
                if o.dsem.last_op is not None:
                    deps.add(o.dsem.last_op)
                o.dsem.last_op = o.idx
                o.dsem.count += 16
                o.dval = o.dsem.count
            deps.discard(o.idx)
            for b in o.reads:
                b.readers.append(o.idx)
            for b in o.writes:
                b.last_w = o.idx
                b.readers = []
            o.deps = deps
        known = {e: {} for e in ENGS}
        for o in ops:
            need = {}
            for d in o.deps:
                od = ops[d]
                if od.dsem is not None:
                    key = ("d", id(od.dsem))
                    val = od.dval
                else:
                    if od.eng == "pe" and o.eng == "pe":
                        continue
                    key = ("e", od.eng)
                    val = d
                if key not in need or need[key][0] < val:
                    need[key] = (val, d)
            o.waits = []
            k = known[o.eng]
            for key, (val, d) in need.items():
                if key in k and k[key] >= val:
                    continue
                k[key] = val
                o.waits.append(d)
                if ops[d].dsem is None:
                    ops[d].signal = True
        cnt = {e: 0 for e in ENGS}
        for o in ops:
            if o.dsem is None and o.signal:
                n = cnt[o.eng]
                cnt[o.eng] = n + 1
                o.sig = (n // EPOCH, n % EPOCH + 1)
        self.n_epochs = {e: (cnt[e] + EPOCH - 1) // EPOCH for e in ENGS}
        return cnt

    def emit(self, nc, stack):
        cnt = self.analyze()
        esem = {}
        for e in ENGS:
            esem[e] = [stack.enter_context(nc.semaphore(f"s_{e}{k}"))
                       for k in range(max(1, self.n_epochs[e]))]
        for s in self.dsems:
            if s.count:
                s.handle = stack.enter_context(nc.semaphore(f"d_{s.name}"))
        ops = self.ops
        by_eng = {e: [o for o in ops if o.eng == e] for e in ENGS}

        def run(e, engobj):
            for o in by_eng[e]:
                for d in o.waits:
                    od = ops[d]
                    if od.dsem is not None:
                        engobj.wait_ge(od.dsem.handle, od.dval)
                    else:
                        ep, v = od.sig
                        engobj.wait_ge(esem[od.eng][ep], v)
                ins = o.fn(engobj)
                if o.dsem is not None:
                    ins.then_inc(o.dsem.handle, 16)
                elif o.signal:
                    ins.then_inc(esem[e][o.sig[0]], 1)

        block = stack.enter_context(nc.Block())

        @block.tensor
        def _(eng):
            run("pe", eng)

        @block.scalar
        def _(eng):
            run("act", eng)

        @block.vector
        def _(eng):
            run("dve", eng)

        @block.gpsimd
        def _(eng):
            run("pool", eng)

        @block.sync
        def _(eng):
            run("sp", eng)
        return cnt


D = 1024
S = 4096
T = 512
NT = S // T
NCH = D // 128
DFF = 2816
NFF = DFF // 128
FFN_PARTS = ((0, 8), (8, 15), (15, 22))
LRU_W = 1280
NLC = 10
NH = 8
EPS = 1e-6
SM_SCALE = 1.0 / math.sqrt(192.0)
GELU_C = 0.044715
GELU_S = 2.0 * math.sqrt(2.0 / math.pi)

SLOT_ELEMS = (33792, 24704)

C_MIXN = 0
C_FFNN = 32
C_CONVW = 64
C_LCW = 112
C_LCB = 152
C_LGAB = 162
C_LGXB = 172
C_LLAM = 182
C_QNORM = 192
C_KVNORM = 195
C_QNN = 197
C_KNN = 198
C_QRN = 199
C_KRN = 201
C_INVF = 203
NCOL = 208

WEIGHT_NAMES = ("conv_w_in", "conv_w_out", "lru_w_in", "lru_gate_a_w", "lru_gate_x_w",
                "lru_w_out", "mla_w_down", "mla_w_uq", "mla_w_ukv", "mla_w_o",
                "ffn_w_gu", "ffn_w_down")
WEIGHT_SHAPES = {
    "conv_w_in": [2, 1024, 3072], "conv_w_out": [2, 1024, 1024],
    "lru_w_in": [1, 1024, 2560], "lru_gate_a_w": [1, 10, 128, 128],
    "lru_gate_x_w": [1, 10, 128, 128], "lru_w_out": [1, 1280, 1024],
    "mla_w_down": [1, 1024, 704], "mla_w_uq": [1, 384, 1536],
    "mla_w_ukv": [1, 256, 2048], "mla_w_o": [1, 1024, 1024],
    "ffn_w_gu": [4, 1024, 5632], "ffn_w_down": [4, 2816, 1024],
}


class TB:
    __slots__ = ("ap", "b")

    def __init__(self, ap, b):
        self.ap = ap
        self.b = b


def build(stage_limit=None, dbg_out=None):
    nc = bass.Bass("TRN2", target_bir_lowering=False)
    P = Prog()
    st = contextlib.ExitStack()

    def din(name, shape, dt=F32):
        return nc.dram_tensor(name, shape, dt, kind="ExternalInput").ap()

    def dscr(name, shape, dt):
        return nc.dram_tensor(name, shape, dt, kind="Internal").ap()

    xT = din("xT", [D, S])
    pos = din("pos", [1, S], I32)
    cst_d = din("consts", [128, NCOL])
    cmat_d = din("cmat", [128, 256])
    Wd = {n: din(n, WEIGHT_SHAPES[n]) for n in WEIGHT_NAMES}
    out = nc.dram_tensor("out", [D, S], F32, kind="ExternalOutput").ap()

    R = dscr("R", [D, S], F32)
    ACC = dscr("ACC", [D, S], F32)
    XNd = dscr("XNd", [D, S], BF16)
    QNd = dscr("QNd", [NH * 128, S], BF16)
    QRd = dscr("QRd", [NH * 64, S], BF16)
    KNd = dscr("KNd", [NH * 128, S], BF16)
    KRd = dscr("KRd", [64, S], BF16)
    Vd = dscr("Vd", [NH, 128, S], BF16)
    OTd = dscr("OTd", [D, S], BF16)
    dram_bufs = {}

    def dbuf(name, j):
        k = (name, j)
        if k not in dram_bufs:
            dram_bufs[k] = Buf(f"{name}{j}")
        return dram_bufs[k]

    with st:
        def sb(name, shape, dt):
            return st.enter_context(nc.sbuf_tensor(name, shape, dt))

        def ps(name, shape, dt):
            return st.enter_context(nc.psum_tensor(name, shape, dt))

        Wt = [sb("W0", [128, SLOT_ELEMS[0]], BF16), sb("W1", [128, SLOT_ELEMS[1]], BF16)]
        Wb = [Buf("W0"), Buf("W1")]
        XS = [sb(f"XS{i}", [128, NCH, T], F32) for i in range(2)]
        XSb = [[Buf(f"XS{i}_{c}") for c in range(NCH)] for i in range(2)]
        XN = [sb(f"XN{i}", [128, NCH, T], BF16) for i in range(2)]
        XNb = [[Buf(f"XN{i}_{c}") for c in range(NCH)] for i in range(2)]
        HB = sb("HB", [128, NLC, T], BF16)
        HBb = [Buf(f"HB{c}") for c in range(NLC)]
        NF = 8
        FT = [sb(f"FT{i}", [128, 516], F32) for i in range(NF)]
        FTb = [Buf(f"FT{i}") for i in range(NF)]
        NB = 9
        BT = [sb(f"BT{i}", [128, T], BF16) for i in range(NB)]
        BTb = [Buf(f"BT{i}") for i in range(NB)]
        cst = sb("cst", [128, NCOL], F32)
        cstb = Buf("cst")
        cmat = sb("cmatb", [128, 256], BF16)
        cmatb = Buf("cmat")
        ones = sb("ones", [128, 128], BF16)
        onesb = Buf("ones")
        halo_c = sb("halo_c", [128, NCH, 2], F32)
        halo_cb = [Buf(f"hc{c}") for c in range(NCH)]
        halo_r = sb("halo_r", [128, NLC, 3], F32)
        halo_rb = [Buf(f"hr{c}") for c in range(NLC)]
        hst = sb("hst", [128, NLC], F32)
        hstb = [Buf(f"hs{c}") for c in range(NLC)]
        ls8 = sb("ls8", [128, 4 * NLC], F32)
        ls8b = Buf("ls8")
        posi = sb("posi", [32, T // 2], I32)
        posib = Buf("posi")
        RS = [sb(f"RS{i}", [128, T], F32) for i in range(2)]
        RSb = [Buf(f"RS{i}") for i in range(2)]
        PB = [ps(f"PB{i}", [128, T], F32) for i in range(8)]
        PBb = [Buf(f"PB{i}") for i in range(8)]

        sem_xs = [P.dma_sem(f"xs{i}") for i in range(2)]
        sem_xs_st = [P.dma_sem(f"xsst{i}") for i in range(2)]
        sem_xn = [P.dma_sem(f"xn{i}") for i in range(2)]
        sem_xn_st = [P.dma_sem(f"xnst{i}") for i in range(2)]
        sem_w = [P.dma_sem("w0"), P.dma_sem("w1")]
        sem_misc = P.dma_sem("misc")
        sem_bt = [P.dma_sem(f"bt{i}") for i in range(NB)]
        sem_hb = P.dma_sem("hbst")
        sem_pos = P.dma_sem("pos")

        def dtile(dr, j):
            return dr.rearrange("(c p) t -> p c t", p=128)[:, :, j * T:(j + 1) * T]

        def mm(o_ap, lhsT, rhs, start, stop, reads, writes, skip=False):
            P.op("pe", lambda e: e.matmul(o_ap, lhsT, rhs, start=start, stop=stop, skip_group_check=skip),
                 reads=reads, writes=writes)

        def act(o_ap, i_ap, func, reads, writes, bias=None, scale=None):
            kw = {}
            if bias is not None:
                kw["bias"] = bias
            if scale is not None:
                kw["scale"] = scale
            P.op("act", lambda e: e.activation(out=o_ap, in_=i_ap, func=func, **kw),
                 reads=reads, writes=writes)

        def tt(o_ap, a_ap, b_ap, op, reads, writes):
            P.op("dve", lambda e: e.tensor_tensor(out=o_ap, in0=a_ap, in1=b_ap, op=op),
                 reads=reads, writes=writes)

        def tsc(o_ap, a_ap, s1, s2, op0, op1, reads, writes):
            if s2 is None:
                P.op("dve", lambda e: e.tensor_scalar(out=o_ap, in0=a_ap, scalar1=s1, scalar2=None,
                                                      op0=op0), reads=reads, writes=writes)
            else:
                P.op("dve", lambda e: e.tensor_scalar(out=o_ap, in0=a_ap, scalar1=s1, scalar2=s2,
                                                      op0=op0, op1=op1), reads=reads, writes=writes)

        def stt(o_ap, a_ap, s_ap, b_ap, op0, op1, reads, writes):
            P.op("dve", lambda e: e.scalar_tensor_tensor(out=o_ap, in0=a_ap, scalar=s_ap, in1=b_ap,
                                                         op0=op0, op1=op1), reads=reads, writes=writes)

        def cpy(o_ap, i_ap, reads, writes):
            P.op("dve", lambda e: e.tensor_copy(out=o_ap, in_=i_ap), reads=reads, writes=writes)

        def dma(eng, o_ap, i_ap, reads, writes, sem):
            P.op(eng, lambda e: e.dma_start(out=o_ap, in_=i_ap), reads=reads, writes=writes, dsem=sem)

        def ccol(col, npart=128):
            return cst[0:npart, col:col + 1]

        dma("sp", cst[:], cst_d, [], [cstb], sem_misc)
        dma("pool", cmat[:], cmat_d, [], [cmatb], sem_w[0])
        P.op("dve", lambda e: e.memset(ones[:], 1.0), writes=[onesb])

        def wview(slot, off, k, n):
            return Wt[slot][:, off:off + k * n].rearrange("p (k n) -> p k n", k=k)

        def wload(slot, off, k, n, src):
            assert off + k * n <= SLOT_ELEMS[slot]
            v = wview(slot, off, k, n)
            for kk in range(k):
                dma("pool", v[:, kk, :], src[:, kk, :], [], [Wb[slot]], sem_w[slot])
            return v

        def kmaj(w2d, c0=None, c1=None):
            v = w2d.rearrange("(k p) n -> p k n", p=128)
            if c0 is not None:
                v = v[:, :, c0:c1]
            return v

        sq_ring = [0]

        def _rs(rs):
            if isinstance(rs, int):
                return RS[rs][:], RSb[rs]
            return rs.ap, rs.b

        def norm_stats(srcs, npart, dn, rs, ssb_i):
            rs_ap, rs_b = _rs(rs)
            n = len(srcs)
            for c, (ap, bufs) in enumerate(srcs):
                k = 8 - (sq_ring[0] % 2)
                sq_ring[0] += 1
                act(BT[k][0:npart, :], ap, AF.Square, bufs, [BTb[k]])
                mm(PB[ssb_i][:], ones[0:npart, :], BT[k][0:npart, :], c == 0, c == n - 1,
                   [BTb[k], onesb], [PBb[ssb_i]])
            act(rs_ap, PB[ssb_i][:], AF.Ln, [PBb[ssb_i]], [rs_b], bias=EPS, scale=1.0 / dn)
            act(rs_ap, rs_ap, AF.Exp, [rs_b], [rs_b], scale=-0.5)

        def norm_apply(srcs, npart, rs, gcols, outs):
            rs_ap, rs_b = _rs(rs)
            for (ap, bufs), gc, (oap, obufs) in zip(srcs, gcols, outs):
                stt(oap, ap, ccol(gc, npart), rs_ap[0:npart, :], ALU.mult, ALU.mult,
                    bufs + [rs_b, cstb], obufs)

        def load_x(src_d, src_name, j, slot):
            dma("sp", XS[slot][:], dtile(src_d, j), [dbuf(src_name, j)], XSb[slot], sem_xs[slot])

        STQ = "sp"

        def store_x(dst_d, dst_name, j, slot):
            dma(STQ, dtile(dst_d, j), XS[slot][:], XSb[slot], [dbuf(dst_name, j)], sem_xs_st[slot])

        def make_norm_pro(src, gbase, extra=None):
            def pa(j):
                load_x(src[0], src[1], j, j % 2)

            def pb(j):
                s_ = j % 2
                for c in range(NCH):
                    act(XN[s_][:, c, :], XS[s_][:, c, :], AF.Square, [XSb[s_][c]], [XNb[s_][c]])

            def pc(j):
                s_ = j % 2
                for c in range(NCH):
                    mm(PB[6][:], ones[:, :], XN[s_][:, c, :], c == 0, c == NCH - 1,
                       [XNb[s_][c], onesb], [PBb[6]])
                act(RS[s_][:], PB[6][:], AF.Ln, [PBb[6]], [RSb[s_]], bias=EPS, scale=1.0 / D)
                act(RS[s_][:], RS[s_][:], AF.Exp, [RSb[s_]], [RSb[s_]], scale=-0.5)

            def pd(j, c):
                s_ = j % 2
                stt(XN[s_][:, c, :], XS[s_][:, c, :], ccol(gbase + c), RS[s_][:], ALU.mult, ALU.mult,
                    [XSb[s_][c], RSb[s_], cstb], [XNb[s_][c]])
                if c == NCH - 1 and extra is not None:
                    extra(j)
            return [pa, pb, pc, pd]

        def pro_all(pro, j):
            pro[0](j)
            pro[1](j)
            pro[2](j)
            for c in range(NCH):
                pro[3](j, c)

        NOPRO = [lambda j: None, lambda j: None, lambda j: None, lambda j, c: None]

        def hook(j, step, pro, nxt):
            if j + 1 < NT:
                pro[step](j + 1)
            elif nxt is not None:
                nxt[step](0)

        def hook_d(j, pro, nxt):
            if j + 1 < NT:
                return lambda m: pro[3](j + 1, m)
            if nxt is not None:
                return lambda m: nxt[3](0, m)
            return None

        def xs_srcs(slot):
            return [(XS[slot][:, c, :], [XSb[slot][c]]) for c in range(NCH)]

        def xn_outs(slot):
            return [(XN[slot][:, c, :], [XNb[slot][c]]) for c in range(NCH)]

        def resid_out(slot, w_ap_fn, nk, rhs_fn, rhs_bufs, wslot, pbs, inter=None):
            for m in range(NCH):
                pb = pbs[m % len(pbs)]
                for k in range(nk):
                    mm(PB[pb][:], w_ap_fn(k, m), rhs_fn(k), k == 0, k == nk - 1,
                       [Wb[wslot]] + rhs_bufs(k), [PBb[pb]])
                tt(XS[slot][:, m, :], XS[slot][:, m, :], PB[pb][:], ALU.add,
                   [XSb[slot][m], PBb[pb]], [XSb[slot][m]])
                if inter is not None:
                    inter(m)

        def stage_ffn(part, l, src, dst, wslot, dst_is_out=False):
            f0, f1 = FFN_PARTS[part]
            nf = f1 - f0
            wv = {}

            def loadw():
                gu = Wd["ffn_w_gu"][l]
                wv["g"] = wload(wslot, 0, NCH, nf * 128, kmaj(gu, f0 * 128, f1 * 128))
                wv["u"] = wload(wslot, NCH * nf * 128, NCH, nf * 128,
                                kmaj(gu, DFF + f0 * 128, DFF + f1 * 128))
                wv["d"] = wload(wslot, 2 * NCH * nf * 128, nf, D,
                                Wd["ffn_w_down"][l][f0 * 128:f1 * 128, :].rearrange("(k p) n -> p k n", p=128))

            if part == 0:
                pro = make_norm_pro(src, C_FFNN + l * 8, extra=lambda j: dma(
                    STQ, dtile(XNd, j), XN[j % 2][:], XNb[j % 2], [dbuf("XNd", j)], sem_xn_st[j % 2]))
            else:
                def _pa(j):
                    load_x(ACC, "ACC", j, j % 2)
                    dma("sp", XN[j % 2][:], dtile(XNd, j), [dbuf("XNd", j)], XNb[j % 2], sem_xn[j % 2])
                pro = [_pa, lambda j: None, lambda j: None, lambda j, c: None]

            def run(nxt=None, hoisted=False):
                if not hoisted:
                    pro_all(pro, 0)
                for j in range(NT):
                    s = j % 2
                    for f in range(nf):
                        pg, pu = f % 2, 2 + f % 2
                        for k in range(NCH):
                            mm(PB[pg][:], wv["g"][:, k, f * 128:(f + 1) * 128], XN[s][:, k, :],
                               k == 0, k == NCH - 1, [Wb[wslot], XNb[s][k]], [PBb[pg]])
                        for k in range(NCH):
                            mm(PB[pu][:], wv["u"][:, k, f * 128:(f + 1) * 128], XN[s][:, k, :],
                               k == 0, k == NCH - 1, [Wb[wslot], XNb[s][k]], [PBb[pu]])
                        sg = f % 2
                        act(BT[sg][:], PB[pg][:], AF.Silu, [PBb[pg]], [BTb[sg]])
                        tt(HB[:, f, :], BT[sg][:], PB[pu][:], ALU.mult, [BTb[sg], PBb[pu]], [HBb[f]])
                        if f in (0, 4, 6):
                            hook(j, {0: 0, 4: 1, 6: 2}[f], pro, nxt)
                    resid_out(s, lambda k, m: wv["d"][:, k, m * 128:(m + 1) * 128], nf,
                              lambda k: HB[:, k, :], lambda k: [HBb[k]], wslot, (4, 5), hook_d(j, pro, nxt))
                    if part == 2:
                        store_x(dst[0], dst[1], j, s)
                    else:
                        store_x(ACC, "ACC", j, s)
            return {"loadw": loadw, "run": run, "pro": pro}

        def stage_conv(l, jc, src, dst, wslot):
            wv = {}

            def loadw():
                wv["in"] = wload(wslot, 0, NCH, 3 * D, kmaj(Wd["conv_w_in"][jc]))
                wv["out"] = wload(wslot, NCH * 3 * D, NCH, D, kmaj(Wd["conv_w_out"][jc]))

            pro = make_norm_pro(src, C_MIXN + l * 8)

            def run(nxt=None, hoisted=False):
                P.op("dve", lambda e: e.memset(halo_c[:], 0.0), writes=halo_cb)
                if not hoisted:
                    pro_all(pro, 0)
                for j in range(NT):
                    s = j % 2
                    for c in range(NCH):
                        q = c % 2
                        pbB, pbC, pbH = 3 * q, 3 * q + 1, 3 * q + 2
                        for which, pb in ((0, pbB), (1, pbC), (2, pbH)):
                            col = which * D + c * 128
                            for k in range(NCH):
                                mm(PB[pb][:], wv["in"][:, k, col:col + 128], XN[s][:, k, :],
                                   k == 0, k == NCH - 1, [Wb[wslot], XNb[s][k]], [PBb[pb]])
                        cs, ut, t1 = 3 * q, 3 * q + 1, 3 * q + 2
                        act(FT[cs][:, 0:T], PB[pbC][:], AF.Copy, [PBb[pbC]], [FTb[cs]])
                        cpy(FT[ut][:, 0:2], halo_c[:, c, :], [halo_cb[c]], [FTb[ut]])
                        tt(FT[ut][:, 2:2 + T], FT[cs][:, 0:T], PB[pbH][:], ALU.mult,
                           [FTb[cs], PBb[pbH]], [FTb[ut]])
                        cpy(halo_c[:, c, :], FT[ut][:, T:T + 2], [FTb[ut]], [halo_cb[c]])
                        wc = C_CONVW + jc * 24 + c
                        tsc(FT[t1][:, 0:T], FT[ut][:, 2:2 + T], ccol(wc + 16), None, ALU.mult, None,
                            [FTb[ut], cstb], [FTb[t1]])
                        stt(FT[t1][:, 0:T], FT[ut][:, 1:1 + T], ccol(wc + 8), FT[t1][:, 0:T], ALU.mult, ALU.add,
                            [FTb[ut], FTb[t1], cstb], [FTb[t1]])
                        stt(FT[t1][:, 0:T], FT[ut][:, 0:T], ccol(wc), FT[t1][:, 0:T], ALU.mult, ALU.add,
                            [FTb[ut], FTb[t1], cstb], [FTb[t1]])
                        tt(HB[:, c, :], FT[t1][:, 0:T], PB[pbB][:], ALU.mult, [FTb[t1], PBb[pbB]], [HBb[c]])
                        if c in (0, 4, 6):
                            hook(j, {0: 0, 4: 1, 6: 2}[c], pro, nxt)
                    resid_out(s, lambda k, m: wv["out"][:, k, m * 128:(m + 1) * 128], NCH,
                              lambda k: HB[:, k, :], lambda k: [HBb[k]], wslot, (6, 0), hook_d(j, pro, nxt))
                    store_x(dst[0], dst[1], j, s)
            return {"loadw": loadw, "run": run, "pro": pro}

        def stage_lru(l, src, dst, wslot):
            wv = {}

            def loadw():
                wv["in"] = wload(wslot, 0, NCH, 2 * LRU_W, kmaj(Wd["lru_w_in"][0]))
                o = NCH * 2 * LRU_W
                wv["a"] = wload(wslot, o, NLC, 128, Wd["lru_gate_a_w"][0].rearrange("n d e -> d n e"))
                o += NLC * 128
                wv["x"] = wload(wslot, o, NLC, 128, Wd["lru_gate_x_w"][0].rearrange("n d e -> d n e"))
                o += NLC * 128
                wv["out"] = wload(wslot, o, NLC, D, kmaj(Wd["lru_w_out"][0]))

            pro = make_norm_pro(src, C_MIXN + l * 8)

            def run(nxt=None, hoisted=False):
                P.op("dve", lambda e: e.memset(halo_r[:], 0.0), writes=halo_rb)
                P.op("dve", lambda e: e.memset(hst[:], 0.0), writes=hstb)
                act(ls8[:, NLC:2 * NLC], cst[:, C_LLAM:C_LLAM + NLC], AF.Exp, [cstb], [ls8b], scale=-1.0)
                act(ls8[:, NLC:2 * NLC], ls8[:, NLC:2 * NLC], AF.Ln, [ls8b], [ls8b], bias=1.0)
                tsc(ls8[:, 0:NLC], ls8[:, NLC:2 * NLC], -4.0, None, ALU.mult, None, [ls8b], [ls8b])
                tsc(ls8[:, NLC:2 * NLC], ls8[:, NLC:2 * NLC], -8.0, None, ALU.mult, None, [ls8b], [ls8b])
                tsc(ls8[:, 2 * NLC:3 * NLC], cst[:, C_LGAB:C_LGAB + NLC], 0.5, None, ALU.mult, None,
                    [cstb, ls8b], [ls8b])
                tsc(ls8[:, 3 * NLC:4 * NLC], cst[:, C_LGXB:C_LGXB + NLC], 0.5, None, ALU.mult, None,
                    [cstb, ls8b], [ls8b])
                if not hoisted:
                    pro_all(pro, 0)
                def temps(c):
                    q = c % 2
                    return (FT[4 * q][:, 0:T], FT[4 * q + 1][:, 0:T], FT[4 * q + 2], FT[4 * q + 3][:, 0:T],
                            4 * q, 4 * q + 1, 4 * q + 2, 4 * q + 3, 2 * q, 2 * q + 1, 2 + q, 4 + q)

                def H1p(s, c):
                    for which, pb in ((0, 0), (1, 1)):
                        col = which * LRU_W + c * 128
                        for k in range(NCH):
                            mm(PB[pb][:], wv["in"][:, k, col:col + 128], XN[s][:, k, :],
                               k == 0, k == NCH - 1, [Wb[wslot], XNb[s][k]], [PBb[pb]])

                def H1(s, c):
                    A0, A1, A2, A3, f0, f1_, f2, f3, gtb, recb, pA, pX = temps(c)
                    pG, pR = 0, 1
                    act(BT[gtb][:], PB[pG][:], AF.Gelu_apprx_tanh, [PBb[pG]], [BTb[gtb]])
                    cpy(A2[:, 0:3], halo_r[:, c, :], [halo_rb[c]], [FTb[f2]])
                    act(A2[:, 3:3 + T], PB[pR][:], AF.Copy, [PBb[pR]], [FTb[f2]])
                    cpy(halo_r[:, c, :], A2[:, T:T + 3], [FTb[f2]], [halo_rb[c]])
                    tsc(A3, A2[:, 3:3 + T], ccol(C_LCW + 30 + c), ccol(C_LCB + c), ALU.mult, ALU.add,
                        [FTb[f2], cstb], [FTb[f3]])
                    for tap in (2, 1, 0):
                        stt(A3, A2[:, tap:tap + T], ccol(C_LCW + tap * 10 + c), A3, ALU.mult, ALU.add,
                            [FTb[f2], FTb[f3], cstb], [FTb[f3]])

                def H1b(s, c):
                    A0, A1, A2, A3, f0, f1_, f2, f3, gtb, recb, pA, pX = temps(c)
                    act(BT[recb][:], A3, AF.Copy, [FTb[f3]], [BTb[recb]])
                    mm(PB[pA][:], wv["a"][:, c, :], BT[recb][:], True, True, [Wb[wslot], BTb[recb]], [PBb[pA]])
                    mm(PB[pX][:], wv["x"][:, c, :], BT[recb][:], True, True, [Wb[wslot], BTb[recb]], [PBb[pX]])

                def H2a(s, c):
                    A0, A1, A2, A3, f0, f1_, f2, f3, gtb, recb, pA, pX = temps(c)
                    act(A0, PB[pA][:], AF.Tanh, [PBb[pA], ls8b], [FTb[f0]],
                        bias=ls8[:, 2 * NLC + c:2 * NLC + c + 1], scale=0.5)
                    act(A1, PB[pX][:], AF.Tanh, [PBb[pX], ls8b], [FTb[f1_]],
                        bias=ls8[:, 3 * NLC + c:3 * NLC + c + 1], scale=0.5)
                    A2t = A2[:, 0:T]
                    act(A2t, A0, AF.Exp, [FTb[f0], ls8b], [FTb[f2]],
                        bias=ls8[:, c:c + 1], scale=ls8[:, c:c + 1])
                    act(A0, A0, AF.Exp, [FTb[f0], ls8b], [FTb[f0]],
                        bias=ls8[:, NLC + c:NLC + c + 1], scale=ls8[:, NLC + c:NLC + c + 1])
                    act(A0, A0, AF.Sqrt, [FTb[f0]], [FTb[f0]], bias=1.0, scale=-1.0)

                def H2b(s, c):
                    A0, A1, A2, A3, f0, f1_, f2, f3, gtb, recb, pA, pX = temps(c)
                    A2t = A2[:, 0:T]
                    stt(A1, A1, 1.0, A3, ALU.add, ALU.mult, [FTb[f1_], FTb[f3]], [FTb[f1_]])
                    stt(A1, A1, 0.5, A0, ALU.mult, ALU.mult, [FTb[f1_], FTb[f0]], [FTb[f1_]])
                    P.op("dve", (lambda e, A3=A3, A2t=A2t, A1=A1, c=c: e.tensor_tensor_scan(
                        out=A3, data0=A2t, data1=A1, initial=hst[:, c:c + 1], op0=ALU.mult, op1=ALU.add)),
                        reads=[FTb[f2], FTb[f1_], hstb[c]], writes=[FTb[f3]])
                    cpy(hst[:, c:c + 1], FT[f3][:, T - 1:T], [FTb[f3]], [hstb[c]])
                    tt(HB[:, c, :], BT[gtb][:], A3, ALU.mult, [BTb[gtb], FTb[f3]], [HBb[c]])

                for j in range(NT):
                    s = j % 2
                    H1p(s, 0)
                    H1(s, 0)
                    H1p(s, 1)
                    H1b(s, 0)
                    for c in range(NLC):
                        if c + 1 < NLC:
                            H1(s, c + 1)
                        if c + 2 < NLC:
                            H1p(s, c + 2)
                        H2a(s, c)
                        if c + 1 < NLC:
                            H1b(s, c + 1)
                        H2b(s, c)
                        if c in (0, 4, 6):
                            hook(j, {0: 0, 4: 1, 6: 2}[c], pro, nxt)
                    resid_out(s, lambda k, m: wv["out"][:, k, m * 128:(m + 1) * 128], NLC,
                              lambda k: HB[:, k, :], lambda k: [HBb[k]], wslot, (6, 0), hook_d(j, pro, nxt))
                    store_x(dst[0], dst[1], j, s)
            return {"loadw": loadw, "run": run, "pro": pro}

        def stage_mla1(l, src, wslot):
            wv = {}
            sem_stg = [P.dma_sem(f"stg{i}") for i in range(6)]

            def loadw():
                wv["down"] = wload(wslot, 0, NCH, 704, kmaj(Wd["mla_w_down"][0]))
                o = NCH * 704
                wv["uq"] = wload(wslot, o, 3, 1536, kmaj(Wd["mla_w_uq"][0]))
                o += 3 * 1536
                wv["ukv"] = wload(wslot, o, 2, 2048, kmaj(Wd["mla_w_ukv"][0]))

            rot = [0]
            srot = [0]

            def nextpb(n=5):
                r = rot[0] % n
                rot[0] += 1
                return r

            def nextss():
                r = 6 + srot[0] % 2
                srot[0] += 1
                return r

            rot6 = [0]
            sqr4 = [0]

            def nextpb6():
                r = rot6[0] % 6
                rot6[0] += 1
                return r

            pro = make_norm_pro(src, C_MIXN + l * 8)

            def prologue(j):
                pro_all(pro, j)

            def rope_pair(p1, p2, b1, b2, gcol, cosb, sinb, o1, o2, ob, rs, tmp):
                rs_ap, rs_b = _rs(rs)
                t1, t2, t3, t4 = [t.ap[0:32, :] for t in tmp]
                tb1, tb2, tb3, tb4 = [t.b for t in tmp]
                cs_, sn_ = FT[cosb][0:32, 0:T], FT[sinb][0:32, 0:T]
                stt(t1, p1, ccol(gcol, 32), rs_ap[0:32, :], ALU.mult, ALU.mult, b1 + [rs_b, cstb], [tb1])
                stt(t2, p2, ccol(gcol + 1, 32), rs_ap[0:32, :], ALU.mult, ALU.mult, b2 + [rs_b, cstb], [tb2])
                tt(t3, t2, sn_, ALU.mult, [tb2, FTb[sinb]], [tb3])
                tt(t4, t1, cs_, ALU.mult, [tb1, FTb[cosb]], [tb4])
                tt(o1, t4, t3, ALU.subtract, [tb4, tb3], ob)
                tt(t3, t1, sn_, ALU.mult, [tb1, FTb[sinb]], [tb3])
                tt(t4, t2, cs_, ALU.mult, [tb2, FTb[cosb]], [tb4])
                tt(o2, t4, t3, ALU.add, [tb4, tb3], ob)

            def run(nxt=None, hoisted=False):
                if not hoisted:
                    prologue(0)
                for j in range(NT):
                    s = j % 2
                    o_ = 1 - s
                    xk = lambda k: XN[s][:, k, :]
                    xtmp = [TB(XS[o_][:, i, :], XSb[o_][i]) for i in range(NCH)]
                    stg = [TB(XN[o_][:, i, :], XNb[o_][i]) for i in range(6)]
                    ftmp = [TB(FT[4 + i][:, 0:T], FTb[4 + i]) for i in range(4)]
                    ang = FT[0][0:32, 0:T]
                    for hf in range(2):
                        t0_ = j * T + hf * (T // 2)
                        dma("sp", posi[:], pos[0:1, t0_:t0_ + T // 2].broadcast_to([32, T // 2]), [], [posib],
                            sem_pos)
                        cpy(FT[0][0:32, hf * (T // 2):(hf + 1) * (T // 2)], posi[:], [posib], [FTb[0]])
                    tsc(ang, ang, ccol(C_INVF, 32), None, ALU.mult, None, [FTb[0], cstb], [FTb[0]])
                    MAGIC = 12582912.0
                    C1 = 6.28125
                    C2 = 2.0 * math.pi - C1
                    kk = FT[1][0:32, 0:T]
                    for dst, shift in ((3, 0.0), (2, 0.5 * math.pi)):
                        a2 = FT[dst][0:32, 0:T]
                        tsc(a2, ang, shift, None, ALU.add, None, [FTb[0]], [FTb[dst]])
                        tsc(kk, a2, 1.0 / (2.0 * math.pi), None, ALU.mult, None, [FTb[dst]], [FTb[1]])
                        tsc(kk, kk, MAGIC, None, ALU.add, None, [FTb[1]], [FTb[1]])
                        tsc(kk, kk, -MAGIC, None, ALU.add, None, [FTb[1]], [FTb[1]])
                        stt(a2, kk, -C1, a2, ALU.mult, ALU.add, [FTb[1], FTb[dst]], [FTb[dst]])
                        stt(a2, kk, -C2, a2, ALU.mult, ALU.add, [FTb[1], FTb[dst]], [FTb[dst]])
                        tsc(a2, a2, math.pi, -math.pi, ALU.min, ALU.max, [FTb[dst]], [FTb[dst]])
                        act(a2, a2, AF.Sin, [FTb[dst]], [FTb[dst]])
                    def proj_down(col, width, pb):
                        for k in range(NCH):
                            mm(PB[pb][0:width, :], wv["down"][:, k, col:col + width], xk(k),
                               k == 0, k == NCH - 1, [Wb[wslot], XNb[s][k]], [PBb[pb]])
                    pbs = [nextpb() for _ in range(3)]
                    for i, pb in enumerate(pbs):
                        proj_down(i * 128, 128, pb)
                    srcs = [(PB[pb][:], [PBb[pb]]) for pb in pbs]
                    norm_stats(srcs, 128, 384, xtmp[0], nextss())
                    cqn = [(BT[i][:], [BTb[i]]) for i in range(3)]
                    norm_apply(srcs, 128, xtmp[0], [C_QNORM + i for i in range(3)], cqn)
                    pbs = [nextpb() for _ in range(2)]
                    for i, pb in enumerate(pbs):
                        proj_down(384 + i * 128, 128, pb)
                    srcs = [(PB[pb][:], [PBb[pb]]) for pb in pbs]
                    norm_stats(srcs, 128, 256, xtmp[1], nextss())
                    ckvn = [(BT[3 + i][:], [BTb[3 + i]]) for i in range(2)]
                    norm_apply(srcs, 128, xtmp[1], [C_KVNORM + i for i in range(2)], ckvn)
                    pb1, pb2 = nextpb(), nextpb()
                    proj_down(640, 32, pb1)
                    proj_down(672, 32, pb2)
                    norm_stats([(PB[pb1][0:32, :], [PBb[pb1]]), (PB[pb2][0:32, :], [PBb[pb2]])], 32, 64,
                               xtmp[2], nextss())
                    rope_pair(PB[pb1][0:32, :], PB[pb2][0:32, :], [PBb[pb1]], [PBb[pb2]], C_KRN, 2, 3,
                              HB[0:32, 8, :], HB[32:64, 8, :], [HBb[8]], xtmp[2], ftmp)
                    dma(STQ, KRd[:, j * T:(j + 1) * T], HB[0:64, 8, :], [HBb[8]], [dbuf("KRd", j)], sem_bt[5])
                    items = []
                    for h in range(NH):
                        hp = h % 2
                        qn_s, qr_s, kn_s = stg[hp], stg[2 + hp], stg[4 + hp]

                        def qn_proj(h=h):
                            pb = nextpb6()
                            for k in range(3):
                                mm(PB[pb][:], wv["uq"][:, k, h * 192:h * 192 + 128], cqn[k][0], k == 0, k == 2,
                                   [Wb[wslot]] + cqn[k][1], [PBb[pb]])
                            return [(PB[pb][:], [PBb[pb]])]

                        def qn_fin(srcs, rs, h=h, hp=hp, qn_s=qn_s):
                            norm_apply(srcs, 128, rs, [C_QNN], [(qn_s.ap, [qn_s.b])])
                            dma(STQ, QNd[h * 128:(h + 1) * 128, j * T:(j + 1) * T], qn_s.ap, [qn_s.b],
                                [dbuf("QNd", j)], sem_stg[hp])

                        def qr_proj(h=h):
                            pb1, pb2 = nextpb6(), nextpb6()
                            for pbx, c0 in ((pb1, h * 192 + 128), (pb2, h * 192 + 160)):
                                for k in range(3):
                                    mm(PB[pbx][0:32, :], wv["uq"][:, k, c0:c0 + 32], cqn[k][0], k == 0, k == 2,
                                       [Wb[wslot]] + cqn[k][1], [PBb[pbx]])
                            return [(PB[pb1][0:32, :], [PBb[pb1]]), (PB[pb2][0:32, :], [PBb[pb2]])]

                        def qr_fin(srcs, rs, h=h, hp=hp, qr_s=qr_s):
                            rtmp = ftmp if hp == 0 else xtmp[4:8]
                            rope_pair(srcs[0][0], srcs[1][0], srcs[0][1], srcs[1][1], C_QRN, 2, 3,
                                      qr_s.ap[0:32, :], qr_s.ap[32:64, :], [qr_s.b], rs, rtmp)
                            dma(STQ, QRd[h * 64:(h + 1) * 64, j * T:(j + 1) * T], qr_s.ap[0:64, :], [qr_s.b],
                                [dbuf("QRd", j)], sem_stg[2 + hp])

                        def kn_proj(h=h):
                            pb = nextpb6()
                            for k in range(2):
                                mm(PB[pb][:], wv["ukv"][:, k, h * 256:h * 256 + 128], ckvn[k][0], k == 0, k == 1,
                                   [Wb[wslot]] + ckvn[k][1], [PBb[pb]])
                            return [(PB[pb][:], [PBb[pb]])]

                        def kn_fin(srcs, rs, h=h, hp=hp, kn_s=kn_s):
                            norm_apply(srcs, 128, rs, [C_KNN], [(kn_s.ap, [kn_s.b])])
                            dma(STQ, KNd[h * 128:(h + 1) * 128, j * T:(j + 1) * T], kn_s.ap, [kn_s.b],
                                [dbuf("KNd", j)], sem_stg[4 + hp])

                        items += [(qn_proj, 128, 128, qn_fin), (qr_proj, 32, 64, qr_fin),
                                  (kn_proj, 128, 128, kn_fin)]

                    def it_squares(srcs, npart):
                        sq = []
                        for ap, bufs in srcs:
                            k = 5 + sqr4[0] % 4
                            sqr4[0] += 1
                            act(BT[k][0:npart, :], ap, AF.Square, bufs, [BTb[k]])
                            sq.append(k)
                        return sq

                    def it_stats(sq, npart, dn, rs):
                        ssb = 6 + srot[0] % 2
                        srot[0] += 1
                        for c, k in enumerate(sq):
                            mm(PB[ssb][:], ones[0:npart, :], BT[k][0:npart, :], c == 0, c == len(sq) - 1,
                               [BTb[k], onesb], [PBb[ssb]])
                        act(rs.ap, PB[ssb][:], AF.Ln, [PBb[ssb]], [rs.b], bias=EPS, scale=1.0 / dn)
                        act(rs.ap, rs.ap, AF.Exp, [rs.b], [rs.b], scale=-0.5)

                    prev = None
                    for i, (proj, npart, dn, fin) in enumerate(items):
                        srcs = proj()
                        sq = it_squares(srcs, npart)
                        if prev is not None:
                            it_stats(prev[1], prev[2], prev[3], prev[4])
                            prev[5](prev[0], prev[4])
                        prev = (srcs, sq, npart, dn, xtmp[i % 4], fin)
                    it_stats(prev[1], prev[2], prev[3], prev[4])
                    prev[5](prev[0], prev[4])
                    vsrc = wv["ukv"].rearrange("p k (h e) -> p k h e", h=NH)
                    for cc in range(4):
                        for hh in range(2):
                            pb = nextpb()
                            for k in range(2):
                                mm(PB[pb][:].rearrange("p (h d) -> p h d", h=4),
                                   ckvn[k][0][:, cc * 128:(cc + 1) * 128],
                                   vsrc[:, k, hh * 4:(hh + 1) * 4, 128:256], k == 0, k == 1,
                                   [Wb[wslot]] + ckvn[k][1], [PBb[pb]])
                            act(HB[:, hh * 4:(hh + 1) * 4, cc * 128:(cc + 1) * 128],
                                PB[pb][:].rearrange("p (h d) -> p h d", h=4), AF.Copy,
                                [PBb[pb]], [HBb[hh * 4 + i] for i in range(4)])
                    dma(STQ, Vd[:, :, j * T:(j + 1) * T].rearrange("h p t -> p h t"), HB[:, 0:8, :],
                        [HBb[i] for i in range(8)], [dbuf("Vd", j)], sem_hb)
                    if j + 1 < NT:
                        prologue(j + 1)
            return {"loadw": loadw, "run": run, "pro": pro}

        def stage_mla2(wslot):
            W = Wt[wslot]
            o = [0]

            def carve(n):
                a = o[0]
                o[0] += n
                assert o[0] <= SLOT_ELEMS[wslot]
                return a
            KN_o = [carve(S), carve(S)]
            KR_o = carve(S)
            VR_o = [carve(S), carve(S)]
            KNb = [Buf("KN0"), Buf("KN1")]
            KRb = Buf("KRs")
            VRb = [Buf("VR0"), Buf("VR1")]
            sem_kn = [P.dma_sem("kn0"), P.dma_sem("kn1")]
            sem_kr = P.dma_sem("krs")
            sem_vr = [P.dma_sem("vr0"), P.dma_sem("vr1")]
            wr = [Wb[wslot]]
            allkn = [dbuf("KNd", j) for j in range(NT)]
            allkr = [dbuf("KRd", j) for j in range(NT)]
            allv = [dbuf("Vd", j) for j in range(NT)]
            kr = W[0:64, KR_o:KR_o + S]

            def kn_ap(h):
                return W[:, KN_o[h % 2]:KN_o[h % 2] + S]

            def v_ap(h):
                return W[:, VR_o[h % 2]:VR_o[h % 2] + S].rearrange("p (c d) -> p c d", d=128)

            def loadw():
                pass

            def load_head_kv(h):
                dma("sp", kn_ap(h), KNd[h * 128:(h + 1) * 128, :], allkn + wr, [KNb[h % 2]], sem_kn[h % 2])
                dma("sp", W[:, VR_o[h % 2]:VR_o[h % 2] + S], Vd[h], allv + wr, [VRb[h % 2]], sem_vr[h % 2])

            units = [(h, j) for h in range(NH) for j in range(NT)]
            G = [(u, kc) for u, (h, j) in enumerate(units) for kc in range(4 * j + 4)]

            def load_q(u):
                h, j = units[u]
                qs = u % 2
                dma("sp", BT[2 * qs][:], QNd[h * 128:(h + 1) * 128, j * T:(j + 1) * T],
                    [dbuf("QNd", j)], [BTb[2 * qs]], sem_bt[2 * qs])
                dma("sp", BT[2 * qs + 1][0:64, :], QRd[h * 64:(h + 1) * 64, j * T:(j + 1) * T],
                    [dbuf("QRd", j)], [BTb[2 * qs + 1]], sem_bt[2 * qs + 1])

            def emit_S(g):
                u, kc = G[g]
                h, j = units[u]
                if kc == 0:
                    if u + 1 < len(units):
                        load_q(u + 1)
                    if j == 1 and h + 1 < NH:
                        load_head_kv(h + 1)
                qs = u % 2
                qn_t, qr_t = 2 * qs, 2 * qs + 1
                q0 = max(kc - 4 * j, 0) * 128
                pb = g % 3
                mm(PB[pb][:, q0:T], kn_ap(h)[:, kc * 128:(kc + 1) * 128], BT[qn_t][:, q0:T], True, False,
                   [KNb[h % 2], BTb[qn_t]] + wr, [PBb[pb]])
                mm(PB[pb][:, q0:T], kr[:, kc * 128:(kc + 1) * 128], BT[qr_t][0:64, q0:T], False, True,
                   [KRb, BTb[qr_t]] + wr, [PBb[pb]])

            def emit_exp(g):
                u, kc = G[g]
                h, j = units[u]
                dgn = kc - 4 * j
                q0 = max(dgn, 0) * 128
                pb = g % 3
                pt = 4 + g % 3
                act(BT[pt][:, q0:T], PB[pb][:, q0:T], AF.Exp, [PBb[pb]], [BTb[pt]], scale=SM_SCALE)
                if dgn >= 0:
                    tt(BT[pt][:, q0:q0 + 128], BT[pt][:, q0:q0 + 128], cmat[:, 128:256], ALU.mult,
                       [BTb[pt], cmatb], [BTb[pt]])

            def emit_PV(g):
                u, kc = G[g]
                h, j = units[u]
                q0 = max(kc - 4 * j, 0) * 128
                pt = 4 + g % 3
                ob = 3 + (u % 2)
                db = 5 + (u % 2)
                last = kc == 4 * j + 3
                mm(PB[ob][:, q0:T], v_ap(h)[:, kc, :], BT[pt][:, q0:T], kc == 0, last,
                   [BTb[pt], VRb[h % 2]] + wr, [PBb[ob]], skip=True)
                mm(PB[db][:, q0:T], ones[:, :], BT[pt][:, q0:T], kc == 0, last,
                   [BTb[pt], onesb], [PBb[db]], skip=True)

            def emit_tail(u):
                h, j = units[u]
                ob = 3 + (u % 2)
                db = 5 + (u % 2)
                rd = u % 2
                P.op("dve", (lambda e, db=db, rd=rd: e.reciprocal(out=FT[rd][:, 0:T], in_=PB[db][:])),
                     reads=[PBb[db]], writes=[FTb[rd]])
                ot = 7 + u % 2
                tt(BT[ot][:], PB[ob][:], FT[rd][:, 0:T], ALU.mult, [PBb[ob], FTb[rd]], [BTb[ot]])
                dma(STQ, OTd[h * 128:(h + 1) * 128, j * T:(j + 1) * T], BT[ot][:], [BTb[ot]],
                    [dbuf("OTd", j)], sem_bt[ot])

            def run(nxt=None, hoisted=False):
                P.op("sp", lambda e: e.nop(), reads=[], writes=[Wb[wslot]])
                dma("sp", kr, KRd, allkr + wr, [KRb], sem_kr)
                load_head_kv(0)
                load_q(0)
                n = len(G)
                LOOK = 2
                for g in range(min(LOOK, n)):
                    emit_S(g)
                for g in range(n):
                    u, kc = G[g]
                    h, j = units[u]
                    emit_exp(g)
                    if g + LOOK < n:
                        emit_S(g + LOOK)
                    emit_PV(g)
                    if kc == 4 * j + 3:
                        emit_tail(u)
            return {"loadw": loadw, "run": run, "pro": None}

        def stage_mla3(src, dst, wslot):
            wv = {}

            def loadw():
                wv["o"] = wload(wslot, 0, NCH, D, kmaj(Wd["mla_w_o"][0]))

            def prologue(j):
                s = j % 2
                load_x(src[0], src[1], j, s)
                dma("sp", XN[s][:], dtile(OTd, j), [dbuf("OTd", j)], XNb[s], sem_xn[s])

            def run(nxt=None, hoisted=False):
                prologue(0)
                for j in range(NT):
                    s = j % 2
                    if j + 1 < NT:
                        prologue(j + 1)
                    elif nxt is not None:
                        nxt[0](0)
                        nxt[1](0)
                        nxt[2](0)
                    resid_out(s, lambda k, m: wv["o"][:, k, m * 128:(m + 1) * 128], NCH,
                              lambda k: XN[s][:, k, :], lambda k: [XNb[s][k]], wslot, (4, 5),
                              hook_d(j, NOPRO, nxt))
                    store_x(dst[0], dst[1], j, s)
            return {"loadw": loadw, "run": run, "pro": None}

        Rr = (R, "R")
        stages = []
        seq = [
            ("conv", 0, 0, (xT, "xT"), Rr),
            ("ffn", 0), ("lru", 1), ("ffn", 1),
            ("mla", 2), ("ffn", 2),
            ("conv", 3, 1, Rr, Rr),
            ("ffn", 3),
        ]
        plan = []
        for item in seq:
            if item[0] == "conv":
                plan.append(("conv", item[1], item[2], item[3], item[4]))
            elif item[0] == "lru":
                plan.append(("lru", item[1]))
            elif item[0] == "mla":
                plan += [("mla1", item[1]), ("mla2",), ("mla3",)]
            else:
                plan += [("ffn", p, item[1]) for p in range(3)]
        if stage_limit is not None:
            plan = plan[:stage_limit]
        built = []
        for si, pl in enumerate(plan):
            slot = si % 2
            last = (si == len(plan) - 1)
            final_dst = (out, "out") if last else Rr
            if pl[0] == "conv":
                built.append(stage_conv(pl[1], pl[2], pl[3], final_dst, slot))
            elif pl[0] == "lru":
                built.append(stage_lru(pl[1], Rr, final_dst, slot))
            elif pl[0] == "mla1":
                built.append(stage_mla1(pl[1], Rr, slot))
            elif pl[0] == "mla2":
                built.append(stage_mla2(slot))
            elif pl[0] == "mla3":
                built.append(stage_mla3(Rr, final_dst, slot))
            else:
                built.append(stage_ffn(pl[1], pl[2], Rr, final_dst, slot))
        built[0]["loadw"]()
        hoisted = False
        for si, stg_ in enumerate(built):
            nxt = None
            if si + 1 < len(built):
                built[si + 1]["loadw"]()
                nxt = built[si + 1]["pro"]
                if plan[si][0] in ("mla1", "mla2"):
                    nxt = None
            stg_["run"](nxt, hoisted)
            hoisted = nxt is not None
        P.op("sp", lambda e: e.nop(), reads=[dbuf("out", j) for j in range(NT)])
        cnt = P.emit(nc, st)
        nc._mk_stats = (len(P.ops), cnt)
    return nc


def host_consts(inp, b):
    c = np.zeros((128, NCOL), np.float32)

    def put(col, vec):
        v = np.asarray(vec, np.float32).reshape(-1, 128)
        for i in range(v.shape[0]):
            c[:, col + i] = v[i]
    for l in range(4):
        put(C_MIXN + l * 8, inp["mix_norm"][l])
        put(C_FFNN + l * 8, inp["ffn_norm"][l])
    for jc in range(2):
        for tap in range(3):
            put(C_CONVW + jc * 24 + tap * 8, inp["conv_w"][jc, tap])
    for tap in range(4):
        put(C_LCW + tap * 10, inp["lru_conv_w"][0, tap])
    put(C_LCB, inp["lru_conv_b"][0])
    put(C_LGAB, inp["lru_gate_a_b"][0].reshape(-1))
    put(C_LGXB, inp["lru_gate_x_b"][0].reshape(-1))
    put(C_LLAM, inp["lru_lambda"][0])
    put(C_QNORM, inp["mla_q_norm"][0])
    put(C_KVNORM, inp["mla_kv_norm"][0])
    put(C_QNN, inp["mla_qn_norm"][0])
    put(C_KNN, inp["mla_kn_norm"][0])
    c[0:32, C_QRN] = inp["mla_qr_norm"][0][0:32]
    c[0:32, C_QRN + 1] = inp["mla_qr_norm"][0][32:64]
    c[0:32, C_KRN] = inp["mla_kr_norm"][0][0:32]
    c[0:32, C_KRN + 1] = inp["mla_kr_norm"][0][32:64]
    c[0:32, C_INVF] = (10000.0 ** (-np.arange(0, 64, 2, dtype=np.float32) / np.float32(64))).astype(np.float32)
    return c


def host_cmat():
    m = np.zeros((128, 256), np.float32)
    m[:, 0:128] = np.eye(128, dtype=np.float32)
    k = np.arange(128)[:, None]
    q = np.arange(128)[None, :]
    m[:, 128:256] = (k <= q).astype(np.float32)
    return m


_NC_CACHE = {}


def kernel(**inputs):
    inp = {k: np.asarray(v) for k, v in inputs.items()}
    n = 8
    if "nc" not in _NC_CACHE:
        _NC_CACHE["nc"] = build()
    nc = _NC_CACHE["nc"]
    cm = host_cmat()
    shared = {name: np.ascontiguousarray(inp[name], dtype=np.float32) for name in WEIGHT_NAMES}
    in_maps = []
    for b in range(n):
        m = dict(shared)
        m["xT"] = np.ascontiguousarray(inp["x"][b].T)
        m["pos"] = np.ascontiguousarray(inp["positions"][b].reshape(1, S).astype(np.int32))
        m["consts"] = host_consts(inp, b)
        m["cmat"] = cm
        in_maps.append(m)
    res = run_bass_kernel_spmd(nc, in_maps, core_ids=list(range(n)))
    outp = np.stack([np.ascontiguousarray(r["out"].T) for r in res.results], axis=0)
    return outp.astype(np.float32)
```
